# Optimizing a Trainium2 kernel written in Bass

```python
import math
import jax, jax.numpy as jnp
from jax import lax
import numpy as np

D_MODEL = 1024
BATCH = 8
SEQ = 4096
DEPTH = 1

HEAD_DIM = 64
N_ATTN_HEADS = 12
ATTN_WIDTH = N_ATTN_HEADS * HEAD_DIM
N_FOURIER_GROUPS = 4
FOURIER_GROUP_DIM = 64
FOURIER_WIDTH = N_FOURIER_GROUPS * FOURIER_GROUP_DIM
MIX_WIDTH = ATTN_WIDTH + FOURIER_WIDTH
IN_PROJ_WIDTH = 3 * ATTN_WIDTH + FOURIER_WIDTH
DILATED_PATTERNS = ((128, 1), (512, 4), (2048, 16))
N_REL_BUCKETS = 32
REL_MAX_DISTANCE = 1024
D_FF = 2816
CONV_WIDTH = 3
EPS = 1e-6
NEG_INF = -1e30

kernel_name = "hymba_dilated_fnet_convglu_block"


def _rms_norm(x, g):
    xf = x.astype(jnp.float32)
    y = xf * lax.rsqrt(jnp.mean(xf * xf, axis=-1, keepdims=True) + EPS)
    return (y * g.astype(jnp.float32)).astype(x.dtype)


def _t5_bucket(rel):
    nb = N_REL_BUCKETS // 2
    max_exact = nb // 2
    ret = jnp.where(rel > 0, nb, 0)
    n = jnp.abs(rel)
    nf = jnp.maximum(n, 1).astype(jnp.float32)
    large = max_exact + (jnp.log(nf / max_exact) / math.log(REL_MAX_DISTANCE / max_exact)
                         * (nb - max_exact)).astype(jnp.int32)
    large = jnp.minimum(large, nb - 1)
    return ret + jnp.where(n < max_exact, n, large)


def _dilated_branch(q, k, v, rel_table, window, dilation):
    B, S, H, hd = q.shape
    half = window // (2 * dilation)
    L = S // dilation
    nb = -(-L // half)
    Lp = nb * half

    def to_sub(t, extra):
        t = t.reshape(B, L, dilation, H, hd).transpose(0, 2, 1, 3, 4)
        return jnp.pad(t, ((0, 0), (0, 0), (extra, Lp - L + extra), (0, 0), (0, 0)))

    def windows(t):
        blk = to_sub(t, half).reshape(B, dilation, nb + 2, half, H, hd)
        return jnp.concatenate([blk[:, :, :-2], blk[:, :, 1:-1], blk[:, :, 2:]], axis=3)

    qb = to_sub(q, 0).reshape(B, dilation, nb, half, H, hd)
    kw, vw = windows(k), windows(v)

    qi = jnp.arange(half)
    kj = jnp.arange(3 * half)
    rel = kj[None, :] - half - qi[:, None]
    kidx = (jnp.arange(nb)[:, None] - 1) * half + kj[None, :]
    mask = (jnp.abs(rel) <= half)[None] & ((kidx >= 0) & (kidx < L))[:, None, :]
    bias = rel_table[_t5_bucket(rel * dilation)].astype(jnp.float32).transpose(2, 0, 1)

    s = jnp.einsum('brnqhd,brnkhd->brnhqk', qb, kw) * (hd ** -0.5) + bias
    s = jnp.where(mask[:, None], s, NEG_INF)
    m = jnp.max(s, axis=-1, keepdims=True)
    p = jnp.exp(s - m)
    l = jnp.sum(p, axis=-1)
    o = jnp.einsum('brnhqk,brnkhd->brnqhd', p, vw) / l.transpose(0, 1, 2, 4, 3)[..., None]
    lse = (m[..., 0] + jnp.log(l)).transpose(0, 1, 2, 4, 3)

    o = o.reshape(B, dilation, Lp, H, hd)[:, :, :L].transpose(0, 2, 1, 3, 4).reshape(B, S, H, hd)
    lse = lse.reshape(B, dilation, Lp, H)[:, :, :L].transpose(0, 2, 1, 3).reshape(B, S, H)
    return o, lse


def _dilated_attention(q, k, v, rel_table):
    outs, lses = [], []
    for window, dilation in DILATED_PATTERNS:
        o, lse = _dilated_branch(q, k, v, rel_table, window, dilation)
        outs.append(o)
        lses.append(lse)
    w = jax.nn.softmax(jnp.stack(lses, axis=0), axis=0)
    return jnp.sum(w[..., None] * jnp.stack(outs, axis=0), axis=0)


def _fourier_mix(u, w, b):
    f = jnp.fft.fft2(u.astype(jnp.float32), axes=(1, 3), norm="ortho").real
    return jnp.einsum('bsgc,gcd->bsgd', f, w.astype(jnp.float32)) + b.astype(jnp.float32)


def _conv_glu_ffn(h, w_gate, w_val, conv_w, conv_b, w_down):
    g = h @ w_gate
    val = h @ w_val
    pad = CONV_WIDTH // 2
    g = lax.conv_general_dilated(
        g, conv_w.astype(g.dtype)[:, None, :], window_strides=(1,), padding=((pad, pad),),
        dimension_numbers=('NWC', 'WIO', 'NWC'), feature_group_count=D_FF) + conv_b
    return (jax.nn.silu(g) * val) @ w_down


def setup_inputs(seed: int = 0) -> dict:
    key = jax.random.key(seed)
    ks = jax.random.split(key, 20)
    nrm = lambda k, shape, scale: jax.random.normal(k, shape, jnp.float32) * scale
    gain = lambda k, shape: 1.0 + 0.01 * jax.random.normal(k, shape, jnp.float32)
    return {
        "x": nrm(ks[0], (BATCH, SEQ, D_MODEL), 1.0),
        "norm_mix_gain": gain(ks[1], (DEPTH, D_MODEL)),
        "w_in": nrm(ks[2], (DEPTH, D_MODEL, IN_PROJ_WIDTH), D_MODEL ** -0.5),
        "attn_out_gain": gain(ks[3], (DEPTH, ATTN_WIDTH)),
        "rel_bias_table": nrm(ks[4], (N_REL_BUCKETS, N_ATTN_HEADS), 0.5),
        "fourier_w": nrm(ks[5], (DEPTH, N_FOURIER_GROUPS, FOURIER_GROUP_DIM, FOURIER_GROUP_DIM), FOURIER_GROUP_DIM ** -0.5),
        "fourier_b": nrm(ks[6], (DEPTH, N_FOURIER_GROUPS, FOURIER_GROUP_DIM), 0.01),
        "fourier_out_gain": gain(ks[7], (DEPTH, FOURIER_WIDTH)),
        "w_out": nrm(ks[8], (DEPTH, MIX_WIDTH, D_MODEL), MIX_WIDTH ** -0.5),
        "norm_ffn_gain": gain(ks[9], (DEPTH, D_MODEL)),
        "w_gate": nrm(ks[10], (DEPTH, D_MODEL, D_FF), D_MODEL ** -0.5),
        "w_val": nrm(ks[11], (DEPTH, D_MODEL, D_FF), D_MODEL ** -0.5),
        "conv_w": nrm(ks[12], (DEPTH, CONV_WIDTH, D_FF), CONV_WIDTH ** -0.5),
        "conv_b": nrm(ks[13], (DEPTH, D_FF), 0.01),
        "w_down": nrm(ks[14], (DEPTH, D_FF, D_MODEL), D_FF ** -0.5),
        "final_norm_gain": gain(ks[15], (D_MODEL,)),
    }


def reference(x, norm_mix_gain, w_in, attn_out_gain, rel_bias_table, fourier_w, fourier_b,
              fourier_out_gain, w_out, norm_ffn_gain, w_gate, w_val, conv_w, conv_b, w_down,
              final_norm_gain):
    B, S, _ = x.shape
    for layer in range(DEPTH):
        h = _rms_norm(x, norm_mix_gain[layer])
        proj = h @ w_in[layer]
        q = proj[..., :ATTN_WIDTH].reshape(B, S, N_ATTN_HEADS, HEAD_DIM).astype(jnp.float32)
        k = proj[..., ATTN_WIDTH:2 * ATTN_WIDTH].reshape(B, S, N_ATTN_HEADS, HEAD_DIM).astype(jnp.float32)
        v = proj[..., 2 * ATTN_WIDTH:3 * ATTN_WIDTH].reshape(B, S, N_ATTN_HEADS, HEAD_DIM).astype(jnp.float32)
        u = proj[..., 3 * ATTN_WIDTH:].reshape(B, S, N_FOURIER_GROUPS, FOURIER_GROUP_DIM)

        attn = _dilated_attention(q, k, v, rel_bias_table).reshape(B, S, ATTN_WIDTH).astype(x.dtype)
        four = _fourier_mix(u, fourier_w[layer], fourier_b[layer]).reshape(B, S, FOURIER_WIDTH).astype(x.dtype)
        mixed = jnp.concatenate([_rms_norm(attn, attn_out_gain[layer]),
                                 _rms_norm(four, fourier_out_gain[layer])], axis=-1)
        x = x + mixed @ w_out[layer]

        h = _rms_norm(x, norm_ffn_gain[layer])
        x = x + _conv_glu_ffn(h, w_gate[layer], w_val[layer], conv_w[layer], conv_b[layer], w_down[layer])
    return _rms_norm(x, final_norm_gain)
```

```python
import math
import os
from contextlib import ExitStack

import numpy as np
import ml_dtypes
import concourse.bass as bass
import concourse.mybir as mybir
from concourse.bass_utils import run_bass_kernel_spmd

F32 = mybir.dt.float32
BF16 = mybir.dt.bfloat16
AF = mybir.ActivationFunctionType
ALU = mybir.AluOpType

S = 4096
D = 1024
NT = 32
DFF = 2816
NF = 22
EPS = 1e-6
DILS = (1, 4, 16)
P_G = 384
REP = 64


class Ctx:
    def __init__(self, nc, es):
        self.nc = nc
        self.E = {"pe": nc.tensor, "act": nc.scalar, "dve": nc.vector, "pool": nc.gpsimd, "sp": nc.sync}
        self.sem = {k: es.enter_context(nc.semaphore("sem_" + k)) for k in self.E}
        self.cnt = {k: 0 for k in self.E}
        self.seen = {k: {} for k in self.E}
        self.lastw = {}
        self.readers = {}
        nd = 64
        self.dsem = [es.enter_context(nc.semaphore("dsem%d" % i)) for i in range(nd)]
        self.dval = [0] * nd
        self.dnext = 0

    def _wait(self, eng, tok):
        sem, key, val = tok
        if self.seen[eng].get(key, 0) >= val:
            return
        self.E[eng].wait_ge(sem, val)
        self.seen[eng][key] = val

    def _deps(self, eng, reads, writes):
        for k in list(reads) + list(writes):
            t = self.lastw.get(k)
            if t is not None:
                self._wait(eng, t)
        for k in writes:
            for t in self.readers.get(k, {}).values():
                self._wait(eng, t)

    def _commit(self, tok, reads, writes):
        for k in writes:
            self.lastw[k] = tok
            self.readers[k] = {}
        for k in reads:
            d = self.readers.setdefault(k, {})
            if tok[1] not in d or d[tok[1]][2] < tok[2]:
                d[tok[1]] = tok

    def op(self, eng, fns, reads=(), writes=()):
        self._deps(eng, reads, writes)
        if callable(fns):
            fns = [fns]
        ins = None
        for f in fns:
            ins = f(self.E[eng])
        self.cnt[eng] += 1
        ins.then_inc(self.sem[eng], 1)
        tok = (self.sem[eng], eng, self.cnt[eng])
        self._commit(tok, reads, writes)

    def dma(self, q, out, in_, reads=(), writes=(), **kw):
        self._deps(q, reads, writes)
        i = self.dnext
        self.dnext = (i + 1) % len(self.dsem)
        key = "d%d" % i
        if self.dval[i] > 0:
            self._wait(q, (self.dsem[i], key, self.dval[i]))
        self.E[q].dma_start(out=out, in_=in_, **kw).then_inc(self.dsem[i], 16)
        self.dval[i] += 16
        tok = (self.dsem[i], key, self.dval[i])
        self._commit(tok, reads, writes)

    def barrier(self):
        if os.environ.get("KDBG"):
            print("barrier counts", self.cnt, max(self.dval))
        toks = [(self.sem[k], k, self.cnt[k]) for k in self.E if self.cnt[k] > 0]
        toks += [(self.dsem[i], "d%d" % i, self.dval[i]) for i in range(len(self.dsem)) if self.dval[i] > 0]
        for eng in self.E:
            for t in toks:
                if t[1] != eng:
                    self._wait(eng, t)

    def finish(self, eng, keys):
        for k in keys:
            t = self.lastw.get(k)
            if t is not None:
                self._wait(eng, t)


_STOP = int(os.environ.get('KSTOP', '9'))


def build_nc():
    nc = bass.Bass("TRN2", target_bir_lowering=False)

    def din(name, shape, dt=F32):
        return nc.dram_tensor(name, list(shape), dt, kind="ExternalInput").ap()

    def dscr(name, shape, dt):
        return nc.dram_tensor(name, list(shape), dt, kind="Internal").ap()

    x_d = din("x", [S, D])
    g1_d = din("g1b", [128, D])
    g2_d = din("g2b", [128, D])
    gF_d = din("gFb", [128, D])
    gf_d = din("gfb", [128, 256])
    ga_d = din("ga_t", [128, 6])
    win_d = din("w_in", [D, 2560])
    wout_d = din("w_out", [D, D])
    wg_d = din("w_gate", [D, DFF])
    wv_d = din("w_val", [D, DFF])
    wd_d = din("w_down", [DFF, D])
    cw_d = din("cw_t", [128, NF, 3])
    cb_d = din("cb_t", [128, NF])
    tab_d = din("rel_tab", [32, 12])
    oh_d = din("onehot", [32, 3, P_G])
    gm_d = din("gmask", [12, 3, P_G])
    fwb_d = din("fw_blk", [128, 2, 128])
    fb_d = din("fb_row", [1, 256])
    c64_d = din("c64blk", [128, 128], BF16)
    s64_d = din("s64blk", [128, 128], BF16)
    id_d = din("ident", [128, 128], BF16)
    tabc_d = din("dft_cos", [8, 128, 32, 512], BF16)
    tabs_d = din("dft_sin", [8, 128, 32, 512], BF16)
    y_d = nc.dram_tensor("y", [S, D], F32, kind="ExternalOutput").ap()

    mix_d = dscr("mix_scr", [D, S], BF16)
    x1_d = dscr("x1_scr", [S, D], F32)
    h2_d = dscr("h2_scr", [D, S + 2], BF16)
    gr_d = dscr("gr_scr", [12, 3, REP, P_G], BF16)

    with ExitStack() as es:
        cx = Ctx(nc, es)

        def sb(stack, name, shape, dt):
            return stack.enter_context(nc.sbuf_tensor("sb_" + name, list(shape), dt))

        def ps(stack, name, shape, dt):
            return stack.enter_context(nc.psum_tensor("ps_" + name, list(shape), dt))

        ident = sb(es, "ident", [128, 128], BF16)
        ones_b = sb(es, "ones_b", [128, 128], BF16)
        epsb = sb(es, "epsb", [128, 1], F32)
        s2 = es.enter_context(ExitStack())
        hT = sb(s2, "hT", [128, 8, S], BF16)
        cx.dma("sp", ident[:], id_d[:, :], writes=["ident"])
        cx.op("pool", lambda e: e.memset(ones_b[:], 1.0), writes=["ones_b"])
        cx.op("pool", lambda e: e.memset(epsb[:], EPS), writes=["epsb"])

        def rms_rstd(stack_tiles, src_ap, src_key, n, tag):
            junk, ssq, rstd, kj, ks, kr = stack_tiles
            cx.op("act", lambda e: e.activation(out=junk, in_=src_ap, func=AF.Square, accum_out=ssq),
                  reads=[src_key], writes=[kj, ks])
            cx.op("act", lambda e: e.activation(out=rstd, in_=ssq, func=AF.Sqrt, scale=1.0 / n, bias=epsb[:, 0:1]),
                  reads=[ks, "epsb"], writes=[kr])
            cx.op("dve", lambda e: e.reciprocal(out=rstd, in_=rstd), reads=[kr], writes=[kr])

        with ExitStack() as pa:
            g1b = sb(pa, "g1b", [128, D], F32)
            cx.dma("sp", g1b[:], g1_d[:, :], writes=["g1b"])
            xt = [sb(pa, "xt%d" % i, [128, D], F32) for i in range(2)]
            junk = sb(pa, "junkA", [128, D], F32)
            hb = [sb(pa, "hb%d" % i, [128, D], BF16) for i in range(2)]
            ssq = [sb(pa, "ssqA%d" % i, [128, 1], F32) for i in range(2)]
            rstd = [sb(pa, "rstdA%d" % i, [128, 1], F32) for i in range(2)]
            pT = [ps(pa, "pTA%d" % i, [128, 8, 128], BF16) for i in range(2)]
            def a_s2(t):
                b = t % 2
                cx.op("pe", [(lambda e, c=c: e.transpose(out=pT[b][:, c, :], in_=hb[b][:, c * 128:(c + 1) * 128],
                                                         identity=ident[:])) for c in range(8)],
                      reads=["hb%d" % b, "ident"], writes=["pTA%d" % b])
                cx.op("act", lambda e: e.copy(out=hT[:, :, t * 128:(t + 1) * 128], in_=pT[b][:]),
                      reads=["pTA%d" % b], writes=[("hT", t // 4)])

            for t in range(NT):
                b = t % 2
                cx.dma("sp", xt[b][:], x_d[t * 128:(t + 1) * 128, :], writes=["xt%d" % b])
                rms_rstd((junk[:], ssq[b][:], rstd[b][:], "junkA", "ssqA%d" % b, "rstdA%d" % b),
                         xt[b][:], "xt%d" % b, D, "A")
                cx.op("dve", lambda e: e.scalar_tensor_tensor(out=hb[b][:], in0=xt[b][:], scalar=rstd[b][:, 0:1],
                                                              in1=g1b[:], op0=ALU.mult, op1=ALU.mult),
                      reads=["xt%d" % b, "rstdA%d" % b, "g1b"], writes=["hb%d" % b])
                if t >= 1:
                    a_s2(t - 1)
            a_s2(NT - 1)
        cx.barrier()
        if _STOP <= 1:
            return nc
        sD = es.enter_context(ExitStack())
        usb = sb(sD, "usb", [128, NT, 256], BF16)
        with ExitStack() as pu:
            wuf = sb(pu, "wuf", [128, 8, 256], F32)
            wub = sb(pu, "wub", [128, 8, 256], BF16)
            pU = [ps(pu, "pUu%d" % i, [128, 512], F32) for i in range(2)]
            cx.dma("sp", wuf[:], win_d.rearrange("(c p) n -> p c n", p=128)[:, :, 2304:2560], writes=["wuf"])
            cx.op("pool", lambda e: e.tensor_copy(out=wub[:], in_=wuf[:]), reads=["wuf"], writes=["wub"])
            for t in range(NT):
                b = t % 2
                cx.op("pe", [(lambda e, c=c: e.matmul(pU[b][:, 0:256], lhsT=hT[:, c, t * 128:(t + 1) * 128], rhs=wub[:, c, :],
                                                      start=(c == 0), stop=(c == 7))) for c in range(8)],
                      reads=[("hT", t // 4), "wub"], writes=["pUu%d" % b])
                cx.op("dve" if b else "act",
                      (lambda e: e.tensor_copy(out=usb[:, t, :], in_=pU[b][:, 0:256])) if b else
                      (lambda e: e.copy(out=usb[:, t, :], in_=pU[b][:, 0:256])),
                      reads=["pUu%d" % b], writes=[("usb", t)])
        cx.barrier()
        if _STOP <= 2:
            return nc
        with ExitStack() as pd:
            ABT = sb(pd, "ABT", [128, 2, 2, S], BF16)
            tabb = [sb(pd, "tabb%d" % i, [128, 32, 512], BF16) for i in range(2)]
            c64 = sb(pd, "c64", [128, 128], BF16)
            s64 = sb(pd, "s64", [128, 128], BF16)
            fwf = sb(pd, "fwf", [128, 2, 128], F32)
            fwb = sb(pd, "fwb", [128, 2, 128], BF16)
            M12 = sb(pd, "M12", [128, 2, 2, 128], BF16)
            fbf = sb(pd, "fbf", [1, 256], F32)
            fbb = sb(pd, "fbb", [1, 256], BF16)
            gfb = sb(pd, "gfb", [128, 256], F32)
            fjunk = sb(pd, "fjunk", [128, 256], F32)
            fssq = [sb(pd, "fssq%d" % i, [128, 1], F32) for i in range(2)]
            frstd = [sb(pd, "frstd%d" % i, [128, 1], F32) for i in range(2)]
            fnb = [sb(pd, "fnb%d" % i, [128, 256], BF16) for i in range(2)]
            fourT = sb(pd, "fourT", [128, 2, S], BF16)
            pU = [ps(pd, "pU%d" % i, [128, 512], F32) for i in range(2)]
            pF = [ps(pd, "pF%d" % i, [128, 512], F32) for i in range(2)]
            pM = ps(pd, "pM", [128, 512], F32)
            pFT = [ps(pd, "pFT%d" % i, [128, 2, 128], BF16) for i in range(2)]

            cx.dma("sp", c64[:], c64_d[:, :], writes=["c64"])
            cx.dma("sp", s64[:], s64_d[:, :], writes=["s64"])
            cx.dma("sp", fwf[:], fwb_d[:, :, :], writes=["fwf"])
            cx.dma("sp", fbf[:], fb_d[:, :], writes=["fbf"])
            cx.dma("sp", gfb[:], gf_d[:, :], writes=["gfb"])
            cx.op("pool", lambda e: e.tensor_copy(out=fwb[:], in_=fwf[:]), reads=["fwf"], writes=["fwb"])
            cx.op("pool", lambda e: e.tensor_copy(out=fbb[:], in_=fbf[:]), reads=["fbf"], writes=["fbb"])
            cx.op("pe", [(lambda e, cs=cs, cc=cc: e.matmul(pM[:, (cs * 2 + cc) * 128:(cs * 2 + cc + 1) * 128],
                                                           lhsT=(c64 if cs == 0 else s64)[:], rhs=fwb[:, cc, :],
                                                           start=True, stop=True)) for cs in range(2) for cc in range(2)],
                  reads=["c64", "s64", "fwb"], writes=["pM"])
            cx.op("act", lambda e: e.copy(out=M12[:].rearrange("p a b e -> p (a b e)"), in_=pM[:]),
                  reads=["pM"], writes=["M12"])
            it = 0
            for sbk in range(8):
                for cs, tsrc in ((0, tabc_d), (1, tabs_d)):
                    tb_ = it % 2
                    it += 1
                    cx.dma("sp", tabb[tb_][:], tsrc[sbk, :, :, :], writes=["tabb%d" % tb_])
                    for cc in range(2):
                        b = cc
                        cx.op("pe", [(lambda e, k=k: e.matmul(pF[b][:], lhsT=usb[:, k, cc * 128:(cc + 1) * 128],
                                                              rhs=tabb[tb_][:, k, :], start=(k == 0), stop=(k == 31)))
                                     for k in range(32)],
                              reads=[("usb", k_) for k_ in range(NT)] + ["tabb%d" % tb_], writes=["pF%d" % b])
                        if cc == 0:
                            cx.op("act", lambda e: e.copy(out=ABT[:, cs, cc, sbk * 512:(sbk + 1) * 512], in_=pF[b][:]),
                                  reads=["pF%d" % b], writes=["ABT"])
                        else:
                            cx.op("dve", lambda e: e.tensor_copy(out=ABT[:, cs, cc, sbk * 512:(sbk + 1) * 512], in_=pF[b][:]),
                                  reads=["pF%d" % b], writes=["ABT"])
            for t in range(NT):
                b = t % 2
                ts_ = slice(t * 128, (t + 1) * 128)
                mm = []
                for cc in range(2):
                    o_ap = pU[b][:, cc * 128:(cc + 1) * 128]
                    mm.append(lambda e, cc=cc, o_ap=o_ap: e.matmul(o_ap, lhsT=ABT[:, 0, cc, ts_], rhs=M12[:, 0, cc, :], start=True, stop=False))
                    mm.append(lambda e, cc=cc, o_ap=o_ap: e.matmul(o_ap, lhsT=ABT[:, 1, cc, ts_], rhs=M12[:, 1, cc, :], start=False, stop=False))
                    mm.append(lambda e, cc=cc, o_ap=o_ap: e.matmul(o_ap, lhsT=ones_b[0:1, :], rhs=fbb[0:1, cc * 128:(cc + 1) * 128], start=False, stop=True))
                cx.op("pe", mm, reads=["ABT", "M12", "ones_b", "fbb"], writes=["pU%d" % b])
                rms_rstd((fjunk[:], fssq[b][:], frstd[b][:], "fjunk", "fssq%d" % b, "frstd%d" % b),
                         pU[b][:, 0:256], "pU%d" % b, 256, "F")
                cx.op("dve", lambda e: e.scalar_tensor_tensor(out=fnb[b][:], in0=pU[b][:, 0:256], scalar=frstd[b][:, 0:1],
                                                              in1=gfb[:], op0=ALU.mult, op1=ALU.mult),
                      reads=["pU%d" % b, "frstd%d" % b, "gfb"], writes=["fnb%d" % b])
                cx.op("pe", [(lambda e, cc=cc: e.transpose(out=pFT[b][:, cc, :], in_=fnb[b][:, cc * 128:(cc + 1) * 128],
                                                           identity=ident[:])) for cc in range(2)],
                      reads=["fnb%d" % b, "ident"], writes=["pFT%d" % b])
                cx.op("act", lambda e: e.copy(out=fourT[:, :, ts_], in_=pFT[b][:]),
                      reads=["pFT%d" % b], writes=["fourT"])
            for cc in range(2):
                cx.dma("pool", mix_d[768 + cc * 128:768 + (cc + 1) * 128, :], fourT[:, cc, :], reads=["fourT"],
                       writes=[("mix_d", 6 + cc)])

        cx.barrier()
        if _STOP <= 3:
            return nc
        sD.close()
        with ExitStack() as pb:
            EB = sb(pb, "EB", [128, 6, 3, 512], BF16)
            with ExitStack() as pe_:
                tab = sb(pe_, "tab", [32, 12], F32)
                oh = sb(pe_, "oh", [32, 3, P_G], F32)
                gm = sb(pe_, "gm", [12, 3, P_G], F32)
                gsb = sb(pe_, "gsb", [12, 3, P_G], F32)
                gbf = sb(pe_, "gbf", [12, 3, P_G], BF16)
                pG = ps(pe_, "pG", [12, 3, 512], F32)
                cx.dma("sp", tab[:], tab_d[:, :], writes=["tab"])
                cx.dma("sp", oh[:], oh_d[:, :, :], writes=["oh"])
                cx.dma("sp", gm[:], gm_d[:, :, :], writes=["gm"])
                cx.op("pe", [(lambda e, d=d: e.matmul(pG[:, d, 0:P_G], lhsT=tab[:], rhs=oh[:, d, :], start=True, stop=True))
                             for d in range(3)], reads=["tab", "oh"], writes=["pG"])
                cx.op("act", lambda e: e.activation(out=gsb[:], in_=pG[:, :, 0:P_G], func=AF.Exp),
                      reads=["pG"], writes=["gsb"])
                cx.op("dve", lambda e: e.tensor_tensor(out=gbf[:], in0=gsb[:], in1=gm[:], op=ALU.mult),
                      reads=["gsb", "gm"], writes=["gbf"])
                cx.dma("sp", gr_d[:, :, 0, :], gbf[:], reads=["gbf"], writes=["gr_d"])
                n = 1
                while n < REP:
                    cx.dma("sp", gr_d[:, :, n:2 * n, :], gr_d[:, :, 0:n, :], reads=["gr_d"], writes=["gr_d"])
                    n *= 2
                for h in range(12):
                    for d in range(3):
                        base = (h * 3 + d) * REP * P_G
                        for (r0, c0, off) in ((0, 0, 191), (64, 0, 255), (0, 128, 63), (64, 128, 127)):
                            src = bass.AP(gr_d.tensor, base + off, [[P_G - 1, 64], [1, 128]])
                            cx.dma("sp", EB[r0:r0 + 64, h // 2, d, c0 * 2 + (h % 2) * 128:c0 * 2 + (h % 2) * 128 + 128], src, reads=["gr_d"], writes=[("EBp", h, d, r0, c0)])

                ebk = [("EBp", h, d, r0, c0) for h in range(12) for d in range(3) for (r0, c0) in ((0, 0), (64, 0), (0, 128), (64, 128))]
                cx.op("dve", lambda e: e.memset(gsb[:, 0, 0:1], 0.0), reads=ebk + ["gsb"], writes=["EB", "gsb"])
                cx.barrier()
            if _STOP <= 4:
                return nc
            wfr = sb(pb, "wfr", [128, 1024], F32)
            wf = [wfr[:].rearrange("p (c n) -> p c n", c=8)]
            wqkv = sb(pb, "wqkv", [128, 3, 8, 128], BF16)
            QT = sb(pb, "QT", [128, 2, S], BF16)
            VT = sb(pb, "VT", [128, S], BF16)
            KPb = [sb(pb, "KP%d" % d, [128, d * (S // d + 128)], BF16) for d in DILS]
            Vs = sb(pb, "Vs", [128, 48, 192], BF16)
            acc = [sb(pb, "acc%d" % i, [128, S], F32) for i in range(2)]
            attn = [sb(pb, "attn%d" % i, [128, 512], BF16) for i in range(2)]
            Eb = [sb(pb, "Eb%d" % i, [128, 512], BF16) for i in range(2)]
            PTb = [sb(pb, "PTb%d" % i, [128, 512], BF16) for i in range(4)]
            pVt2 = [ps(pb, "pVt%d" % i, [128, 8, 128], BF16) for i in range(2)]
            vbi = 0
            pS = [ps(pb, "pS%d" % i, [128, 512], F32) for i in range(2)]
            pI = pS
            pO = [ps(pb, "pO%d" % i, [128, 512], F32) for i in range(4)]

            cx.op("pool", lambda e: e.memset(Vs[:], 1.0), writes=["VsL", "VsU"])
            for i_ in range(3):
                cx.op("pool", lambda e: e.memset(KPb[i_][:], 0.0), writes=[("KP", i_)])
            cx.op("pool", lambda e: e.memset(QT[:], 0.0), writes=[("QT", i) for i in range(8)])
            win_v = win_d.rearrange("(c p) n -> p c n", p=128)

            for j in range(6):
                for wi, col0 in enumerate((j * 128, 768 + j * 128, 1536 + j * 128)):
                    b = 0
                    cx.dma("sp", wf[b], win_v[:, :, col0:col0 + 128], writes=["wfr"])
                    cx.op("pool", lambda e: e.tensor_copy(out=wqkv[:, wi, :, :], in_=wf[b]),
                          reads=["wfr"], writes=[("wqkv", wi)])
                it = 0
                QTk = [("QT", i) for i in range(8)] + [("QTb", i) for i in range(8)]
                KTk = [("KT", i) for i in range(8)]
                VTk = [("VT", i) for i in range(8)]
                for wi, dst, key0 in ((0, QT, "QT"), (1, None, "KT"), (2, VT, "VT")):
                    for tb in range(8):
                        b = it % 2
                        it += 1
                        cx.op("pe", [(lambda e, c=c: e.matmul(pI[b][:], lhsT=wqkv[:, wi, c, :],
                                                              rhs=hT[:, c, tb * 512:(tb + 1) * 512],
                                                              start=(c == 0), stop=(c == 7))) for c in range(8)],
                              reads=[("wqkv", wi), ("hT", tb)], writes=["pS%d" % b])
                        key = (key0, tb)
                        tsl = slice(tb * 512, (tb + 1) * 512)
                        if wi == 0:
                            cx.op("act", lambda e: e.copy(out=QT[0:64, 0, tsl], in_=pI[b][0:64, :]),
                                  reads=["pS%d" % b], writes=[key])
                            cx.op("dve", lambda e: e.tensor_copy(out=QT[64:128, 1, tsl], in_=pI[b][64:128, :]),
                                  reads=["pS%d" % b], writes=[(key0 + "b", tb)])
                        elif wi == 2:
                            if tb % 2:
                                cx.op("act", lambda e: e.copy(out=VT[:, tsl], in_=pI[b][:]), reads=["pS%d" % b], writes=[key])
                            else:
                                cx.op("dve", lambda e: e.tensor_copy(out=VT[:, tsl], in_=pI[b][:]), reads=["pS%d" % b], writes=[key])
                        else:
                            src1 = pI[b][:].rearrange("p (m x) -> p m x", x=128)
                            k1 = KPb[0]
                            d0 = k1[:, tb * 512:tb * 512 + 512].rearrange("p (m x) -> p m x", x=128)
                            d1 = k1[:, tb * 512 + 128:tb * 512 + 640].rearrange("p (m x) -> p m x", x=128)
                            src4 = pI[b][:].rearrange("p (l r) -> p r l", r=4)
                            k4 = KPb[1][:].rearrange("p (r l) -> p r l", r=4)
                            src16 = pI[b][:].rearrange("p (l r) -> p r l", r=16)
                            k16 = KPb[2][:].rearrange("p (r l) -> p r l", r=16)
                            pos16 = 32 * tb + (128 if ((tb // 2) % 2) else 0)
                            for eng, pr in (("act", slice(0, 64)), ("dve", slice(64, 128))):
                                cp = (lambda e, o, i_: e.copy(out=o, in_=i_)) if eng == "act" else (lambda e, o, i_: e.tensor_copy(out=o, in_=i_))
                                cx.op(eng, [lambda e: cp(e, d0[pr, :, 0:64], src1[pr, :, 0:64]),
                                            lambda e: cp(e, d1[pr, :, 64:128], src1[pr, :, 64:128]),
                                            lambda e: cp(e, k4[pr, :, 128 * tb:128 * tb + 64], src4[pr, :, 0:64]),
                                            lambda e: cp(e, k4[pr, :, 128 * tb + 192:128 * tb + 256], src4[pr, :, 64:128]),
                                            lambda e: cp(e, k16[pr, :, pos16:pos16 + 32], src16[pr, :, :])],
                                      reads=["pS%d" % b], writes=[("KPa" if eng == "act" else "KPd", tb)])
                KPk = [("KPa", i) for i in range(8)] + [("KPd", i) for i in range(8)] + [("KP", i) for i in range(3)]
                for h in range(2):
                    pass

                oi = 0
                si = 0
                for di, d in enumerate(DILS):
                    L = S // d
                    Lb = L + 128
                    nqb = L // 128
                    nsl = nqb + 1
                    KPv = KPb[di][:].rearrange("p (r l) -> p r l", r=d)
                    VTv = VT[:].rearrange("p (l r) -> p r l", r=d)
                    Vsv = Vs[:, 0:d * nsl, :].rearrange("p (r s) c -> p r s c", r=d)
                    for r in range(d):
                        for t0 in range(0, nqb, 8):
                            nt_ = min(8, nqb - t0)
                            vb = vbi % 2
                            vbi += 1
                            pVt = pVt2[vb]
                            cx.op("pe", [(lambda e, i=i: e.transpose(out=pVt[:, i, :],
                                                                     in_=VTv[:, r, (t0 + i) * 128:(t0 + i + 1) * 128],
                                                                     identity=ident[:])) for i in range(nt_)],
                                  reads=VTk + ["ident"], writes=["pVt%d" % vb])
                            cx.op("dve", [lambda e: e.tensor_copy(out=Vsv[0:64, r, t0:t0 + nt_, 0:64], in_=pVt[0:64, 0:nt_, 0:64]),
                                          lambda e: e.tensor_copy(out=Vsv[0:64, r, t0:t0 + nt_, 128:192], in_=pVt[0:64, 0:nt_, 64:128])],
                                  reads=["pVt%d" % vb], writes=["VsL"])
                            cx.op("act", [lambda e: e.copy(out=Vsv[64:128, r, t0 + 1:t0 + 1 + nt_, 0:64], in_=pVt[64:128, 0:nt_, 0:64]),
                                          lambda e: e.copy(out=Vsv[64:128, r, t0 + 1:t0 + 1 + nt_, 128:192], in_=pVt[64:128, 0:nt_, 64:128])],
                                  reads=["pVt%d" % vb], writes=["VsU"])
                    QTv = QT[:].rearrange("p a (l r) -> p a r l", r=d)
                    accv = [acc[h][:].rearrange("p (l r) -> p r l", r=d) for h in range(2)]
                    vcols = (slice(0, 128), slice(64, 192))
                    groups = []
                    if d == 16:
                        for r0 in range(0, 16, 2):
                            groups.append([(r0, 0), (r0, 1), (r0 + 1, 0), (r0 + 1, 1)])
                    else:
                        for r in range(d):
                            for g0 in range(0, nqb, 4):
                                groups.append([(r, g0 + i) for i in range(4)])
                    pend = []

                    def emit_pv(item):
                        (r, qb, sidx, ob, slot, grp) = item
                        pt = PTb[sidx]
                        lo_ok = qb > 0
                        hi_ok = qb < nqb - 1
                        for h in range(2):
                            vcol = vcols[h]
                            pob = pO[h * 2 + ob]
                            o_ap = pob[:, slot * 128:(slot + 1) * 128]
                            ca = slice(h * 128, h * 128 + 128)
                            cb2 = slice(256 + h * 128, 256 + h * 128 + 128)
                            mm = []
                            if lo_ok:
                                mm.append(lambda e: e.matmul(o_ap, lhsT=Vsv[:, r, qb, vcol], rhs=pt[:, ca], start=(slot == 0), stop=False, skip_group_check=True))
                            else:
                                mm.append(lambda e: e.matmul(o_ap, lhsT=Vsv[0:64, r, qb, vcol], rhs=pt[0:64, ca], start=(slot == 0), stop=False, skip_group_check=True))
                            if hi_ok:
                                mm.append(lambda e: e.matmul(o_ap, lhsT=Vsv[:, r, qb + 1, vcol], rhs=pt[:, cb2], start=False, stop=True, skip_group_check=True))
                            else:
                                mm.append(lambda e: e.matmul(o_ap, lhsT=Vsv[64:128, r, qb + 1, vcol], rhs=pt[64:128, cb2], start=False, stop=True, skip_group_check=True))
                            cx.op("pe", mm, reads=["VsL", "VsU", "PTb%d" % sidx], writes=[("pO", h * 2 + ob, slot)])
                            if slot == 3:
                                if d == 16:
                                    r0 = grp[0][0]
                                    dst = accv[h][:, r0:r0 + 2, :]
                                    srcp = pob[:].rearrange("p (a l) -> p a l", a=2)
                                else:
                                    r_, q0 = grp[0]
                                    dst = accv[h][:, r_, q0 * 128:q0 * 128 + 512]
                                    srcp = pob[:]
                                if di == 0:
                                    cx.op("act", lambda e: e.copy(out=dst, in_=srcp),
                                          reads=[("pO", h * 2 + ob, s_) for s_ in range(4)], writes=["acc%d" % h])
                                else:
                                    cx.op("dve", lambda e: e.tensor_tensor(out=dst, in0=srcp, in1=dst, op=ALU.add),
                                          reads=[("pO", h * 2 + ob, s_) for s_ in range(4)] + ["acc%d" % h], writes=["acc%d" % h])

                    for grp in groups:
                        ob = oi % 2
                        oi += 1
                        for slot, (r, qb) in enumerate(grp):
                            sidx = si % 4
                            ebi = si % 2
                            psi = si % 2
                            si += 1
                            q_ap = QTv[:, :, r, qb * 128:(qb + 1) * 128]
                            kA = KPv[:, r, qb * 128:(qb + 1) * 128]
                            kB = KPv[:, r, (qb + 1) * 128:(qb + 2) * 128]
                            oA = pS[psi][:, 0:256].rearrange("p (a q) -> p a q", a=2)
                            oB = pS[psi][:, 256:512].rearrange("p (a q) -> p a q", a=2)
                            cx.op("pe", [lambda e: e.matmul(oA, lhsT=kA, rhs=q_ap, start=True, stop=True),
                                         lambda e: e.matmul(oB, lhsT=kB, rhs=q_ap, start=True, stop=True)],
                                  reads=QTk + KPk, writes=["pS%d" % psi])
                            cx.op("act", lambda e: e.activation(out=Eb[ebi][:], in_=pS[psi][:], func=AF.Exp, scale=0.125),
                                  reads=["pS%d" % psi], writes=["Eb%d" % ebi])
                            cx.op("dve", lambda e: e.tensor_tensor(out=PTb[sidx][:], in0=Eb[ebi][:], in1=EB[:, j, di, :], op=ALU.mult),
                                  reads=["Eb%d" % ebi, "EB"], writes=["PTb%d" % sidx])
                            pend.append((r, qb, sidx, ob, slot, grp))
                            if len(pend) > 2:
                                emit_pv(pend.pop(0))
                    while pend:
                        emit_pv(pend.pop(0))

                for cb_ in range(0 if os.environ.get('KNOEPI') else 8):
                    cs = slice(cb_ * 512, (cb_ + 1) * 512)
                    rden = wfr[:, (cb_ % 2) * 512:(cb_ % 2) * 512 + 512]
                    rk = "wfr"
                    cx.op("dve", [lambda e: e.tensor_copy(out=rden[0:64, :], in_=acc[0][64:128, cs]),
                                  lambda e: e.tensor_copy(out=rden[64:128, :], in_=acc[1][0:64, cs])],
                          reads=["acc0", "acc1"], writes=[rk])
                    cx.op("act", lambda e: e.activation(out=rden, in_=rden, func=AF.Ln), reads=[rk], writes=[rk])
                    cx.op("act", lambda e: e.activation(out=rden, in_=rden, func=AF.Exp, scale=-1.0), reads=[rk], writes=[rk])
                    ab_ = cb_ % 2
                    cx.op("dve", [lambda e: e.tensor_tensor(out=attn[ab_][0:64, :], in0=acc[0][0:64, cs], in1=rden[0:64, :], op=ALU.mult),
                                  lambda e: e.tensor_tensor(out=attn[ab_][64:128, :], in0=acc[1][64:128, cs], in1=rden[64:128, :], op=ALU.mult)],
                          reads=["acc0", "acc1", rk], writes=["attn%d" % ab_])
                    cx.dma("pool", mix_d[j * 128:(j + 1) * 128, cs], attn[ab_][:], reads=["attn%d" % ab_], writes=[("mix_d", j, cb_)])
        cx.barrier()
        if _STOP <= 5:
            return nc
        s2.close()
        with ExitStack() as pe2:
            wof = [sb(pe2, "wof%d" % i, [128, D], F32) for i in range(2)]
            wob = sb(pe2, "wob", [128, 8, D], BF16)
            gat = sb(pe2, "gat", [128, 6], F32)
            g2b = sb(pe2, "g2b", [128, D], F32)
            mixb = [sb(pe2, "mixb%d" % i, [128, 8, 512], BF16) for i in range(2)]
            sqb = [sb(pe2, "sqb%d" % i, [128, 512], BF16) for i in range(2)]
            rsa = sb(pe2, "rsa", [128, 512], F32)
            xr = [sb(pe2, "xr%d" % i, [128, D], F32) for i in range(2)]
            x1 = [sb(pe2, "x1_%d" % i, [128, D], F32) for i in range(2)]
            junkE = sb(pe2, "junkE", [128, D], F32)
            ssqE = [sb(pe2, "ssqE%d" % i, [128, 1], F32) for i in range(2)]
            rstdE = [sb(pe2, "rstdE%d" % i, [128, 1], F32) for i in range(2)]
            h2b = [sb(pe2, "h2b%d" % i, [128, D], BF16) for i in range(3)]
            h2T = [sb(pe2, "h2T%d" % i, [128, 8, 128], BF16) for i in range(2)]
            zcol = sb(pe2, "zcol", [128, 8, 1], BF16)
            pR = ps(pe2, "pR", [128, 512], F32)
            pY = [ps(pe2, "pY%d" % i, [128, D], F32) for i in range(2)]
            pT2 = [ps(pe2, "pT2%d" % i, [128, 8, 128], BF16) for i in range(2)]

            cx.dma("sp", gat[:], ga_d[:, :], writes=["gat"])
            cx.dma("sp", g2b[:], g2_d[:, :], writes=["g2b"])
            for c in range(8):
                b = c % 2
                cx.dma("sp", wof[b][:], wout_d[c * 128:(c + 1) * 128, :], writes=["wof%d" % b])
                cx.op("pool", lambda e: e.tensor_copy(out=wob[:, c, :], in_=wof[b][:]), reads=["wof%d" % b], writes=["wob"])
            cx.op("pool", lambda e: e.memset(zcol[:], 0.0), writes=["zcol"])
            h2v = h2_d.rearrange("(c p) s -> p c s", p=128)
            cx.dma("pool", h2v[:, :, 0:1], zcol[:], reads=["zcol"], writes=["h2halo"], allow_slow_non_contiguous=True)
            cx.dma("pool", h2v[:, :, S + 1:S + 2], zcol[:], reads=["zcol"], writes=["h2halo"], allow_slow_non_contiguous=True)
            mixv = mix_d.rearrange("(c p) s -> p c s", p=128)
            def e_prologue(blk):
                mb = blk % 2
                bs = slice(blk * 512, (blk + 1) * 512)
                cx.dma("sp", mixb[mb][:], mixv[:, :, bs], reads=[("mix_d", i, blk) for i in range(6)] + [("mix_d", 6), ("mix_d", 7)], writes=["mixb%d" % mb])
                for jj in range(6):
                    q = jj % 2
                    cx.op("act", lambda e: e.activation(out=sqb[q][:], in_=mixb[mb][:, jj, :], func=AF.Square),
                          reads=["mixb%d" % mb], writes=["sqb%d" % q])
                    cx.op("pe", lambda e: e.matmul(pR[:], lhsT=ones_b[:], rhs=sqb[q][:], start=(jj == 0), stop=(jj == 5)),
                          reads=["ones_b", "sqb%d" % q], writes=["pR"])
                cx.op("act", lambda e: e.activation(out=rsa[:], in_=pR[:], func=AF.Sqrt, scale=1.0 / 768, bias=epsb[:, 0:1]),
                      reads=["pR", "epsb"], writes=["rsa"])
                cx.op("dve", lambda e: e.reciprocal(out=rsa[:], in_=rsa[:]), reads=["rsa"], writes=["rsa"])
                for jj in range(6):
                    cx.op("dve", lambda e: e.scalar_tensor_tensor(out=mixb[mb][:, jj, :], in0=mixb[mb][:, jj, :],
                                                                  scalar=gat[:, jj:jj + 1], in1=rsa[:],
                                                                  op0=ALU.mult, op1=ALU.mult),
                          reads=["mixb%d" % mb, "gat", "rsa"], writes=["mixb%d" % mb])

            def e_s1(t):
                blk, tt = t // 4, t % 4
                mb = blk % 2
                b = t % 2
                b3 = t % 3
                cx.dma("sp", xr[b][:], x_d[t * 128:(t + 1) * 128, :], writes=["xr%d" % b])
                mm = []
                for hf in range(2):
                    for c in range(8):
                        mm.append(lambda e, hf=hf, c=c: e.matmul(pY[b][:, hf * 512:(hf + 1) * 512],
                                                                 lhsT=mixb[mb][:, c, tt * 128:(tt + 1) * 128],
                                                                 rhs=wob[:, c, hf * 512:(hf + 1) * 512],
                                                                 start=(c == 0), stop=(c == 7)))
                cx.op("pe", mm, reads=["mixb%d" % mb, "wob"], writes=["pY%d" % b])
                cx.op("dve", lambda e: e.tensor_tensor(out=x1[b][:], in0=pY[b][:], in1=xr[b][:], op=ALU.add),
                      reads=["pY%d" % b, "xr%d" % b], writes=["x1_%d" % b])
                cx.dma("pool", x1_d[t * 128:(t + 1) * 128, :], x1[b][:], reads=["x1_%d" % b], writes=[("x1_d", t)])
                rms_rstd((junkE[:], ssqE[b][:], rstdE[b][:], "junkE", "ssqE%d" % b, "rstdE%d" % b),
                         x1[b][:], "x1_%d" % b, D, "E")
                cx.op("dve", lambda e: e.scalar_tensor_tensor(out=h2b[b3][:], in0=x1[b][:], scalar=rstdE[b][:, 0:1],
                                                              in1=g2b[:], op0=ALU.mult, op1=ALU.mult),
                      reads=["x1_%d" % b, "rstdE%d" % b, "g2b"], writes=["h2b%d" % b3])

            def e_s2(t):
                b = t % 2
                b3 = t % 3
                cx.op("pe", [(lambda e, c=c: e.transpose(out=pT2[b][:, c, :], in_=h2b[b3][:, c * 128:(c + 1) * 128],
                                                         identity=ident[:])) for c in range(8)],
                      reads=["h2b%d" % b3, "ident"], writes=["pT2%d" % b])
                cx.op("act", lambda e: e.copy(out=h2T[b][:], in_=pT2[b][:]), reads=["pT2%d" % b], writes=["h2T%d" % b])
                cx.dma("pool", h2v[:, :, 1 + t * 128:1 + (t + 1) * 128], h2T[b][:], reads=["h2T%d" % b],
                       writes=[("h2_d", t)])

            e_prologue(0)
            for t in range(NT):
                if t % 4 == 0 and t // 4 + 1 < 8:
                    e_prologue(t // 4 + 1)
                e_s1(t)
                if t >= 2:
                    e_s2(t - 2)
            e_s2(NT - 2)
            e_s2(NT - 1)
        cx.barrier()
        if _STOP <= 6:
            return nc
        with ExitStack() as pf:
            wgb = sb(pf, "wgb", [128, 8, DFF], BF16)
            wvb = sb(pf, "wvb", [128, 8, DFF], BF16)
            wdb = sb(pf, "wdb", [128, NF, D], BF16)
            wst = [sb(pf, "wst%d" % i, [128, DFF // 2], F32) for i in range(2)]
            cwt = sb(pf, "cwt", [128, NF, 3], F32)
            cbt = sb(pf, "cbt", [128, NF], F32)
            gFb = sb(pf, "gFb", [128, D], F32)
            h2s = [sb(pf, "h2s%d" % i, [128, 8, 258], BF16) for i in range(2)]
            t1 = [sb(pf, "t1_%d" % i, [128, 256], F32) for i in range(2)]
            t2 = [sb(pf, "t2_%d" % i, [128, 256], F32) for i in range(2)]
            sg = [sb(pf, "sg_%d" % i, [128, 256], F32) for i in range(2)]
            aT = [sb(pf, "aT_%d" % i, [128, 256], BF16) for i in range(4)]
            x1r = [sb(pf, "x1r%d" % i, [128, D], F32) for i in range(2)]
            of_ = [sb(pf, "of%d" % i, [128, D], F32) for i in range(2)]
            junkF = sb(pf, "junkF", [128, D], F32)
            ssqF = [sb(pf, "ssqF%d" % i, [128, 1], F32) for i in range(2)]
            rstdF = [sb(pf, "rstdF%d" % i, [128, 1], F32) for i in range(2)]
            yo = [sb(pf, "yo%d" % i, [128, D], F32) for i in range(2)]
            pGt = [ps(pf, "pGt%d" % i, [128, 512], F32) for i in range(2)]
            pVl = [ps(pf, "pVl%d" % i, [128, 512], F32) for i in range(2)]
            pD = [ps(pf, "pD%d" % i, [128, D], F32) for i in range(2)]

            cx.dma("sp", cwt[:], cw_d[:, :, :], writes=["cwt"])
            cx.dma("sp", cbt[:], cb_d[:, :], writes=["cbt"])
            cx.dma("sp", gFb[:], gF_d[:, :], writes=["gFb"])
            wi = 0
            for (src_d, dstw, key) in ((wg_d, wgb, "wgb"), (wv_d, wvb, "wvb")):
                for c in range(16):
                    b = wi % 2
                    wi += 1
                    c_, hf_ = c // 2, c % 2
                    cols = slice(hf_ * (DFF // 2), (hf_ + 1) * (DFF // 2))
                    cx.dma("sp", wst[b][:], src_d[c_ * 128:(c_ + 1) * 128, cols], writes=["wst%d" % b])
                    eng = "pool" if (wi % 2) else "dve"
                    cx.op(eng, lambda e: e.tensor_copy(out=dstw[:, c_, cols], in_=wst[b][:]), reads=["wst%d" % b], writes=[key])
            for f in range(NF):
                b = wi % 2
                wi += 1
                cx.dma("sp", wst[b][:, 0:D], wd_d[f * 128:(f + 1) * 128, :], writes=["wst%d" % b])
                eng = "pool" if (wi % 2) else "dve"
                cx.op(eng, lambda e: e.tensor_copy(out=wdb[:, f, :], in_=wst[b][:, 0:D]), reads=["wst%d" % b], writes=["wdb"])

            h2v = h2_d.rearrange("(c p) s -> p c s", p=128)
            _SUB = int(os.environ.get("KSUB", "9"))
            for st in range(16 if _SUB > 1 else 0):
                hb_ = st % 2
                cx.dma("sp", h2s[hb_][:], h2v[:, :, st * 256:st * 256 + 258],
                       reads=[("h2_d", t) for t in range(max(0, 2 * st - 1), min(NT, 2 * st + 3))] + ["h2halo"],
                       writes=["h2s%d" % hb_])
                pend = []

                def emit_down(item):
                    f, ab = item
                    mm = []
                    for tt in range(2):
                        for hf in range(2):
                            mm.append(lambda e, tt=tt, hf=hf: e.matmul(pD[tt][:, hf * 512:(hf + 1) * 512],
                                                                       lhsT=aT[ab][:, tt * 128:(tt + 1) * 128],
                                                                       rhs=wdb[:, f, hf * 512:(hf + 1) * 512],
                                                                       start=(f == 0), stop=(f == NF - 1)))
                    cx.op("pe", mm, reads=["aT_%d" % ab, "wdb"], writes=(["pD0", "pD1"] if f in (0, NF - 1) else []))

                for f in range(NF):
                    b = f % 2
                    ab = f % 4
                    fs = slice(f * 128, (f + 1) * 128)
                    cx.op("pe", [(lambda e, c=c: e.matmul(pGt[b][:, 0:258], lhsT=wgb[:, c, fs], rhs=h2s[hb_][:, c, 0:258],
                                                          start=(c == 0), stop=(c == 7))) for c in range(8)],
                          reads=["wgb", "h2s%d" % hb_], writes=["pGt%d" % b])
                    cx.op("pe", [(lambda e, c=c: e.matmul(pVl[b][:, 0:256], lhsT=wvb[:, c, fs], rhs=h2s[hb_][:, c, 1:257],
                                                          start=(c == 0), stop=(c == 7))) for c in range(8)],
                          reads=["wvb", "h2s%d" % hb_], writes=["pVl%d" % b])
                    cx.op("act", lambda e: e.activation(out=t1[b][:], in_=pGt[b][:, 1:257], func=AF.Identity,
                                                        scale=cwt[:, f, 1:2], bias=cbt[:, f:f + 1]),
                          reads=["pGt%d" % b, "cwt", "cbt"], writes=["t1_%d" % b])
                    cx.op("dve", lambda e: e.scalar_tensor_tensor(out=t2[b][:], in0=pGt[b][:, 0:256], scalar=cwt[:, f, 0:1],
                                                                  in1=t1[b][:], op0=ALU.mult, op1=ALU.add),
                          reads=["pGt%d" % b, "cwt", "t1_%d" % b], writes=["t2_%d" % b])
                    cx.op("dve", lambda e: e.scalar_tensor_tensor(out=t1[b][:], in0=pGt[b][:, 2:258], scalar=cwt[:, f, 2:3],
                                                                  in1=t2[b][:], op0=ALU.mult, op1=ALU.add),
                          reads=["pGt%d" % b, "cwt", "t2_%d" % b], writes=["t1_%d" % b])
                    cx.op("act", lambda e: e.activation(out=sg[b][:], in_=t1[b][:], func=AF.Silu),
                          reads=["t1_%d" % b], writes=["sg_%d" % b])
                    cx.op("dve", lambda e: e.tensor_tensor(out=aT[ab][:], in0=pVl[b][:, 0:256], in1=sg[b][:], op=ALU.mult),
                          reads=["pVl%d" % b, "sg_%d" % b], writes=["aT_%d" % ab])
                    if _SUB > 2:
                        pend.append((f, ab))
                    if len(pend) > 2:
                        emit_down(pend.pop(0))
                while pend:
                    emit_down(pend.pop(0))
                for tt in range(2 if _SUB > 3 else 0):
                    t = st * 2 + tt
                    b = t % 2
                    cx.dma("sp", x1r[b][:], x1_d[t * 128:(t + 1) * 128, :], reads=[("x1_d", t)], writes=["x1r%d" % b])
                    cx.op("dve", lambda e: e.tensor_tensor(out=of_[b][:], in0=pD[tt][:], in1=x1r[b][:], op=ALU.add),
                          reads=["pD%d" % tt, "x1r%d" % b], writes=["of%d" % b])
                    rms_rstd((junkF[:], ssqF[b][:], rstdF[b][:], "junkF", "ssqF%d" % b, "rstdF%d" % b),
                             of_[b][:], "of%d" % b, D, "F")
                    cx.op("dve", lambda e: e.scalar_tensor_tensor(out=yo[b][:], in0=of_[b][:], scalar=rstdF[b][:, 0:1],
                                                                  in1=gFb[:], op0=ALU.mult, op1=ALU.mult),
                          reads=["of%d" % b, "rstdF%d" % b, "gFb"], writes=["yo%d" % b])
                    cx.dma(os.environ.get("KYQ", "sp"), y_d[t * 128:(t + 1) * 128, :], yo[b][:], reads=["yo%d" % b], writes=[("y", t)])
            cx.finish("pool", [("y", t) for t in range(NT)])
            cx.barrier()
    return nc


def _t5_bucket_np(rel):
    nb = 16
    max_exact = 8
    ret = np.where(rel > 0, nb, 0)
    n = np.abs(rel)
    nf = np.maximum(n, 1).astype(np.float32)
    large = max_exact + (np.log(nf / np.float32(max_exact)) / np.float32(math.log(1024 / max_exact))
                         * np.float32(nb - max_exact)).astype(np.int32)
    large = np.minimum(large, nb - 1)
    return ret + np.where(n < max_exact, n, large)


_CONST = {}


def _constants():
    if _CONST:
        return _CONST
    bf = ml_dtypes.bfloat16
    m = np.arange(P_G)
    rel = 191 - m
    oh = np.zeros((32, 3, P_G), np.float32)
    gmask = np.zeros((12, 3, P_G), np.float32)
    for di, d in enumerate(DILS):
        bk = _t5_bucket_np(rel * d)
        valid = np.abs(rel) <= 64
        oh[bk[valid], di, m[valid]] = 1.0
        gmask[:, di, valid] = 1.0
    _CONST["onehot"] = oh
    _CONST["gmask"] = gmask
    c = np.arange(64)
    ang = 2 * np.pi * np.outer(c, c) / 64
    C64 = np.cos(ang) / 512.0
    S64 = -np.sin(ang) / 512.0
    z = np.zeros((64, 64))
    _CONST["c64blk"] = np.block([[C64, z], [z, C64]]).astype(bf)
    _CONST["s64blk"] = np.block([[S64, z], [z, S64]]).astype(bf)
    _CONST["ident"] = np.eye(128, dtype=np.float32).astype(bf)
    s = np.arange(S, dtype=np.int64)
    prod = (s[:, None] * s[None, :]) % S
    angt = prod.astype(np.float64) * (2 * np.pi / S)
    for name, fn in (("dft_cos", np.cos), ("dft_sin", np.sin)):
        t = fn(angt).astype(np.float32)
        t = t.reshape(32, 128, 8, 512).transpose(2, 1, 0, 3)
        _CONST[name] = np.ascontiguousarray(t).astype(bf)
    return _CONST


_NC = {}


def kernel(x, norm_mix_gain, w_in, attn_out_gain, rel_bias_table, fourier_w, fourier_b, fourier_out_gain,
           w_out, norm_ffn_gain, w_gate, w_val, conv_w, conv_b, w_down, final_norm_gain):
    f32 = np.float32
    x = np.asarray(x, f32)
    cst = _constants()
    bc = lambda v, n: np.ascontiguousarray(np.broadcast_to(np.asarray(v, f32).reshape(1, n), (128, n)))
    fw = np.asarray(fourier_w, f32)[0]
    fw_blk = np.zeros((128, 2, 128), f32)
    for cc in range(2):
        fw_blk[0:64, cc, 0:64] = fw[2 * cc]
        fw_blk[64:128, cc, 64:128] = fw[2 * cc + 1]
    shared = {
        "g1b": bc(norm_mix_gain[0], D), "g2b": bc(norm_ffn_gain[0], D), "gFb": bc(final_norm_gain, D),
        "gfb": bc(fourier_out_gain[0], 256),
        "ga_t": np.ascontiguousarray(np.asarray(attn_out_gain, f32)[0].reshape(6, 128).T),
        "w_in": np.ascontiguousarray(np.asarray(w_in, f32)[0]),
        "w_out": np.ascontiguousarray(np.asarray(w_out, f32)[0]),
        "w_gate": np.ascontiguousarray(np.asarray(w_gate, f32)[0]),
        "w_val": np.ascontiguousarray(np.asarray(w_val, f32)[0]),
        "w_down": np.ascontiguousarray(np.asarray(w_down, f32)[0]),
        "cw_t": np.ascontiguousarray(np.asarray(conv_w, f32)[0].reshape(3, NF, 128).transpose(2, 1, 0)),
        "cb_t": np.ascontiguousarray(np.asarray(conv_b, f32)[0].reshape(NF, 128).T),
        "rel_tab": np.ascontiguousarray(np.asarray(rel_bias_table, f32)),
        "onehot": cst["onehot"], "gmask": cst["gmask"],
        "fw_blk": fw_blk,
        "fb_row": np.ascontiguousarray(np.asarray(fourier_b, f32)[0].reshape(1, 256)),
        "c64blk": cst["c64blk"], "s64blk": cst["s64blk"], "ident": cst["ident"],
        "dft_cos": cst["dft_cos"], "dft_sin": cst["dft_sin"],
    }
    n = x.shape[0]
    if "nc" not in _NC:
        _NC["nc"] = build_nc()
    nc = _NC["nc"]
    in_maps = [dict(shared, x=np.ascontiguousarray(x[b])) for b in range(n)]
    res = run_bass_kernel_spmd(nc, in_maps, core_ids=list(range(n)))
    return np.stack([np.asarray(r["y"], f32) for r in res.results], axis=0)
```

```python
import math
import os
from contextlib import ExitStack

import numpy as np
import ml_dtypes
import concourse.bass as bass
import concourse.mybir as mybir
from concourse.bass_utils import run_bass_kernel_spmd

F32 = mybir.dt.float32
BF16 = mybir.dt.bfloat16
AF = mybir.ActivationFunctionType
ALU = mybir.AluOpType

S = 4096
D = 1024
NT = 32
DFF = 2816
NF = 22
EPS = 1e-6
DILS = (1, 4, 16)
P_G = 384
REP = 64


class Ctx:
    def __init__(self, nc, es):
        self.nc = nc
        self.E = {"pe": nc.tensor, "act": nc.scalar, "dve": nc.vector, "pool": nc.gpsimd, "sp": nc.sync}
        self.sem = {k: es.enter_context(nc.semaphore("sem_" + k)) for k in self.E}
        self.cnt = {k: 0 for k in self.E}
        self.seen = {k: {} for k in self.E}
        self.lastw = {}
        self.readers = {}
        nd = 64
        self.dsem = [es.enter_context(nc.semaphore("dsem%d" % i)) for i in range(nd)]
        self.dval = [0] * nd
        self.dnext = 0

    def _wait(self, eng, tok):
        sem, key, val = tok
        if self.seen[eng].get(key, 0) >= val:
            return
        self.E[eng].wait_ge(sem, val)
        self.seen[eng][key] = val

    def _deps(self, eng, reads, writes):
        for k in list(reads) + list(writes):
            t = self.lastw.get(k)
            if t is not None:
                self._wait(eng, t)
        for k in writes:
            for t in self.readers.get(k, {}).values():
                self._wait(eng, t)

    def _commit(self, tok, reads, writes):
        for k in writes:
            self.lastw[k] = tok
            self.readers[k] = {}
        for k in reads:
            d = self.readers.setdefault(k, {})
            if tok[1] not in d or d[tok[1]][2] < tok[2]:
                d[tok[1]] = tok

    def op(self, eng, fns, reads=(), writes=()):
        self._deps(eng, reads, writes)
        if callable(fns):
            fns = [fns]
        ins = None
        for f in fns:
            ins = f(self.E[eng])
        self.cnt[eng] += 1
        ins.then_inc(self.sem[eng], 1)
        tok = (self.sem[eng], eng, self.cnt[eng])
        self._commit(tok, reads, writes)

    def dma(self, q, out, in_, reads=(), writes=(), **kw):
        self._deps(q, reads, writes)
        i = self.dnext
        self.dnext = (i + 1) % len(self.dsem)
        key = "d%d" % i
        if self.dval[i] > 0:
            self._wait(q, (self.dsem[i], key, self.dval[i]))
        self.E[q].dma_start(out=out, in_=in_, **kw).then_inc(self.dsem[i], 16)
        self.dval[i] += 16
        tok = (self.dsem[i], key, self.dval[i])
        self._commit(tok, reads, writes)

    def barrier(self):
        if os.environ.get("KDBG"):
            print("barrier counts", self.cnt, max(self.dval))
        toks = [(self.sem[k], k, self.cnt[k]) for k in self.E if self.cnt[k] > 0]
        toks += [(self.dsem[i], "d%d" % i, self.dval[i]) for i in range(len(self.dsem)) if self.dval[i] > 0]
        for eng in self.E:
            for t in toks:
                if t[1] != eng:
                    self._wait(eng, t)

    def finish(self, eng, keys):
        for k in keys:
            t = self.lastw.get(k)
            if t is not None:
                self._wait(eng, t)


_STOP = int(os.environ.get('KSTOP', '9'))


def build_nc():
    nc = bass.Bass("TRN2", target_bir_lowering=False)

    def din(name, shape, dt=F32):
        return nc.dram_tensor(name, list(shape), dt, kind="ExternalInput").ap()

    def dscr(name, shape, dt):
        return nc.dram_tensor(name, list(shape), dt, kind="Internal").ap()

    x_d = din("x", [S, D])
    g1_d = din("g1b", [128, D])
    g2_d = din("g2b", [128, D])
    gF_d = din("gFb", [128, D])
    gf_d = din("gfb", [128, 256])
    ga_d = din("ga_t", [128, 6])
    win_d = din("w_in", [D, 2560])
    wout_d = din("w_out", [D, D])
    wg_d = din("w_gate", [D, DFF])
    wv_d = din("w_val", [D, DFF])
    wd_d = din("w_down", [DFF, D])
    cw_d = din("cw_t", [128, NF, 3])
    cb_d = din("cb_t", [128, NF])
    tab_d = din("rel_tab", [32, 12])
    oh_d = din("onehot", [32, 3, P_G])
    gm_d = din("gmask", [12, 3, P_G])
    fwb_d = din("fw_blk", [128, 2, 128])
    fb_d = din("fb_row", [1, 256])
    c64_d = din("c64blk", [128, 128], BF16)
    s64_d = din("s64blk", [128, 128], BF16)
    id_d = din("ident", [128, 128], BF16)
    tabc_d = din("dft_cos", [8, 128, 32, 512], BF16)
    tabs_d = din("dft_sin", [8, 128, 32, 512], BF16)
    y_d = nc.dram_tensor("y", [S, D], F32, kind="ExternalOutput").ap()

    mix_d = dscr("mix_scr", [D, S], BF16)
    x1_d = dscr("x1_scr", [S, D], F32)
    h2_d = dscr("h2_scr", [D, S + 2], BF16)
    gr_d = dscr("gr_scr", [12, 3, REP, P_G], BF16)

    with ExitStack() as es:
        cx = Ctx(nc, es)

        def sb(stack, name, shape, dt):
            return stack.enter_context(nc.sbuf_tensor("sb_" + name, list(shape), dt))

        def ps(stack, name, shape, dt):
            return stack.enter_context(nc.psum_tensor("ps_" + name, list(shape), dt))

        ident = sb(es, "ident", [128, 128], BF16)
        ones_b = sb(es, "ones_b", [128, 128], BF16)
        epsb = sb(es, "epsb", [128, 1], F32)
        s2 = es.enter_context(ExitStack())
        hT = sb(s2, "hT", [128, 8, S], BF16)
        cx.dma("sp", ident[:], id_d[:, :], writes=["ident"])
        cx.op("pool", lambda e: e.memset(ones_b[:], 1.0), writes=["ones_b"])
        cx.op("pool", lambda e: e.memset(epsb[:], EPS), writes=["epsb"])

        def rms_rstd(stack_tiles, src_ap, src_key, n, tag):
            junk, ssq, rstd, kj, ks, kr = stack_tiles
            cx.op("act", lambda e: e.activation(out=junk, in_=src_ap, func=AF.Square, accum_out=ssq),
                  reads=[src_key], writes=[kj, ks])
            cx.op("act", lambda e: e.activation(out=rstd, in_=ssq, func=AF.Sqrt, scale=1.0 / n, bias=epsb[:, 0:1]),
                  reads=[ks, "epsb"], writes=[kr])
            cx.op("dve", lambda e: e.reciprocal(out=rstd, in_=rstd), reads=[kr], writes=[kr])

        with ExitStack() as pa:
            g1b = sb(pa, "g1b", [128, D], F32)
            cx.dma("sp", g1b[:], g1_d[:, :], writes=["g1b"])
            xt = [sb(pa, "xt%d" % i, [128, D], F32) for i in range(2)]
            junk = sb(pa, "junkA", [128, D], F32)
            hb = [sb(pa, "hb%d" % i, [128, D], BF16) for i in range(2)]
            ssq = [sb(pa, "ssqA%d" % i, [128, 1], F32) for i in range(2)]
            rstd = [sb(pa, "rstdA%d" % i, [128, 1], F32) for i in range(2)]
            pT = [ps(pa, "pTA%d" % i, [128, 8, 128], BF16) for i in range(2)]
            def a_s2(t):
                b = t % 2
                cx.op("pe", [(lambda e, c=c: e.transpose(out=pT[b][:, c, :], in_=hb[b][:, c * 128:(c + 1) * 128],
                                                         identity=ident[:])) for c in range(8)],
                      reads=["hb%d" % b, "ident"], writes=["pTA%d" % b])
                cx.op("act", lambda e: e.copy(out=hT[:, :, t * 128:(t + 1) * 128], in_=pT[b][:]),
                      reads=["pTA%d" % b], writes=[("hT", t // 4)])

            for t in range(NT):
                b = t % 2
                cx.dma("sp", xt[b][:], x_d[t * 128:(t + 1) * 128, :], writes=["xt%d" % b])
                rms_rstd((junk[:], ssq[b][:], rstd[b][:], "junkA", "ssqA%d" % b, "rstdA%d" % b),
                         xt[b][:], "xt%d" % b, D, "A")
                cx.op("dve", lambda e: e.scalar_tensor_tensor(out=hb[b][:], in0=xt[b][:], scalar=rstd[b][:, 0:1],
                                                              in1=g1b[:], op0=ALU.mult, op1=ALU.mult),
                      reads=["xt%d" % b, "rstdA%d" % b, "g1b"], writes=["hb%d" % b])
                if t >= 1:
                    a_s2(t - 1)
            a_s2(NT - 1)
        cx.barrier()
        if _STOP <= 1:
            return nc
        sD = es.enter_context(ExitStack())
        usb = sb(sD, "usb", [128, NT, 256], BF16)
        with ExitStack() as pu:
            wuf = sb(pu, "wuf", [128, 8, 256], F32)
            wub = sb(pu, "wub", [128, 8, 256], BF16)
            pU = [ps(pu, "pUu%d" % i, [128, 512], F32) for i in range(2)]
            cx.dma("sp", wuf[:], win_d.rearrange("(c p) n -> p c n", p=128)[:, :, 2304:2560], writes=["wuf"])
            cx.op("pool", lambda e: e.tensor_copy(out=wub[:], in_=wuf[:]), reads=["wuf"], writes=["wub"])
            for t in range(NT):
                b = t % 2
                cx.op("pe", [(lambda e, c=c: e.matmul(pU[b][:, 0:256], lhsT=hT[:, c, t * 128:(t + 1) * 128], rhs=wub[:, c, :],
                                                      start=(c == 0), stop=(c == 7))) for c in range(8)],
                      reads=[("hT", t // 4), "wub"], writes=["pUu%d" % b])
                cx.op("dve" if b else "act",
                      (lambda e: e.tensor_copy(out=usb[:, t, :], in_=pU[b][:, 0:256])) if b else
                      (lambda e: e.copy(out=usb[:, t, :], in_=pU[b][:, 0:256])),
                      reads=["pUu%d" % b], writes=[("usb", t)])
        cx.barrier()
        if _STOP <= 2:
            return nc
        with ExitStack() as pd:
            ABT = sb(pd, "ABT", [128, 2, 2, S], BF16)
            tabb = [sb(pd, "tabb%d" % i, [128, 32, 512], BF16) for i in range(2)]
            c64 = sb(pd, "c64", [128, 128], BF16)
            s64 = sb(pd, "s64", [128, 128], BF16)
            fwf = sb(pd, "fwf", [128, 2, 128], F32)
            fwb = sb(pd, "fwb", [128, 2, 128], BF16)
            M12 = sb(pd, "M12", [128, 2, 2, 128], BF16)
            fbf = sb(pd, "fbf", [1, 256], F32)
            fbb = sb(pd, "fbb", [1, 256], BF16)
            gfb = sb(pd, "gfb", [128, 256], F32)
            fjunk = sb(pd, "fjunk", [128, 256], F32)
            fssq = [sb(pd, "fssq%d" % i, [128, 1], F32) for i in range(2)]
            frstd = [sb(pd, "frstd%d" % i, [128, 1], F32) for i in range(2)]
            fnb = [sb(pd, "fnb%d" % i, [128, 256], BF16) for i in range(2)]
            fourT = sb(pd, "fourT", [128, 2, S], BF16)
            pU = [ps(pd, "pU%d" % i, [128, 512], F32) for i in range(2)]
            pF = [ps(pd, "pF%d" % i, [128, 512], F32) for i in range(2)]
            pM = ps(pd, "pM", [128, 512], F32)
            pFT = [ps(pd, "pFT%d" % i, [128, 2, 128], BF16) for i in range(2)]

            cx.dma("sp", c64[:], c64_d[:, :], writes=["c64"])
            cx.dma("sp", s64[:], s64_d[:, :], writes=["s64"])
            cx.dma("sp", fwf[:], fwb_d[:, :, :], writes=["fwf"])
            cx.dma("sp", fbf[:], fb_d[:, :], writes=["fbf"])
            cx.dma("sp", gfb[:], gf_d[:, :], writes=["gfb"])
            cx.op("pool", lambda e: e.tensor_copy(out=fwb[:], in_=fwf[:]), reads=["fwf"], writes=["fwb"])
            cx.op("pool", lambda e: e.tensor_copy(out=fbb[:], in_=fbf[:]), reads=["fbf"], writes=["fbb"])
            cx.op("pe", [(lambda e, cs=cs, cc=cc: e.matmul(pM[:, (cs * 2 + cc) * 128:(cs * 2 + cc + 1) * 128],
                                                           lhsT=(c64 if cs == 0 else s64)[:], rhs=fwb[:, cc, :],
                                                           start=True, stop=True)) for cs in range(2) for cc in range(2)],
                  reads=["c64", "s64", "fwb"], writes=["pM"])
            cx.op("act", lambda e: e.copy(out=M12[:].rearrange("p a b e -> p (a b e)"), in_=pM[:]),
                  reads=["pM"], writes=["M12"])
            it = 0
            for sbk in range(8):
                for cs, tsrc in ((0, tabc_d), (1, tabs_d)):
                    tb_ = it % 2
                    it += 1
                    cx.dma("sp", tabb[tb_][:], tsrc[sbk, :, :, :], writes=["tabb%d" % tb_])
                    for cc in range(2):
                        b = cc
                        cx.op("pe", [(lambda e, k=k: e.matmul(pF[b][:], lhsT=usb[:, k, cc * 128:(cc + 1) * 128],
                                                              rhs=tabb[tb_][:, k, :], start=(k == 0), stop=(k == 31)))
                                     for k in range(32)],
                              reads=[("usb", k_) for k_ in range(NT)] + ["tabb%d" % tb_], writes=["pF%d" % b])
                        if cc == 0:
                            cx.op("act", lambda e: e.copy(out=ABT[:, cs, cc, sbk * 512:(sbk + 1) * 512], in_=pF[b][:]),
                                  reads=["pF%d" % b], writes=["ABT"])
                        else:
                            cx.op("dve", lambda e: e.tensor_copy(out=ABT[:, cs, cc, sbk * 512:(sbk + 1) * 512], in_=pF[b][:]),
                                  reads=["pF%d" % b], writes=["ABT"])
            for t in range(NT):
                b = t % 2
                ts_ = slice(t * 128, (t + 1) * 128)
                mm = []
                for cc in range(2):
                    o_ap = pU[b][:, cc * 128:(cc + 1) * 128]
                    mm.append(lambda e, cc=cc, o_ap=o_ap: e.matmul(o_ap, lhsT=ABT[:, 0, cc, ts_], rhs=M12[:, 0, cc, :], start=True, stop=False))
                    mm.append(lambda e, cc=cc, o_ap=o_ap: e.matmul(o_ap, lhsT=ABT[:, 1, cc, ts_], rhs=M12[:, 1, cc, :], start=False, stop=False))
                    mm.append(lambda e, cc=cc, o_ap=o_ap: e.matmul(o_ap, lhsT=ones_b[0:1, :], rhs=fbb[0:1, cc * 128:(cc + 1) * 128], start=False, stop=True))
                cx.op("pe", mm, reads=["ABT", "M12", "ones_b", "fbb"], writes=["pU%d" % b])
                rms_rstd((fjunk[:], fssq[b][:], frstd[b][:], "fjunk", "fssq%d" % b, "frstd%d" % b),
                         pU[b][:, 0:256], "pU%d" % b, 256, "F")
                cx.op("dve", lambda e: e.scalar_tensor_tensor(out=fnb[b][:], in0=pU[b][:, 0:256], scalar=frstd[b][:, 0:1],
                                                              in1=gfb[:], op0=ALU.mult, op1=ALU.mult),
                      reads=["pU%d" % b, "frstd%d" % b, "gfb"], writes=["fnb%d" % b])
                cx.op("pe", [(lambda e, cc=cc: e.transpose(out=pFT[b][:, cc, :], in_=fnb[b][:, cc * 128:(cc + 1) * 128],
                                                           identity=ident[:])) for cc in range(2)],
                      reads=["fnb%d" % b, "ident"], writes=["pFT%d" % b])
                cx.op("act", lambda e: e.copy(out=fourT[:, :, ts_], in_=pFT[b][:]),
                      reads=["pFT%d" % b], writes=["fourT"])
            for cc in range(2):
                cx.dma("pool", mix_d[768 + cc * 128:768 + (cc + 1) * 128, :], fourT[:, cc, :], reads=["fourT"],
                       writes=[("mix_d", 6 + cc)])

        cx.barrier()
        if _STOP <= 3:
            return nc
        sD.close()
        with ExitStack() as pb:
            EB = sb(pb, "EB", [128, 6, 3, 512], BF16)
            with ExitStack() as pe_:
                tab = sb(pe_, "tab", [32, 12], F32)
                oh = sb(pe_, "oh", [32, 3, P_G], F32)
                gm = sb(pe_, "gm", [12, 3, P_G], F32)
                gsb = sb(pe_, "gsb", [12, 3, P_G], F32)
                gbf = sb(pe_, "gbf", [12, 3, P_G], BF16)
                pG = ps(pe_, "pG", [12, 3, 512], F32)
                cx.dma("sp", tab[:], tab_d[:, :], writes=["tab"])
                cx.dma("sp", oh[:], oh_d[:, :, :], writes=["oh"])
                cx.dma("sp", gm[:], gm_d[:, :, :], writes=["gm"])
                cx.op("pe", [(lambda e, d=d: e.matmul(pG[:, d, 0:P_G], lhsT=tab[:], rhs=oh[:, d, :], start=True, stop=True))
                             for d in range(3)], reads=["tab", "oh"], writes=["pG"])
                cx.op("act", lambda e: e.activation(out=gsb[:], in_=pG[:, :, 0:P_G], func=AF.Exp),
                      reads=["pG"], writes=["gsb"])
                cx.op("dve", lambda e: e.tensor_tensor(out=gbf[:], in0=gsb[:], in1=gm[:], op=ALU.mult),
                      reads=["gsb", "gm"], writes=["gbf"])
                cx.dma("sp", gr_d[:, :, 0, :], gbf[:], reads=["gbf"], writes=["gr_d"])
                n = 1
                while n < REP:
                    cx.dma("sp", gr_d[:, :, n:2 * n, :], gr_d[:, :, 0:n, :], reads=["gr_d"], writes=["gr_d"])
                    n *= 2
                for h in range(12):
                    for d in range(3):
                        base = (h * 3 + d) * REP * P_G
                        for (r0, c0, off) in ((0, 0, 191), (64, 0, 255), (0, 128, 63), (64, 128, 127)):
                            src = bass.AP(gr_d.tensor, base + off, [[P_G - 1, 64], [1, 128]])
                            cx.dma("sp", EB[r0:r0 + 64, h // 2, d, c0 * 2 + (h % 2) * 128:c0 * 2 + (h % 2) * 128 + 128], src, reads=["gr_d"], writes=[("EBp", h, d, r0, c0)])

                ebk = [("EBp", h, d, r0, c0) for h in range(12) for d in range(3) for (r0, c0) in ((0, 0), (64, 0), (0, 128), (64, 128))]
                cx.op("dve", lambda e: e.memset(gsb[:, 0, 0:1], 0.0), reads=ebk + ["gsb"], writes=["EB", "gsb"])
                cx.barrier()
            if _STOP <= 4:
                return nc
            wfr = sb(pb, "wfr", [128, 1024], F32)
            wf = [wfr[:].rearrange("p (c n) -> p c n", c=8)]
            wqkv = sb(pb, "wqkv", [128, 3, 8, 128], BF16)
            QT = sb(pb, "QT", [128, 2, S], BF16)
            VT = sb(pb, "VT", [128, S], BF16)
            KPb = [sb(pb, "KP%d" % d, [128, d * (S // d + 128)], BF16) for d in DILS]
            Vs = sb(pb, "Vs", [128, 48, 192], BF16)
            acc = [sb(pb, "acc%d" % i, [128, S], F32) for i in range(2)]
            attn = [sb(pb, "attn%d" % i, [128, 512], BF16) for i in range(2)]
            Eb = [sb(pb, "Eb%d" % i, [128, 512], BF16) for i in range(2)]
            PTb = [sb(pb, "PTb%d" % i, [128, 512], BF16) for i in range(4)]
            pVt2 = [ps(pb, "pVt%d" % i, [128, 8, 128], BF16) for i in range(2)]
            vbi = 0
            pS = [ps(pb, "pS%d" % i, [128, 512], F32) for i in range(2)]
            pI = pS
            pO = [ps(pb, "pO%d" % i, [128, 512], F32) for i in range(4)]

            cx.op("pool", lambda e: e.memset(Vs[:], 1.0), writes=["VsL", "VsU"])
            for i_ in range(3):
                cx.op("pool", lambda e: e.memset(KPb[i_][:], 0.0), writes=[("KP", i_)])
            cx.op("pool", lambda e: e.memset(QT[:], 0.0), writes=[("QT", i) for i in range(8)])
            win_v = win_d.rearrange("(c p) n -> p c n", p=128)

            for j in range(6):
                for wi, col0 in enumerate((j * 128, 768 + j * 128, 1536 + j * 128)):
                    b = 0
                    cx.dma("sp", wf[b], win_v[:, :, col0:col0 + 128], writes=["wfr"])
                    cx.op("pool", lambda e: e.tensor_copy(out=wqkv[:, wi, :, :], in_=wf[b]),
                          reads=["wfr"], writes=[("wqkv", wi)])
                it = 0
                QTk = [("QT", i) for i in range(8)] + [("QTb", i) for i in range(8)]
                KTk = [("KT", i) for i in range(8)]
                VTk = [("VT", i) for i in range(8)]
                for wi, dst, key0 in ((0, QT, "QT"), (1, None, "KT"), (2, VT, "VT")):
                    for tb in range(8):
                        b = it % 2
                        it += 1
                        cx.op("pe", [(lambda e, c=c: e.matmul(pI[b][:], lhsT=wqkv[:, wi, c, :],
                                                              rhs=hT[:, c, tb * 512:(tb + 1) * 512],
                                                              start=(c == 0), stop=(c == 7))) for c in range(8)],
                              reads=[("wqkv", wi), ("hT", tb)], writes=["pS%d" % b])
                        key = (key0, tb)
                        tsl = slice(tb * 512, (tb + 1) * 512)
                        if wi == 0:
                            cx.op("act", lambda e: e.copy(out=QT[0:64, 0, tsl], in_=pI[b][0:64, :]),
                                  reads=["pS%d" % b], writes=[key])
                            cx.op("dve", lambda e: e.tensor_copy(out=QT[64:128, 1, tsl], in_=pI[b][64:128, :]),
                                  reads=["pS%d" % b], writes=[(key0 + "b", tb)])
                        elif wi == 2:
                            if tb % 2:
                                cx.op("act", lambda e: e.copy(out=VT[:, tsl], in_=pI[b][:]), reads=["pS%d" % b], writes=[key])
                            else:
                                cx.op("dve", lambda e: e.tensor_copy(out=VT[:, tsl], in_=pI[b][:]), reads=["pS%d" % b], writes=[key])
                        else:
                            src1 = pI[b][:].rearrange("p (m x) -> p m x", x=128)
                            k1 = KPb[0]
                            d0 = k1[:, tb * 512:tb * 512 + 512].rearrange("p (m x) -> p m x", x=128)
                            d1 = k1[:, tb * 512 + 128:tb * 512 + 640].rearrange("p (m x) -> p m x", x=128)
                            src4 = pI[b][:].rearrange("p (l r) -> p r l", r=4)
                            k4 = KPb[1][:].rearrange("p (r l) -> p r l", r=4)
                            src16 = pI[b][:].rearrange("p (l r) -> p r l", r=16)
                            k16 = KPb[2][:].rearrange("p (r l) -> p r l", r=16)
                            pos16 = 32 * tb + (128 if ((tb // 2) % 2) else 0)
                            for eng, pr in (("act", slice(0, 64)), ("dve", slice(64, 128))):
                                cp = (lambda e, o, i_: e.copy(out=o, in_=i_)) if eng == "act" else (lambda e, o, i_: e.tensor_copy(out=o, in_=i_))
                                cx.op(eng, [lambda e: cp(e, d0[pr, :, 0:64], src1[pr, :, 0:64]),
                                            lambda e: cp(e, d1[pr, :, 64:128], src1[pr, :, 64:128]),
                                            lambda e: cp(e, k4[pr, :, 128 * tb:128 * tb + 64], src4[pr, :, 0:64]),
                                            lambda e: cp(e, k4[pr, :, 128 * tb + 192:128 * tb + 256], src4[pr, :, 64:128]),
                                            lambda e: cp(e, k16[pr, :, pos16:pos16 + 32], src16[pr, :, :])],
                                      reads=["pS%d" % b], writes=[("KPa" if eng == "act" else "KPd", tb)])
                KPk = [("KPa", i) for i in range(8)] + [("KPd", i) for i in range(8)] + [("KP", i) for i in range(3)]
                for h in range(2):
                    pass

                oi = 0
                si = 0
                for di, d in enumerate(DILS):
                    L = S // d
                    Lb = L + 128
                    nqb = L // 128
                    nsl = nqb + 1
                    KPv = KPb[di][:].rearrange("p (r l) -> p r l", r=d)
                    VTv = VT[:].rearrange("p (l r) -> p r l", r=d)
                    Vsv = Vs[:, 0:d * nsl, :].rearrange("p (r s) c -> p r s c", r=d)
                    for r in range(d):
                        for t0 in range(0, nqb, 8):
                            nt_ = min(8, nqb - t0)
                            vb = vbi % 2
                            vbi += 1
                            pVt = pVt2[vb]
                            cx.op("pe", [(lambda e, i=i: e.transpose(out=pVt[:, i, :],
                                                                     in_=VTv[:, r, (t0 + i) * 128:(t0 + i + 1) * 128],
                                                                     identity=ident[:])) for i in range(nt_)],
                                  reads=VTk + ["ident"], writes=["pVt%d" % vb])
                            cx.op("dve", [lambda e: e.tensor_copy(out=Vsv[0:64, r, t0:t0 + nt_, 0:64], in_=pVt[0:64, 0:nt_, 0:64]),
                                          lambda e: e.tensor_copy(out=Vsv[0:64, r, t0:t0 + nt_, 128:192], in_=pVt[0:64, 0:nt_, 64:128])],
                                  reads=["pVt%d" % vb], writes=["VsL"])
                            cx.op("act", [lambda e: e.copy(out=Vsv[64:128, r, t0 + 1:t0 + 1 + nt_, 0:64], in_=pVt[64:128, 0:nt_, 0:64]),
                                          lambda e: e.copy(out=Vsv[64:128, r, t0 + 1:t0 + 1 + nt_, 128:192], in_=pVt[64:128, 0:nt_, 64:128])],
                                  reads=["pVt%d" % vb], writes=["VsU"])
                    QTv = QT[:].rearrange("p a (l r) -> p a r l", r=d)
                    accv = [acc[h][:].rearrange("p (l r) -> p r l", r=d) for h in range(2)]
                    vcols = (slice(0, 128), slice(64, 192))
                    groups = []
                    if d == 16:
                        for r0 in range(0, 16, 2):
                            groups.append([(r0, 0), (r0, 1), (r0 + 1, 0), (r0 + 1, 1)])
                    else:
                        for r in range(d):
                            for g0 in range(0, nqb, 4):
                                groups.append([(r, g0 + i) for i in range(4)])
                    pend = []

                    def emit_pv(item):
                        (r, qb, sidx, ob, slot, grp) = item
                        pt = PTb[sidx]
                        lo_ok = qb > 0
                        hi_ok = qb < nqb - 1
                        for h in range(2):
                            vcol = vcols[h]
                            pob = pO[h * 2 + ob]
                            o_ap = pob[:, slot * 128:(slot + 1) * 128]
                            ca = slice(h * 128, h * 128 + 128)
                            cb2 = slice(256 + h * 128, 256 + h * 128 + 128)
                            mm = []
                            if lo_ok:
                                mm.append(lambda e: e.matmul(o_ap, lhsT=Vsv[:, r, qb, vcol], rhs=pt[:, ca], start=(slot == 0), stop=False, skip_group_check=True))
                            else:
                                mm.append(lambda e: e.matmul(o_ap, lhsT=Vsv[0:64, r, qb, vcol], rhs=pt[0:64, ca], start=(slot == 0), stop=False, skip_group_check=True))
                            if hi_ok:
                                mm.append(lambda e: e.matmul(o_ap, lhsT=Vsv[:, r, qb + 1, vcol], rhs=pt[:, cb2], start=False, stop=True, skip_group_check=True))
                            else:
                                mm.append(lambda e: e.matmul(o_ap, lhsT=Vsv[64:128, r, qb + 1, vcol], rhs=pt[64:128, cb2], start=False, stop=True, skip_group_check=True))
                            cx.op("pe", mm, reads=["VsL", "VsU", "PTb%d" % sidx], writes=[("pO", h * 2 + ob, slot)])
                            if slot == 3:
                                if d == 16:
                                    r0 = grp[0][0]
                                    dst = accv[h][:, r0:r0 + 2, :]
                                    srcp = pob[:].rearrange("p (a l) -> p a l", a=2)
                                else:
                                    r_, q0 = grp[0]
                                    dst = accv[h][:, r_, q0 * 128:q0 * 128 + 512]
                                    srcp = pob[:]
                                if di == 0:
                                    cx.op("act", lambda e: e.copy(out=dst, in_=srcp),
                                          reads=[("pO", h * 2 + ob, s_) for s_ in range(4)], writes=["acc%d" % h])
                                else:
                                    cx.op("dve", lambda e: e.tensor_tensor(out=dst, in0=srcp, in1=dst, op=ALU.add),
                                          reads=[("pO", h * 2 + ob, s_) for s_ in range(4)] + ["acc%d" % h], writes=["acc%d" % h])

                    for grp in groups:
                        ob = oi % 2
                        oi += 1
                        for slot, (r, qb) in enumerate(grp):
                            sidx = si % 4
                            ebi = si % 2
                            psi = si % 2
                            si += 1
                            q_ap = QTv[:, :, r, qb * 128:(qb + 1) * 128]
                            kA = KPv[:, r, qb * 128:(qb + 1) * 128]
                            kB = KPv[:, r, (qb + 1) * 128:(qb + 2) * 128]
                            oA = pS[psi][:, 0:256].rearrange("p (a q) -> p a q", a=2)
                            oB = pS[psi][:, 256:512].rearrange("p (a q) -> p a q", a=2)
                            cx.op("pe", [lambda e: e.matmul(oA, lhsT=kA, rhs=q_ap, start=True, stop=True),
                                         lambda e: e.matmul(oB, lhsT=kB, rhs=q_ap, start=True, stop=True)],
                                  reads=QTk + KPk, writes=["pS%d" % psi])
                            cx.op("act", lambda e: e.activation(out=Eb[ebi][:], in_=pS[psi][:], func=AF.Exp, scale=0.125),
                                  reads=["pS%d" % psi], writes=["Eb%d" % ebi])
                            cx.op("dve", lambda e: e.tensor_tensor(out=PTb[sidx][:], in0=Eb[ebi][:], in1=EB[:, j, di, :], op=ALU.mult),
                                  reads=["Eb%d" % ebi, "EB"], writes=["PTb%d" % sidx])
                            pend.append((r, qb, sidx, ob, slot, grp))
                            if len(pend) > 2:
                                emit_pv(pend.pop(0))
                    while pend:
                        emit_pv(pend.pop(0))

                for cb_ in range(0 if os.environ.get('KNOEPI') else 8):
                    cs = slice(cb_ * 512, (cb_ + 1) * 512)
                    rden = wfr[:, (cb_ % 2) * 512:(cb_ % 2) * 512 + 512]
                    rk = "wfr"
                    cx.op("dve", [lambda e: e.tensor_copy(out=rden[0:64, :], in_=acc[0][64:128, cs]),
                                  lambda e: e.tensor_copy(out=rden[64:128, :], in_=acc[1][0:64, cs])],
                          reads=["acc0", "acc1"], writes=[rk])
                    cx.op("act", lambda e: e.activation(out=rden, in_=rden, func=AF.Ln), reads=[rk], writes=[rk])
                    cx.op("act", lambda e: e.activation(out=rden, in_=rden, func=AF.Exp, scale=-1.0), reads=[rk], writes=[rk])
                    ab_ = cb_ % 2
                    cx.op("dve", [lambda e: e.tensor_tensor(out=attn[ab_][0:64, :], in0=acc[0][0:64, cs], in1=rden[0:64, :], op=ALU.mult),
                                  lambda e: e.tensor_tensor(out=attn[ab_][64:128, :], in0=acc[1][64:128, cs], in1=rden[64:128, :], op=ALU.mult)],
                          reads=["acc0", "acc1", rk], writes=["attn%d" % ab_])
                    cx.dma("pool", mix_d[j * 128:(j + 1) * 128, cs], attn[ab_][:], reads=["attn%d" % ab_], writes=[("mix_d", j, cb_)])
        cx.barrier()
        if _STOP <= 5:
            return nc
        s2.close()
        with ExitStack() as pe2:
            wof = [sb(pe2, "wof%d" % i, [128, D], F32) for i in range(2)]
            wob = sb(pe2, "wob", [128, 8, D], BF16)
            gat = sb(pe2, "gat", [128, 6], F32)
            g2b = sb(pe2, "g2b", [128, D], F32)
            mixb = [sb(pe2, "mixb%d" % i, [128, 8, 512], BF16) for i in range(2)]
            sqb = [sb(pe2, "sqb%d" % i, [128, 512], BF16) for i in range(2)]
            rsa = sb(pe2, "rsa", [128, 512], F32)
            xr = [sb(pe2, "xr%d" % i, [128, D], F32) for i in range(2)]
            x1 = [sb(pe2, "x1_%d" % i, [128, D], F32) for i in range(2)]
            junkE = sb(pe2, "junkE", [128, D], F32)
            ssqE = [sb(pe2, "ssqE%d" % i, [128, 1], F32) for i in range(2)]
            rstdE = [sb(pe2, "rstdE%d" % i, [128, 1], F32) for i in range(2)]
            h2b = [sb(pe2, "h2b%d" % i, [128, D], BF16) for i in range(3)]
            h2T = [sb(pe2, "h2T%d" % i, [128, 8, 128], BF16) for i in range(2)]
            zcol = sb(pe2, "zcol", [128, 8, 1], BF16)
            pR = ps(pe2, "pR", [128, 512], F32)
            pY = [ps(pe2, "pY%d" % i, [128, D], F32) for i in range(2)]
            pT2 = [ps(pe2, "pT2%d" % i, [128, 8, 128], BF16) for i in range(2)]

            cx.dma("sp", gat[:], ga_d[:, :], writes=["gat"])
            cx.dma("sp", g2b[:], g2_d[:, :], writes=["g2b"])
            for c in range(8):
                b = c % 2
                cx.dma("sp", wof[b][:], wout_d[c * 128:(c + 1) * 128, :], writes=["wof%d" % b])
                cx.op("pool", lambda e: e.tensor_copy(out=wob[:, c, :], in_=wof[b][:]), reads=["wof%d" % b], writes=["wob"])
            cx.op("pool", lambda e: e.memset(zcol[:], 0.0), writes=["zcol"])
            h2v = h2_d.rearrange("(c p) s -> p c s", p=128)
            cx.dma("pool", h2v[:, :, 0:1], zcol[:], reads=["zcol"], writes=["h2halo"], allow_slow_non_contiguous=True)
            cx.dma("pool", h2v[:, :, S + 1:S + 2], zcol[:], reads=["zcol"], writes=["h2halo"], allow_slow_non_contiguous=True)
            mixv = mix_d.rearrange("(c p) s -> p c s", p=128)
            def e_prologue(blk):
                mb = blk % 2
                bs = slice(blk * 512, (blk + 1) * 512)
                cx.dma("sp", mixb[mb][:], mixv[:, :, bs], reads=[("mix_d", i, blk) for i in range(6)] + [("mix_d", 6), ("mix_d", 7)], writes=["mixb%d" % mb])
                for jj in range(6):
                    q = jj % 2
                    cx.op("act", lambda e: e.activation(out=sqb[q][:], in_=mixb[mb][:, jj, :], func=AF.Square),
                          reads=["mixb%d" % mb], writes=["sqb%d" % q])
                    cx.op("pe", lambda e: e.matmul(pR[:], lhsT=ones_b[:], rhs=sqb[q][:], start=(jj == 0), stop=(jj == 5)),
                          reads=["ones_b", "sqb%d" % q], writes=["pR"])
                cx.op("act", lambda e: e.activation(out=rsa[:], in_=pR[:], func=AF.Sqrt, scale=1.0 / 768, bias=epsb[:, 0:1]),
                      reads=["pR", "epsb"], writes=["rsa"])
                cx.op("dve", lambda e: e.reciprocal(out=rsa[:], in_=rsa[:]), reads=["rsa"], writes=["rsa"])
                for jj in range(6):
                    cx.op("dve", lambda e: e.scalar_tensor_tensor(out=mixb[mb][:, jj, :], in0=mixb[mb][:, jj, :],
                                                                  scalar=gat[:, jj:jj + 1], in1=rsa[:],
                                                                  op0=ALU.mult, op1=ALU.mult),
                          reads=["mixb%d" % mb, "gat", "rsa"], writes=["mixb%d" % mb])

            def e_s1(t):
                blk, tt = t // 4, t % 4
                mb = blk % 2
                b = t % 2
                b3 = t % 3
                cx.dma("sp", xr[b][:], x_d[t * 128:(t + 1) * 128, :], writes=["xr%d" % b])
                mm = []
                for hf in range(2):
                    for c in range(8):
                        mm.append(lambda e, hf=hf, c=c: e.matmul(pY[b][:, hf * 512:(hf + 1) * 512],
                                                                 lhsT=mixb[mb][:, c, tt * 128:(tt + 1) * 128],
                                                                 rhs=wob[:, c, hf * 512:(hf + 1) * 512],
                                                                 start=(c == 0), stop=(c == 7)))
                cx.op("pe", mm, reads=["mixb%d" % mb, "wob"], writes=["pY%d" % b])
                cx.op("dve", lambda e: e.tensor_tensor(out=x1[b][:], in0=pY[b][:], in1=xr[b][:], op=ALU.add),
                      reads=["pY%d" % b, "xr%d" % b], writes=["x1_%d" % b])
                cx.dma("pool", x1_d[t * 128:(t + 1) * 128, :], x1[b][:], reads=["x1_%d" % b], writes=[("x1_d", t)])
                rms_rstd((junkE[:], ssqE[b][:], rstdE[b][:], "junkE", "ssqE%d" % b, "rstdE%d" % b),
                         x1[b][:], "x1_%d" % b, D, "E")
                cx.op("dve", lambda e: e.scalar_tensor_tensor(out=h2b[b3][:], in0=x1[b][:], scalar=rstdE[b][:, 0:1],
                                                              in1=g2b[:], op0=ALU.mult, op1=ALU.mult),
                      reads=["x1_%d" % b, "rstdE%d" % b, "g2b"], writes=["h2b%d" % b3])

            def e_s2(t):
                b = t % 2
                b3 = t % 3
                cx.op("pe", [(lambda e, c=c: e.transpose(out=pT2[b][:, c, :], in_=h2b[b3][:, c * 128:(c + 1) * 128],
                                                         identity=ident[:])) for c in range(8)],
                      reads=["h2b%d" % b3, "ident"], writes=["pT2%d" % b])
                cx.op("act", lambda e: e.copy(out=h2T[b][:], in_=pT2[b][:]), reads=["pT2%d" % b], writes=["h2T%d" % b])
                cx.dma("pool", h2v[:, :, 1 + t * 128:1 + (t + 1) * 128], h2T[b][:], reads=["h2T%d" % b],
                       writes=[("h2_d", t)])

            e_prologue(0)
            for t in range(NT):
                if t % 4 == 0 and t // 4 + 1 < 8:
                    e_prologue(t // 4 + 1)
                e_s1(t)
                if t >= 2:
                    e_s2(t - 2)
            e_s2(NT - 2)
            e_s2(NT - 1)
        cx.barrier()
        if _STOP <= 6:
            return nc
        with ExitStack() as pf:
            wgb = sb(pf, "wgb", [128, 8, DFF], BF16)
            wvb = sb(pf, "wvb", [128, 8, DFF], BF16)
            wdb = sb(pf, "wdb", [128, NF, D], BF16)
            wst = [sb(pf, "wst%d" % i, [128, DFF // 2], F32) for i in range(2)]
            cwt = sb(pf, "cwt", [128, NF, 3], F32)
            cbt = sb(pf, "cbt", [128, NF], F32)
            gFb = sb(pf, "gFb", [128, D], F32)
            h2s = [sb(pf, "h2s%d" % i, [128, 8, 258], BF16) for i in range(2)]
            t1 = [sb(pf, "t1_%d" % i, [128, 256], F32) for i in range(2)]
            t2 = [sb(pf, "t2_%d" % i, [128, 256], F32) for i in range(2)]
            sg = [sb(pf, "sg_%d" % i, [128, 256], F32) for i in range(2)]
            aT = [sb(pf, "aT_%d" % i, [128, 256], BF16) for i in range(4)]
            x1r = [sb(pf, "x1r%d" % i, [128, D], F32) for i in range(2)]
            of_ = [sb(pf, "of%d" % i, [128, D], F32) for i in range(2)]
            junkF = sb(pf, "junkF", [128, D], F32)
            ssqF = [sb(pf, "ssqF%d" % i, [128, 1], F32) for i in range(2)]
            rstdF = [sb(pf, "rstdF%d" % i, [128, 1], F32) for i in range(2)]
            yo = [sb(pf, "yo%d" % i, [128, D], F32) for i in range(2)]
            pGt = [ps(pf, "pGt%d" % i, [128, 512], F32) for i in range(2)]
            pVl = [ps(pf, "pVl%d" % i, [128, 512], F32) for i in range(2)]
            pD = [ps(pf, "pD%d" % i, [128, D], F32) for i in range(2)]

            cx.dma("sp", cwt[:], cw_d[:, :, :], writes=["cwt"])
            cx.dma("sp", cbt[:], cb_d[:, :], writes=["cbt"])
            cx.dma("sp", gFb[:], gF_d[:, :], writes=["gFb"])
            wi = 0
            for (src_d, dstw, key) in ((wg_d, wgb, "wgb"), (wv_d, wvb, "wvb")):
                for c in range(16):
                    b = wi % 2
                    wi += 1
                    c_, hf_ = c // 2, c % 2
                    cols = slice(hf_ * (DFF // 2), (hf_ + 1) * (DFF // 2))
                    cx.dma("sp", wst[b][:], src_d[c_ * 128:(c_ + 1) * 128, cols], writes=["wst%d" % b])
                    eng = "pool" if (wi % 2) else "dve"
                    cx.op(eng, lambda e: e.tensor_copy(out=dstw[:, c_, cols], in_=wst[b][:]), reads=["wst%d" % b], writes=[key])
            for f in range(NF):
                b = wi % 2
                wi += 1
                cx.dma("sp", wst[b][:, 0:D], wd_d[f * 128:(f + 1) * 128, :], writes=["wst%d" % b])
                eng = "pool" if (wi % 2) else "dve"
                cx.op(eng, lambda e: e.tensor_copy(out=wdb[:, f, :], in_=wst[b][:, 0:D]), reads=["wst%d" % b], writes=["wdb"])

            h2v = h2_d.rearrange("(c p) s -> p c s", p=128)
            _SUB = int(os.environ.get("KSUB", "9"))
            def load_h2s(st_):
                cx.dma("sp", h2s[st_ % 2][:], h2v[:, :, st_ * 256:st_ * 256 + 258],
                       reads=[("h2_d", t) for t in range(max(0, 2 * st_ - 1), min(NT, 2 * st_ + 3))] + ["h2halo"],
                       writes=["h2s%d" % (st_ % 2)])

            load_h2s(0)
            for st in range(16 if _SUB > 1 else 0):
                hb_ = st % 2
                for tt in range(2):
                    t = st * 2 + tt
                    cx.dma("sp", x1r[t % 2][:], x1_d[t * 128:(t + 1) * 128, :], reads=[("x1_d", t)], writes=["x1r%d" % (t % 2)])
                if st + 1 < 16:
                    load_h2s(st + 1)
                pend = []

                def emit_down(item):
                    f, ab = item
                    mm = []
                    for tt in range(2):
                        for hf in range(2):
                            mm.append(lambda e, tt=tt, hf=hf: e.matmul(pD[tt][:, hf * 512:(hf + 1) * 512],
                                                                       lhsT=aT[ab][:, tt * 128:(tt + 1) * 128],
                                                                       rhs=wdb[:, f, hf * 512:(hf + 1) * 512],
                                                                       start=(f == 0), stop=(f == NF - 1)))
                    cx.op("pe", mm, reads=["aT_%d" % ab, "wdb"], writes=(["pD0", "pD1"] if f in (0, NF - 1) else []))

                for f in range(NF):
                    b = f % 2
                    ab = f % 4
                    fs = slice(f * 128, (f + 1) * 128)
                    cx.op("pe", [(lambda e, c=c: e.matmul(pGt[b][:, 0:258], lhsT=wgb[:, c, fs], rhs=h2s[hb_][:, c, 0:258],
                                                          start=(c == 0), stop=(c == 7))) for c in range(8)],
                          reads=["wgb", "h2s%d" % hb_], writes=["pGt%d" % b])
                    cx.op("pe", [(lambda e, c=c: e.matmul(pVl[b][:, 0:256], lhsT=wvb[:, c, fs], rhs=h2s[hb_][:, c, 1:257],
                                                          start=(c == 0), stop=(c == 7))) for c in range(8)],
                          reads=["wvb", "h2s%d" % hb_], writes=["pVl%d" % b])
                    cx.op("act", lambda e: e.activation(out=t1[b][:], in_=pGt[b][:, 1:257], func=AF.Identity,
                                                        scale=cwt[:, f, 1:2], bias=cbt[:, f:f + 1]),
                          reads=["pGt%d" % b, "cwt", "cbt"], writes=["t1_%d" % b])
                    cx.op("dve", lambda e: e.scalar_tensor_tensor(out=t2[b][:], in0=pGt[b][:, 0:256], scalar=cwt[:, f, 0:1],
                                                                  in1=t1[b][:], op0=ALU.mult, op1=ALU.add),
                          reads=["pGt%d" % b, "cwt", "t1_%d" % b], writes=["t2_%d" % b])
                    cx.op("dve", lambda e: e.scalar_tensor_tensor(out=t1[b][:], in0=pGt[b][:, 2:258], scalar=cwt[:, f, 2:3],
                                                                  in1=t2[b][:], op0=ALU.mult, op1=ALU.add),
                          reads=["pGt%d" % b, "cwt", "t2_%d" % b], writes=["t1_%d" % b])
                    cx.op("act", lambda e: e.activation(out=sg[b][:], in_=t1[b][:], func=AF.Silu),
                          reads=["t1_%d" % b], writes=["sg_%d" % b])
                    cx.op("dve", lambda e: e.tensor_tensor(out=aT[ab][:], in0=pVl[b][:, 0:256], in1=sg[b][:], op=ALU.mult),
                          reads=["pVl%d" % b, "sg_%d" % b], writes=["aT_%d" % ab])
                    if _SUB > 2:
                        pend.append((f, ab))
                    if len(pend) > 2:
                        emit_down(pend.pop(0))
                while pend:
                    emit_down(pend.pop(0))
                for tt in range(2 if _SUB > 3 else 0):
                    t = st * 2 + tt
                    b = t % 2
                    cx.op("dve", lambda e: e.tensor_tensor(out=of_[b][:], in0=pD[tt][:], in1=x1r[b][:], op=ALU.add),
                          reads=["pD%d" % tt, "x1r%d" % b], writes=["of%d" % b])
                    rms_rstd((junkF[:], ssqF[b][:], rstdF[b][:], "junkF", "ssqF%d" % b, "rstdF%d" % b),
                             of_[b][:], "of%d" % b, D, "F")
                    cx.op("dve", lambda e: e.scalar_tensor_tensor(out=yo[b][:], in0=of_[b][:], scalar=rstdF[b][:, 0:1],
                                                                  in1=gFb[:], op0=ALU.mult, op1=ALU.mult),
                          reads=["of%d" % b, "rstdF%d" % b, "gFb"], writes=["yo%d" % b])
                    cx.dma(os.environ.get("KYQ", "sp"), y_d[t * 128:(t + 1) * 128, :], yo[b][:], reads=["yo%d" % b], writes=[("y", t)])
            cx.finish("pool", [("y", t) for t in range(NT)])
            cx.barrier()
    return nc


def _t5_bucket_np(rel):
    nb = 16
    max_exact = 8
    ret = np.where(rel > 0, nb, 0)
    n = np.abs(rel)
    nf = np.maximum(n, 1).astype(np.float32)
    large = max_exact + (np.log(nf / np.float32(max_exact)) / np.float32(math.log(1024 / max_exact))
                         * np.float32(nb - max_exact)).astype(np.int32)
    large = np.minimum(large, nb - 1)
    return ret + np.where(n < max_exact, n, large)


_CONST = {}


def _constants():
    if _CONST:
        return _CONST
    bf = ml_dtypes.bfloat16
    m = np.arange(P_G)
    rel = 191 - m
    oh = np.zeros((32, 3, P_G), np.float32)
    gmask = np.zeros((12, 3, P_G), np.float32)
    for di, d in enumerate(DILS):
        bk = _t5_bucket_np(rel * d)
        valid = np.abs(rel) <= 64
        oh[bk[valid], di, m[valid]] = 1.0
        gmask[:, di, valid] = 1.0
    _CONST["onehot"] = oh
    _CONST["gmask"] = gmask
    c = np.arange(64)
    ang = 2 * np.pi * np.outer(c, c) / 64
    C64 = np.cos(ang) / 512.0
    S64 = -np.sin(ang) / 512.0
    z = np.zeros((64, 64))
    _CONST["c64blk"] = np.block([[C64, z], [z, C64]]).astype(bf)
    _CONST["s64blk"] = np.block([[S64, z], [z, S64]]).astype(bf)
    _CONST["ident"] = np.eye(128, dtype=np.float32).astype(bf)
    s = np.arange(S, dtype=np.int64)
    prod = (s[:, None] * s[None, :]) % S
    angt = prod.astype(np.float64) * (2 * np.pi / S)
    for name, fn in (("dft_cos", np.cos), ("dft_sin", np.sin)):
        t = fn(angt).astype(np.float32)
        t = t.reshape(32, 128, 8, 512).transpose(2, 1, 0, 3)
        _CONST[name] = np.ascontiguousarray(t).astype(bf)
    return _CONST


_NC = {}


def kernel(x, norm_mix_gain, w_in, attn_out_gain, rel_bias_table, fourier_w, fourier_b, fourier_out_gain,
           w_out, norm_ffn_gain, w_gate, w_val, conv_w, conv_b, w_down, final_norm_gain):
    f32 = np.float32
    x = np.asarray(x, f32)
    cst = _constants()
    bc = lambda v, n: np.ascontiguousarray(np.broadcast_to(np.asarray(v, f32).reshape(1, n), (128, n)))
    fw = np.asarray(fourier_w, f32)[0]
    fw_blk = np.zeros((128, 2, 128), f32)
    for cc in range(2):
        fw_blk[0:64, cc, 0:64] = fw[2 * cc]
        fw_blk[64:128, cc, 64:128] = fw[2 * cc + 1]
    shared = {
        "g1b": bc(norm_mix_gain[0], D), "g2b": bc(norm_ffn_gain[0], D), "gFb": bc(final_norm_gain, D),
        "gfb": bc(fourier_out_gain[0], 256),
        "ga_t": np.ascontiguousarray(np.asarray(attn_out_gain, f32)[0].reshape(6, 128).T),
        "w_in": np.ascontiguousarray(np.asarray(w_in, f32)[0]),
        "w_out": np.ascontiguousarray(np.asarray(w_out, f32)[0]),
        "w_gate": np.ascontiguousarray(np.asarray(w_gate, f32)[0]),
        "w_val": np.ascontiguousarray(np.asarray(w_val, f32)[0]),
        "w_down": np.ascontiguousarray(np.asarray(w_down, f32)[0]),
        "cw_t": np.ascontiguousarray(np.asarray(conv_w, f32)[0].reshape(3, NF, 128).transpose(2, 1, 0)),
        "cb_t": np.ascontiguousarray(np.asarray(conv_b, f32)[0].reshape(NF, 128).T),
        "rel_tab": np.ascontiguousarray(np.asarray(rel_bias_table, f32)),
        "onehot": cst["onehot"], "gmask": cst["gmask"],
        "fw_blk": fw_blk,
        "fb_row": np.ascontiguousarray(np.asarray(fourier_b, f32)[0].reshape(1, 256)),
        "c64blk": cst["c64blk"], "s64blk": cst["s64blk"], "ident": cst["ident"],
        "dft_cos": cst["dft_cos"], "dft_sin": cst["dft_sin"],
    }
    n = x.shape[0]
    if "nc" not in _NC:
        _NC["nc"] = build_nc()
    nc = _NC["nc"]
    in_maps = [dict(shared, x=np.ascontiguousarray(x[b])) for b in range(n)]
    res = run_bass_kernel_spmd(nc, in_maps, core_ids=list(range(n)))
    return np.stack([np.asarray(r["y"], f32) for r in res.results], axis=0)
```

```python
import math
import os
from contextlib import ExitStack

import numpy as np
import ml_dtypes
import concourse.bass as bass
import concourse.mybir as mybir
from concourse.bass_utils import run_bass_kernel_spmd

F32 = mybir.dt.float32
BF16 = mybir.dt.bfloat16
AF = mybir.ActivationFunctionType
ALU = mybir.AluOpType

S = 4096
D = 1024
NT = 32
DFF = 2816
NF = 22
EPS = 1e-6
DILS = (1, 4, 16)
P_G = 384
REP = 64


class Ctx:
    def __init__(self, nc, es):
        self.nc = nc
        self.E = {"pe": nc.tensor, "act": nc.scalar, "dve": nc.vector, "pool": nc.gpsimd, "sp": nc.sync}
        self.sem = {k: es.enter_context(nc.semaphore("sem_" + k)) for k in self.E}
        self.cnt = {k: 0 for k in self.E}
        self.seen = {k: {} for k in self.E}
        self.lastw = {}
        self.readers = {}
        nd = 64
        self.dsem = [es.enter_context(nc.semaphore("dsem%d" % i)) for i in range(nd)]
        self.dval = [0] * nd
        self.dnext = 0

    def _wait(self, eng, tok):
        sem, key, val = tok
        if self.seen[eng].get(key, 0) >= val:
            return
        self.E[eng].wait_ge(sem, val)
        self.seen[eng][key] = val

    def _deps(self, eng, reads, writes):
        for k in list(reads) + list(writes):
            t = self.lastw.get(k)
            if t is not None:
                self._wait(eng, t)
        for k in writes:
            for t in self.readers.get(k, {}).values():
                self._wait(eng, t)

    def _commit(self, tok, reads, writes):
        for k in writes:
            self.lastw[k] = tok
            self.readers[k] = {}
        for k in reads:
            d = self.readers.setdefault(k, {})
            if tok[1] not in d or d[tok[1]][2] < tok[2]:
                d[tok[1]] = tok

    def op(self, eng, fns, reads=(), writes=()):
        self._deps(eng, reads, writes)
        if callable(fns):
            fns = [fns]
        ins = None
        for f in fns:
            ins = f(self.E[eng])
        self.cnt[eng] += 1
        ins.then_inc(self.sem[eng], 1)
        tok = (self.sem[eng], eng, self.cnt[eng])
        self._commit(tok, reads, writes)

    def dma(self, q, out, in_, reads=(), writes=(), **kw):
        self._deps(q, reads, writes)
        i = self.dnext
        self.dnext = (i + 1) % len(self.dsem)
        key = "d%d" % i
        if self.dval[i] > 0:
            self._wait(q, (self.dsem[i], key, self.dval[i]))
        self.E[q].dma_start(out=out, in_=in_, **kw).then_inc(self.dsem[i], 16)
        self.dval[i] += 16
        tok = (self.dsem[i], key, self.dval[i])
        self._commit(tok, reads, writes)

    def barrier(self):
        if os.environ.get("KDBG"):
            print("barrier counts", self.cnt, max(self.dval))
        toks = [(self.sem[k], k, self.cnt[k]) for k in self.E if self.cnt[k] > 0]
        toks += [(self.dsem[i], "d%d" % i, self.dval[i]) for i in range(len(self.dsem)) if self.dval[i] > 0]
        for eng in self.E:
            for t in toks:
                if t[1] != eng:
                    self._wait(eng, t)

    def finish(self, eng, keys):
        for k in keys:
            t = self.lastw.get(k)
            if t is not None:
                self._wait(eng, t)


_STOP = int(os.environ.get('KSTOP', '9'))


def build_nc():
    nc = bass.Bass("TRN2", target_bir_lowering=False)

    def din(name, shape, dt=F32):
        return nc.dram_tensor(name, list(shape), dt, kind="ExternalInput").ap()

    def dscr(name, shape, dt):
        return nc.dram_tensor(name, list(shape), dt, kind="Internal").ap()

    x_d = din("x", [S, D])
    g1_d = din("g1b", [128, D])
    g2_d = din("g2b", [128, D])
    gF_d = din("gFb", [128, D])
    gf_d = din("gfb", [128, 256])
    ga_d = din("ga_t", [128, 6])
    win_d = din("w_in", [D, 2560])
    wout_d = din("w_out", [D, D])
    wg_d = din("w_gate", [D, DFF])
    wv_d = din("w_val", [D, DFF])
    wd_d = din("w_down", [DFF, D])
    cw_d = din("cw_t", [128, NF, 3])
    cb_d = din("cb_t", [128, NF])
    tab_d = din("rel_tab", [32, 12])
    oh_d = din("onehot", [32, 3, P_G])
    gm_d = din("gmask", [12, 3, P_G])
    fwb_d = din("fw_blk", [128, 2, 128])
    fb_d = din("fb_row", [1, 256])
    c64_d = din("c64blk", [128, 128], BF16)
    s64_d = din("s64blk", [128, 128], BF16)
    id_d = din("ident", [128, 128], BF16)
    tabc_d = din("dft_cos", [8, 128, 32, 512], BF16)
    tabs_d = din("dft_sin", [8, 128, 32, 512], BF16)
    y_d = nc.dram_tensor("y", [S, D], F32, kind="ExternalOutput").ap()

    mix_d = dscr("mix_scr", [D, S], BF16)
    x1_d = dscr("x1_scr", [S, D], F32)
    h2_d = dscr("h2_scr", [D, S + 2], BF16)
    gr_d = dscr("gr_scr", [12, 3, REP, P_G], BF16)

    with ExitStack() as es:
        cx = Ctx(nc, es)

        def sb(stack, name, shape, dt):
            return stack.enter_context(nc.sbuf_tensor("sb_" + name, list(shape), dt))

        def ps(stack, name, shape, dt):
            return stack.enter_context(nc.psum_tensor("ps_" + name, list(shape), dt))

        ident = sb(es, "ident", [128, 128], BF16)
        ones_b = sb(es, "ones_b", [128, 128], BF16)
        epsb = sb(es, "epsb", [128, 1], F32)
        s2 = es.enter_context(ExitStack())
        hT = sb(s2, "hT", [128, 8, S], BF16)
        cx.dma("sp", ident[:], id_d[:, :], writes=["ident"])
        cx.op("pool", lambda e: e.memset(ones_b[:], 1.0), writes=["ones_b"])
        cx.op("pool", lambda e: e.memset(epsb[:], EPS), writes=["epsb"])

        def rms_rstd(stack_tiles, src_ap, src_key, n, tag):
            junk, ssq, rstd, kj, ks, kr = stack_tiles
            cx.op("act", lambda e: e.activation(out=junk, in_=src_ap, func=AF.Square, accum_out=ssq),
                  reads=[src_key], writes=[kj, ks])
            cx.op("act", lambda e: e.activation(out=rstd, in_=ssq, func=AF.Sqrt, scale=1.0 / n, bias=epsb[:, 0:1]),
                  reads=[ks, "epsb"], writes=[kr])
            cx.op("dve", lambda e: e.reciprocal(out=rstd, in_=rstd), reads=[kr], writes=[kr])

        with ExitStack() as pa:
            g1b = sb(pa, "g1b", [128, D], F32)
            tab = sb(pa, "tab", [32, 12], F32)
            oh = sb(pa, "oh", [32, 3, P_G], F32)
            gm = sb(pa, "gm", [12, 3, P_G], F32)
            gsb = sb(pa, "gsb", [12, 3, P_G], F32)
            gbf = sb(pa, "gbf", [12, 3, P_G], BF16)
            pG = ps(pa, "pG", [12, 3, 512], F32)
            cx.dma("sp", g1b[:], g1_d[:, :], writes=["g1b"])
            xt = [sb(pa, "xt%d" % i, [128, D], F32) for i in range(2)]
            junk = sb(pa, "junkA", [128, D], F32)
            hb = [sb(pa, "hb%d" % i, [128, D], BF16) for i in range(2)]
            ssq = [sb(pa, "ssqA%d" % i, [128, 1], F32) for i in range(2)]
            rstd = [sb(pa, "rstdA%d" % i, [128, 1], F32) for i in range(2)]
            pT = [ps(pa, "pTA%d" % i, [128, 8, 128], BF16) for i in range(2)]
            def a_s2(t):
                b = t % 2
                cx.op("pe", [(lambda e, c=c: e.transpose(out=pT[b][:, c, :], in_=hb[b][:, c * 128:(c + 1) * 128],
                                                         identity=ident[:])) for c in range(8)],
                      reads=["hb%d" % b, "ident"], writes=["pTA%d" % b])
                cx.op("act", lambda e: e.copy(out=hT[:, :, t * 128:(t + 1) * 128], in_=pT[b][:]),
                      reads=["pTA%d" % b], writes=[("hT", t // 4)])

            for t in range(NT):
                b = t % 2
                cx.dma("sp", xt[b][:], x_d[t * 128:(t + 1) * 128, :], writes=["xt%d" % b])
                rms_rstd((junk[:], ssq[b][:], rstd[b][:], "junkA", "ssqA%d" % b, "rstdA%d" % b),
                         xt[b][:], "xt%d" % b, D, "A")
                cx.op("dve", lambda e: e.scalar_tensor_tensor(out=hb[b][:], in0=xt[b][:], scalar=rstd[b][:, 0:1],
                                                              in1=g1b[:], op0=ALU.mult, op1=ALU.mult),
                      reads=["xt%d" % b, "rstdA%d" % b, "g1b"], writes=["hb%d" % b])
                if t >= 1:
                    a_s2(t - 1)
            a_s2(NT - 1)
            cx.dma("sp", tab[:], tab_d[:, :], writes=["tab"])
            cx.dma("sp", oh[:], oh_d[:, :, :], writes=["oh"])
            cx.dma("sp", gm[:], gm_d[:, :, :], writes=["gm"])
            cx.op("pe", [(lambda e, d=d: e.matmul(pG[:, d, 0:P_G], lhsT=tab[:], rhs=oh[:, d, :], start=True, stop=True))
                         for d in range(3)], reads=["tab", "oh"], writes=["pG"])
            cx.op("act", lambda e: e.activation(out=gsb[:], in_=pG[:, :, 0:P_G], func=AF.Exp),
                  reads=["pG"], writes=["gsb"])
            cx.op("dve", lambda e: e.tensor_tensor(out=gbf[:], in0=gsb[:], in1=gm[:], op=ALU.mult),
                  reads=["gsb", "gm"], writes=["gbf"])
            cx.dma("sp", gr_d[:, :, 0, :], gbf[:], reads=["gbf"], writes=["gr_d"])
            n = 1
            while n < REP:
                cx.dma("sp", gr_d[:, :, n:2 * n, :], gr_d[:, :, 0:n, :], reads=["gr_d"], writes=["gr_d"])
                n *= 2
        cx.barrier()
        if _STOP <= 1:
            return nc
        sD = es.enter_context(ExitStack())
        usb = sb(sD, "usb", [128, NT, 256], BF16)
        with ExitStack() as pu:
            wuf = sb(pu, "wuf", [128, 8, 256], F32)
            wub = sb(pu, "wub", [128, 8, 256], BF16)
            pU = [ps(pu, "pUu%d" % i, [128, 512], F32) for i in range(2)]
            cx.dma("sp", wuf[:], win_d.rearrange("(c p) n -> p c n", p=128)[:, :, 2304:2560], writes=["wuf"])
            cx.op("pool", lambda e: e.tensor_copy(out=wub[:], in_=wuf[:]), reads=["wuf"], writes=["wub"])
            for t in range(NT):
                b = t % 2
                cx.op("pe", [(lambda e, c=c: e.matmul(pU[b][:, 0:256], lhsT=hT[:, c, t * 128:(t + 1) * 128], rhs=wub[:, c, :],
                                                      start=(c == 0), stop=(c == 7))) for c in range(8)],
                      reads=[("hT", t // 4), "wub"], writes=["pUu%d" % b])
                cx.op("dve" if b else "act",
                      (lambda e: e.tensor_copy(out=usb[:, t, :], in_=pU[b][:, 0:256])) if b else
                      (lambda e: e.copy(out=usb[:, t, :], in_=pU[b][:, 0:256])),
                      reads=["pUu%d" % b], writes=[("usb", t)])
        cx.barrier()
        if _STOP <= 2:
            return nc
        with ExitStack() as pd:
            ABT = sb(pd, "ABT", [128, 2, 2, S], BF16)
            tabb = [sb(pd, "tabb%d" % i, [128, 32, 512], BF16) for i in range(2)]
            c64 = sb(pd, "c64", [128, 128], BF16)
            s64 = sb(pd, "s64", [128, 128], BF16)
            fwf = sb(pd, "fwf", [128, 2, 128], F32)
            fwb = sb(pd, "fwb", [128, 2, 128], BF16)
            M12 = sb(pd, "M12", [128, 2, 2, 128], BF16)
            fbf = sb(pd, "fbf", [1, 256], F32)
            fbb = sb(pd, "fbb", [1, 256], BF16)
            gfb = sb(pd, "gfb", [128, 256], F32)
            fjunk = sb(pd, "fjunk", [128, 256], F32)
            fssq = [sb(pd, "fssq%d" % i, [128, 1], F32) for i in range(2)]
            frstd = [sb(pd, "frstd%d" % i, [128, 1], F32) for i in range(2)]
            fnb = [sb(pd, "fnb%d" % i, [128, 256], BF16) for i in range(2)]
            fourT = sb(pd, "fourT", [128, 2, S], BF16)
            pU = [ps(pd, "pU%d" % i, [128, 512], F32) for i in range(2)]
            pF = [ps(pd, "pF%d" % i, [128, 512], F32) for i in range(2)]
            pM = ps(pd, "pM", [128, 512], F32)
            pFT = [ps(pd, "pFT%d" % i, [128, 2, 128], BF16) for i in range(2)]

            cx.dma("sp", c64[:], c64_d[:, :], writes=["c64"])
            cx.dma("sp", s64[:], s64_d[:, :], writes=["s64"])
            cx.dma("sp", fwf[:], fwb_d[:, :, :], writes=["fwf"])
            cx.dma("sp", fbf[:], fb_d[:, :], writes=["fbf"])
            cx.dma("sp", gfb[:], gf_d[:, :], writes=["gfb"])
            cx.op("pool", lambda e: e.tensor_copy(out=fwb[:], in_=fwf[:]), reads=["fwf"], writes=["fwb"])
            cx.op("pool", lambda e: e.tensor_copy(out=fbb[:], in_=fbf[:]), reads=["fbf"], writes=["fbb"])
            cx.op("pe", [(lambda e, cs=cs, cc=cc: e.matmul(pM[:, (cs * 2 + cc) * 128:(cs * 2 + cc + 1) * 128],
                                                           lhsT=(c64 if cs == 0 else s64)[:], rhs=fwb[:, cc, :],
                                                           start=True, stop=True)) for cs in range(2) for cc in range(2)],
                  reads=["c64", "s64", "fwb"], writes=["pM"])
            cx.op("act", lambda e: e.copy(out=M12[:].rearrange("p a b e -> p (a b e)"), in_=pM[:]),
                  reads=["pM"], writes=["M12"])
            it = 0
            for sbk in range(8):
                for cs, tsrc in ((0, tabc_d), (1, tabs_d)):
                    tb_ = it % 2
                    it += 1
                    cx.dma("sp", tabb[tb_][:], tsrc[sbk, :, :, :], writes=["tabb%d" % tb_])
                    for cc in range(2):
                        b = cc
                        cx.op("pe", [(lambda e, k=k: e.matmul(pF[b][:], lhsT=usb[:, k, cc * 128:(cc + 1) * 128],
                                                              rhs=tabb[tb_][:, k, :], start=(k == 0), stop=(k == 31)))
                                     for k in range(32)],
                              reads=[("usb", k_) for k_ in range(NT)] + ["tabb%d" % tb_], writes=["pF%d" % b])
                        if cc == 0:
                            cx.op("act", lambda e: e.copy(out=ABT[:, cs, cc, sbk * 512:(sbk + 1) * 512], in_=pF[b][:]),
                                  reads=["pF%d" % b], writes=["ABT"])
                        else:
                            cx.op("dve", lambda e: e.tensor_copy(out=ABT[:, cs, cc, sbk * 512:(sbk + 1) * 512], in_=pF[b][:]),
                                  reads=["pF%d" % b], writes=["ABT"])
            def f_s1(t):
                b = t % 2
                ts_ = slice(t * 128, (t + 1) * 128)
                mm = []
                for cc in range(2):
                    o_ap = pU[b][:, cc * 128:(cc + 1) * 128]
                    mm.append(lambda e, cc=cc, o_ap=o_ap: e.matmul(o_ap, lhsT=ABT[:, 0, cc, ts_], rhs=M12[:, 0, cc, :], start=True, stop=False))
                    mm.append(lambda e, cc=cc, o_ap=o_ap: e.matmul(o_ap, lhsT=ABT[:, 1, cc, ts_], rhs=M12[:, 1, cc, :], start=False, stop=False))
                    mm.append(lambda e, cc=cc, o_ap=o_ap: e.matmul(o_ap, lhsT=ones_b[0:1, :], rhs=fbb[0:1, cc * 128:(cc + 1) * 128], start=False, stop=True))
                cx.op("pe", mm, reads=["ABT", "M12", "ones_b", "fbb"], writes=["pU%d" % b])
                rms_rstd((fjunk[:], fssq[b][:], frstd[b][:], "fjunk", "fssq%d" % b, "frstd%d" % b),
                         pU[b][:, 0:256], "pU%d" % b, 256, "F")
                cx.op("dve", lambda e: e.scalar_tensor_tensor(out=fnb[b][:], in0=pU[b][:, 0:256], scalar=frstd[b][:, 0:1],
                                                              in1=gfb[:], op0=ALU.mult, op1=ALU.mult),
                      reads=["pU%d" % b, "frstd%d" % b, "gfb"], writes=["fnb%d" % b])

            def f_s2(t):
                b = t % 2
                ts_ = slice(t * 128, (t + 1) * 128)
                cx.op("pe", [(lambda e, cc=cc: e.transpose(out=pFT[b][:, cc, :], in_=fnb[b][:, cc * 128:(cc + 1) * 128],
                                                           identity=ident[:])) for cc in range(2)],
                      reads=["fnb%d" % b, "ident"], writes=["pFT%d" % b])
                cx.op("act", lambda e: e.copy(out=fourT[:, :, ts_], in_=pFT[b][:]),
                      reads=["pFT%d" % b], writes=["fourT"])

            for t in range(NT):
                f_s1(t)
                if t >= 1:
                    f_s2(t - 1)
            f_s2(NT - 1)
            for cc in range(2):
                cx.dma("pool", mix_d[768 + cc * 128:768 + (cc + 1) * 128, :], fourT[:, cc, :], reads=["fourT"],
                       writes=[("mix_d", 6 + cc)])

        cx.barrier()
        if _STOP <= 3:
            return nc
        sD.close()
        with ExitStack() as pb:
            EB = sb(pb, "EB", [128, 6, 3, 512], BF16)
            if _STOP <= 4:
                return nc
            wfr = sb(pb, "wfr", [128, 1024], F32)
            wf = [wfr[:].rearrange("p (c n) -> p c n", c=8)]
            wqkv = sb(pb, "wqkv", [128, 3, 8, 128], BF16)
            QT = sb(pb, "QT", [128, 2, S], BF16)
            VT = sb(pb, "VT", [128, S], BF16)
            KPb = [sb(pb, "KP%d" % d, [128, d * (S // d + 128)], BF16) for d in DILS]
            Vs = sb(pb, "Vs", [128, 48, 192], BF16)
            acc = [sb(pb, "acc%d" % i, [128, S], F32) for i in range(2)]
            attn = [sb(pb, "attn%d" % i, [128, 512], BF16) for i in range(2)]
            Eb = [sb(pb, "Eb%d" % i, [128, 512], BF16) for i in range(2)]
            PTb = [sb(pb, "PTb%d" % i, [128, 512], BF16) for i in range(4)]
            pVt2 = [ps(pb, "pVt%d" % i, [128, 8, 128], BF16) for i in range(2)]
            vbi = 0
            pS = [ps(pb, "pS%d" % i, [128, 512], F32) for i in range(2)]
            pI = pS
            pO = [ps(pb, "pO%d" % i, [128, 512], F32) for i in range(4)]

            cx.op("pool", lambda e: e.memset(Vs[:], 1.0), writes=["VsL", "VsU"])
            for i_ in range(3):
                cx.op("pool", lambda e: e.memset(KPb[i_][:], 0.0), writes=[("KP", i_)])
            cx.op("pool", lambda e: e.memset(QT[:], 0.0), writes=[("QT", i) for i in range(8)])
            win_v = win_d.rearrange("(c p) n -> p c n", p=128)
            for j in range(6):
                for wi, col0 in enumerate((j * 128, 768 + j * 128, 1536 + j * 128)):
                    b = 0
                    cx.dma("sp", wf[b], win_v[:, :, col0:col0 + 128], writes=["wfr"])
                    cx.op("pool", lambda e: e.tensor_copy(out=wqkv[:, wi, :, :], in_=wf[b]),
                          reads=["wfr"], writes=[("wqkv", wi)])
                if j == 0:
                    for jp in range(6):
                        for d_ in range(3):
                            for (r0, off) in ((0, 63), (64, 127)):
                                for h_ in range(2):
                                    base = ((2 * jp + h_) * 3 + d_) * REP * P_G
                                    src = bass.AP(gr_d.tensor, base + off, [[P_G - 1, 64], [128, 2], [1, 128]])
                                    dst = EB[r0:r0 + 64, jp, d_, :].rearrange("p (k h c) -> p k h c", k=2, h=2)[:, :, h_, :]
                                    q_ = "sp"
                                    cx.dma(q_, dst, src, reads=["gr_d"], writes=[("EBp", jp, d_, r0, h_)])
                it = 0
                QTk = [("QT", i) for i in range(8)] + [("QTb", i) for i in range(8)]
                KTk = [("KT", i) for i in range(8)]
                VTk = [("VT", i) for i in range(8)]
                for wi, dst, key0 in ((0, QT, "QT"), (1, None, "KT"), (2, VT, "VT")):
                    for tb in range(8):
                        b = it % 2
                        it += 1
                        cx.op("pe", [(lambda e, c=c: e.matmul(pI[b][:], lhsT=wqkv[:, wi, c, :],
                                                              rhs=hT[:, c, tb * 512:(tb + 1) * 512],
                                                              start=(c == 0), stop=(c == 7))) for c in range(8)],
                              reads=[("wqkv", wi), ("hT", tb)], writes=["pS%d" % b])
                        key = (key0, tb)
                        tsl = slice(tb * 512, (tb + 1) * 512)
                        if wi == 0:
                            cx.op("act", lambda e: e.copy(out=QT[0:64, 0, tsl], in_=pI[b][0:64, :]),
                                  reads=["pS%d" % b], writes=[key])
                            cx.op("dve", lambda e: e.tensor_copy(out=QT[64:128, 1, tsl], in_=pI[b][64:128, :]),
                                  reads=["pS%d" % b], writes=[(key0 + "b", tb)])
                        elif wi == 2:
                            if tb % 2:
                                cx.op("act", lambda e: e.copy(out=VT[:, tsl], in_=pI[b][:]), reads=["pS%d" % b], writes=[key])
                            else:
                                cx.op("dve", lambda e: e.tensor_copy(out=VT[:, tsl], in_=pI[b][:]), reads=["pS%d" % b], writes=[key])
                        else:
                            src1 = pI[b][:].rearrange("p (m x) -> p m x", x=128)
                            k1 = KPb[0]
                            d0 = k1[:, tb * 512:tb * 512 + 512].rearrange("p (m x) -> p m x", x=128)
                            d1 = k1[:, tb * 512 + 128:tb * 512 + 640].rearrange("p (m x) -> p m x", x=128)
                            src4 = pI[b][:].rearrange("p (l r) -> p r l", r=4)
                            k4 = KPb[1][:].rearrange("p (r l) -> p r l", r=4)
                            src16 = pI[b][:].rearrange("p (l r) -> p r l", r=16)
                            k16 = KPb[2][:].rearrange("p (r l) -> p r l", r=16)
                            pos16 = 32 * tb + (128 if ((tb // 2) % 2) else 0)
                            for eng, pr in (("act", slice(0, 64)), ("dve", slice(64, 128))):
                                cp = (lambda e, o, i_: e.copy(out=o, in_=i_)) if eng == "act" else (lambda e, o, i_: e.tensor_copy(out=o, in_=i_))
                                cx.op(eng, [lambda e: cp(e, d0[pr, :, 0:64], src1[pr, :, 0:64]),
                                            lambda e: cp(e, d1[pr, :, 64:128], src1[pr, :, 64:128]),
                                            lambda e: cp(e, k4[pr, :, 128 * tb:128 * tb + 64], src4[pr, :, 0:64]),
                                            lambda e: cp(e, k4[pr, :, 128 * tb + 192:128 * tb + 256], src4[pr, :, 64:128]),
                                            lambda e: cp(e, k16[pr, :, pos16:pos16 + 32], src16[pr, :, :])],
                                      reads=["pS%d" % b], writes=[("KPa" if eng == "act" else "KPd", tb)])
                KPk = [("KPa", i) for i in range(8)] + [("KPd", i) for i in range(8)] + [("KP", i) for i in range(3)]
                for h in range(2):
                    pass

                oi = 0
                si = 0
                for di, d in enumerate(DILS):
                    L = S // d
                    Lb = L + 128
                    nqb = L // 128
                    nsl = nqb + 1
                    KPv = KPb[di][:].rearrange("p (r l) -> p r l", r=d)
                    VTv = VT[:].rearrange("p (l r) -> p r l", r=d)
                    Vsv = Vs[:, 0:d * nsl, :].rearrange("p (r s) c -> p r s c", r=d)
                    for r in range(d):
                        for t0 in range(0, nqb, 8):
                            nt_ = min(8, nqb - t0)
                            vb = vbi % 2
                            vbi += 1
                            pVt = pVt2[vb]
                            cx.op("pe", [(lambda e, i=i: e.transpose(out=pVt[:, i, :],
                                                                     in_=VTv[:, r, (t0 + i) * 128:(t0 + i + 1) * 128],
                                                                     identity=ident[:])) for i in range(nt_)],
                                  reads=VTk + ["ident"], writes=["pVt%d" % vb])
                            cx.op("dve", [lambda e: e.tensor_copy(out=Vsv[0:64, r, t0:t0 + nt_, 0:64], in_=pVt[0:64, 0:nt_, 0:64]),
                                          lambda e: e.tensor_copy(out=Vsv[0:64, r, t0:t0 + nt_, 128:192], in_=pVt[0:64, 0:nt_, 64:128])],
                                  reads=["pVt%d" % vb], writes=["VsL"])
                            cx.op("act", [lambda e: e.copy(out=Vsv[64:128, r, t0 + 1:t0 + 1 + nt_, 0:64], in_=pVt[64:128, 0:nt_, 0:64]),
                                          lambda e: e.copy(out=Vsv[64:128, r, t0 + 1:t0 + 1 + nt_, 128:192], in_=pVt[64:128, 0:nt_, 64:128])],
                                  reads=["pVt%d" % vb], writes=["VsU"])
                    QTv = QT[:].rearrange("p a (l r) -> p a r l", r=d)
                    accv = [acc[h][:].rearrange("p (l r) -> p r l", r=d) for h in range(2)]
                    vcols = (slice(0, 128), slice(64, 192))
                    groups = []
                    if d == 16:
                        for r0 in range(0, 16, 2):
                            groups.append([(r0, 0), (r0, 1), (r0 + 1, 0), (r0 + 1, 1)])
                    else:
                        for r in range(d):
                            for g0 in range(0, nqb, 4):
                                groups.append([(r, g0 + i) for i in range(4)])
                    pend = []

                    def emit_pv(item):
                        (r, qb, sidx, ob, slot, grp) = item
                        pt = PTb[sidx]
                        lo_ok = qb > 0
                        hi_ok = qb < nqb - 1
                        for h in range(2):
                            vcol = vcols[h]
                            pob = pO[h * 2 + ob]
                            o_ap = pob[:, slot * 128:(slot + 1) * 128]
                            cb2 = slice(h * 128, h * 128 + 128)
                            ca = slice(256 + h * 128, 256 + h * 128 + 128)
                            mm = []
                            if lo_ok:
                                mm.append(lambda e: e.matmul(o_ap, lhsT=Vsv[:, r, qb, vcol], rhs=pt[:, ca], start=(slot == 0), stop=False, skip_group_check=True))
                            else:
                                mm.append(lambda e: e.matmul(o_ap, lhsT=Vsv[0:64, r, qb, vcol], rhs=pt[0:64, ca], start=(slot == 0), stop=False, skip_group_check=True))
                            if hi_ok:
                                mm.append(lambda e: e.matmul(o_ap, lhsT=Vsv[:, r, qb + 1, vcol], rhs=pt[:, cb2], start=False, stop=True, skip_group_check=True))
                            else:
                                mm.append(lambda e: e.matmul(o_ap, lhsT=Vsv[64:128, r, qb + 1, vcol], rhs=pt[64:128, cb2], start=False, stop=True, skip_group_check=True))
                            cx.op("pe", mm, reads=["VsL", "VsU", "PTb%d" % sidx], writes=[("pO", h * 2 + ob, slot)])
                            if slot == 3:
                                if d == 16:
                                    r0 = grp[0][0]
                                    dst = accv[h][:, r0:r0 + 2, :]
                                    srcp = pob[:].rearrange("p (a l) -> p a l", a=2)
                                else:
                                    r_, q0 = grp[0]
                                    dst = accv[h][:, r_, q0 * 128:q0 * 128 + 512]
                                    srcp = pob[:]
                                if di == 0:
                                    cx.op("act", lambda e: e.copy(out=dst, in_=srcp),
                                          reads=[("pO", h * 2 + ob, s_) for s_ in range(4)], writes=["acc%d" % h])
                                else:
                                    cx.op("dve", lambda e: e.tensor_tensor(out=dst, in0=srcp, in1=dst, op=ALU.add),
                                          reads=[("pO", h * 2 + ob, s_) for s_ in range(4)] + ["acc%d" % h], writes=["acc%d" % h])

                    for grp in groups:
                        ob = oi % 2
                        oi += 1
                        for slot, (r, qb) in enumerate(grp):
                            sidx = si % 4
                            ebi = si % 2
                            psi = si % 2
                            si += 1
                            q_ap = QTv[:, :, r, qb * 128:(qb + 1) * 128]
                            kA = KPv[:, r, qb * 128:(qb + 1) * 128]
                            kB = KPv[:, r, (qb + 1) * 128:(qb + 2) * 128]
                            oB = pS[psi][:, 0:256].rearrange("p (a q) -> p a q", a=2)
                            oA = pS[psi][:, 256:512].rearrange("p (a q) -> p a q", a=2)
                            cx.op("pe", [lambda e: e.matmul(oA, lhsT=kA, rhs=q_ap, start=True, stop=True),
                                         lambda e: e.matmul(oB, lhsT=kB, rhs=q_ap, start=True, stop=True)],
                                  reads=QTk + KPk, writes=["pS%d" % psi])
                            cx.op("act", lambda e: e.activation(out=Eb[ebi][:], in_=pS[psi][:], func=AF.Exp, scale=0.125),
                                  reads=["pS%d" % psi], writes=["Eb%d" % ebi])
                            cx.op("dve", lambda e: e.tensor_tensor(out=PTb[sidx][:], in0=Eb[ebi][:], in1=EB[:, j, di, :], op=ALU.mult),
                                  reads=["Eb%d" % ebi] + [("EBp", j, di, r0_, h_) for r0_ in (0, 64) for h_ in range(2)], writes=["PTb%d" % sidx])
                            pend.append((r, qb, sidx, ob, slot, grp))
                            if len(pend) > 2:
                                emit_pv(pend.pop(0))
                    while pend:
                        emit_pv(pend.pop(0))

                for cb_ in range(0 if os.environ.get('KNOEPI') else 8):
                    cs = slice(cb_ * 512, (cb_ + 1) * 512)
                    rden = wfr[:, (cb_ % 2) * 512:(cb_ % 2) * 512 + 512]
                    rk = "wfr"
                    cx.op("dve", [lambda e: e.tensor_copy(out=rden[0:64, :], in_=acc[0][64:128, cs]),
                                  lambda e: e.tensor_copy(out=rden[64:128, :], in_=acc[1][0:64, cs])],
                          reads=["acc0", "acc1"], writes=[rk])
                    cx.op("act", lambda e: e.activation(out=rden, in_=rden, func=AF.Ln), reads=[rk], writes=[rk])
                    cx.op("act", lambda e: e.activation(out=rden, in_=rden, func=AF.Exp, scale=-1.0), reads=[rk], writes=[rk])
                    ab_ = cb_ % 2
                    cx.op("dve", [lambda e: e.tensor_tensor(out=attn[ab_][0:64, :], in0=acc[0][0:64, cs], in1=rden[0:64, :], op=ALU.mult),
                                  lambda e: e.tensor_tensor(out=attn[ab_][64:128, :], in0=acc[1][64:128, cs], in1=rden[64:128, :], op=ALU.mult)],
                          reads=["acc0", "acc1", rk], writes=["attn%d" % ab_])
                    cx.dma("pool", mix_d[j * 128:(j + 1) * 128, cs], attn[ab_][:], reads=["attn%d" % ab_], writes=[("mix_d", j, cb_)])
        cx.barrier()
        if _STOP <= 5:
            return nc
        s2.close()
        sW = es.enter_context(ExitStack())
        wgb = sb(sW, "wgb", [128, 8, DFF], BF16)
        wvb = sb(sW, "wvb", [128, 8, DFF], BF16)
        wdb = sb(sW, "wdb", [128, NF, D], BF16)
        wst = [sb(sW, "wst%d" % i, [128, D], F32) for i in range(2)]
        wjobs = []
        for (src_d, dstw, key) in ((wg_d, wgb, "wgb"), (wv_d, wvb, "wvb")):
            for c in range(8):
                for q4 in range(4):
                    cols = slice(q4 * 704, (q4 + 1) * 704)
                    wjobs.append((src_d[c * 128:(c + 1) * 128, cols], dstw[:, c, cols], 704, key))
        for f in range(NF):
            wjobs.append((wd_d[f * 128:(f + 1) * 128, :], wdb[:, f, :], D, "wdb"))
        wstate = {"i": 0}

        def emit_wjob():
            if not wjobs:
                return
            src, dst, wid, key = wjobs.pop(0)
            i = wstate["i"]
            wstate["i"] += 1
            b = i % 2
            cx.dma("sp", wst[b][:, 0:wid], src, writes=["wst%d" % b])
            eng = ("act", "pool", "act", "dve")[i % 4]
            if eng == "act":
                cx.op("act", lambda e: e.copy(out=dst, in_=wst[b][:, 0:wid]), reads=["wst%d" % b], writes=[key])
            else:
                cx.op(eng, lambda e: e.tensor_copy(out=dst, in_=wst[b][:, 0:wid]), reads=["wst%d" % b], writes=[key])

        with ExitStack() as pe2:
            wob = sb(pe2, "wob", [128, 8, D], BF16)
            gat = sb(pe2, "gat", [128, 6], F32)
            g2b = sb(pe2, "g2b", [128, D], F32)
            mixb = [sb(pe2, "mixb%d" % i, [128, 8, 512], BF16) for i in range(2)]
            sqb = [sb(pe2, "sqb%d" % i, [128, 512], BF16) for i in range(2)]
            rsa = sb(pe2, "rsa", [128, 512], F32)
            x1 = [sb(pe2, "x1_%d" % i, [128, D], F32) for i in range(3)]
            ssqE = [sb(pe2, "ssqE%d" % i, [128, 1], F32) for i in range(2)]
            rstdE = [sb(pe2, "rstdE%d" % i, [128, 1], F32) for i in range(2)]
            h2b = [sb(pe2, "h2b%d" % i, [128, D], BF16) for i in range(3)]
            h2T = [sb(pe2, "h2T%d" % i, [128, 8, 128], BF16) for i in range(2)]
            zcol = sb(pe2, "zcol", [128, 8, 1], BF16)
            pR = ps(pe2, "pR", [128, 512], F32)
            pY = [ps(pe2, "pY%d" % i, [128, D], F32) for i in range(2)]
            pT2 = [ps(pe2, "pT2%d" % i, [128, 8, 128], BF16) for i in range(2)]

            cx.dma("sp", gat[:], ga_d[:, :], writes=["gat"])
            cx.dma("sp", g2b[:], g2_d[:, :], writes=["g2b"])
            for c in range(8):
                b = c % 2
                cx.dma("sp", wst[b][:], wout_d[c * 128:(c + 1) * 128, :], writes=["wst%d" % b])
                if c % 2:
                    cx.op("act", lambda e: e.copy(out=wob[:, c, :], in_=wst[b][:]), reads=["wst%d" % b], writes=["wob"])
                else:
                    cx.op("dve", lambda e: e.tensor_copy(out=wob[:, c, :], in_=wst[b][:]), reads=["wst%d" % b], writes=["wob"])
            cx.op("pool", lambda e: e.memset(zcol[:], 0.0), writes=["zcol"])
            h2v = h2_d.rearrange("(c p) s -> p c s", p=128)
            cx.dma("pool", h2v[:, :, 0:1], zcol[:], reads=["zcol"], writes=["h2halo"], allow_slow_non_contiguous=True)
            cx.dma("pool", h2v[:, :, S + 1:S + 2], zcol[:], reads=["zcol"], writes=["h2halo"], allow_slow_non_contiguous=True)
            mixv = mix_d.rearrange("(c p) s -> p c s", p=128)
            def e_prologue(blk):
                mb = blk % 2
                bs = slice(blk * 512, (blk + 1) * 512)
                cx.dma("sp", mixb[mb][:], mixv[:, :, bs], reads=[("mix_d", i, blk) for i in range(6)] + [("mix_d", 6), ("mix_d", 7)], writes=["mixb%d" % mb])
                for jj in range(6):
                    q = jj % 2
                    cx.op("act", lambda e: e.activation(out=sqb[q][:], in_=mixb[mb][:, jj, :], func=AF.Square),
                          reads=["mixb%d" % mb], writes=["sqb%d" % q])
                    cx.op("pe", lambda e: e.matmul(pR[:], lhsT=ones_b[:], rhs=sqb[q][:], start=(jj == 0), stop=(jj == 5)),
                          reads=["ones_b", "sqb%d" % q], writes=["pR"])
                cx.op("act", lambda e: e.activation(out=rsa[:], in_=pR[:], func=AF.Sqrt, scale=1.0 / 768, bias=epsb[:, 0:1]),
                      reads=["pR", "epsb"], writes=["rsa"])
                cx.op("dve", lambda e: e.reciprocal(out=rsa[:], in_=rsa[:]), reads=["rsa"], writes=["rsa"])
                for jj in range(6):
                    cx.op("dve", lambda e: e.scalar_tensor_tensor(out=mixb[mb][:, jj, :], in0=mixb[mb][:, jj, :],
                                                                  scalar=gat[:, jj:jj + 1], in1=rsa[:],
                                                                  op0=ALU.mult, op1=ALU.mult),
                          reads=["mixb%d" % mb, "gat", "rsa"], writes=["mixb%d" % mb])

            def e_s1(t):
                blk, tt = t // 4, t % 4
                mb = blk % 2
                b = t % 2
                b3 = t % 3
                xk = "x1_%d" % b3
                mm = []
                for hf in range(2):
                    for c in range(8):
                        mm.append(lambda e, hf=hf, c=c: e.matmul(pY[b][:, hf * 512:(hf + 1) * 512],
                                                                 lhsT=mixb[mb][:, c, tt * 128:(tt + 1) * 128],
                                                                 rhs=wob[:, c, hf * 512:(hf + 1) * 512],
                                                                 start=(c == 0), stop=(c == 7)))
                cx.op("pe", mm, reads=["mixb%d" % mb, "wob"], writes=["pY%d" % b])
                cx.op("dve", lambda e: e.tensor_tensor(out=x1[b3][:], in0=pY[b][:], in1=x1[b3][:], op=ALU.add),
                      reads=["pY%d" % b, xk], writes=[xk])
                cx.dma("pool", x1_d[t * 128:(t + 1) * 128, :], x1[b3][:], reads=[xk], writes=[("x1_d", t)])
                rms_rstd((h2b[b3][:], ssqE[b][:], rstdE[b][:], "h2b%d" % b3, "ssqE%d" % b, "rstdE%d" % b),
                         x1[b3][:], xk, D, "E")
                cx.op("dve", lambda e: e.scalar_tensor_tensor(out=h2b[b3][:], in0=x1[b3][:], scalar=rstdE[b][:, 0:1],
                                                              in1=g2b[:], op0=ALU.mult, op1=ALU.mult),
                      reads=[xk, "rstdE%d" % b, "g2b"], writes=["h2b%d" % b3])
                for _ in range(3):
                    emit_wjob()

            def e_s2(t):
                b = t % 2
                b3 = t % 3
                cx.op("pe", [(lambda e, c=c: e.transpose(out=pT2[b][:, c, :], in_=h2b[b3][:, c * 128:(c + 1) * 128],
                                                         identity=ident[:])) for c in range(8)],
                      reads=["h2b%d" % b3, "ident"], writes=["pT2%d" % b])
                cx.op("act", lambda e: e.copy(out=h2T[b][:], in_=pT2[b][:]), reads=["pT2%d" % b], writes=["h2T%d" % b])
                cx.dma("pool", h2v[:, :, 1 + t * 128:1 + (t + 1) * 128], h2T[b][:], reads=["h2T%d" % b],
                       writes=[("h2_d", t)])

            def load_x(t):
                cx.dma("sp", x1[t % 3][:], x_d[t * 128:(t + 1) * 128, :], writes=["x1_%d" % (t % 3)])

            e_prologue(0)
            load_x(0)
            for t in range(NT):
                if t % 4 == 0 and t // 4 + 1 < 8:
                    e_prologue(t // 4 + 1)
                if t + 1 < NT:
                    load_x(t + 1)
                e_s1(t)
                if t >= 2:
                    e_s2(t - 2)
            e_s2(NT - 2)
            e_s2(NT - 1)
            while wjobs:
                emit_wjob()
        cx.barrier()
        if _STOP <= 6:
            return nc
        with ExitStack() as pf:
            cwt = sb(pf, "cwt", [128, NF, 3], F32)
            cbt = sb(pf, "cbt", [128, NF], F32)
            gFb = sb(pf, "gFb", [128, D], F32)
            h2s = [sb(pf, "h2s%d" % i, [128, 8, 258], BF16) for i in range(2)]
            t1 = [sb(pf, "t1_%d" % i, [128, 256], F32) for i in range(2)]
            t2 = [sb(pf, "t2_%d" % i, [128, 256], F32) for i in range(2)]
            sg = [sb(pf, "sg_%d" % i, [128, 256], F32) for i in range(2)]
            aT = [sb(pf, "aT_%d" % i, [128, 256], BF16) for i in range(4)]
            x1r = [sb(pf, "x1r%d" % i, [128, D], F32) for i in range(2)]
            of_ = [sb(pf, "of%d" % i, [128, D], F32) for i in range(2)]
            junkF = sb(pf, "junkF", [128, D], F32)
            ssqF = [sb(pf, "ssqF%d" % i, [128, 1], F32) for i in range(2)]
            rstdF = [sb(pf, "rstdF%d" % i, [128, 1], F32) for i in range(2)]
            yo = [sb(pf, "yo%d" % i, [128, D], F32) for i in range(2)]
            pGt = [ps(pf, "pGt%d" % i, [128, 512], F32) for i in range(2)]
            pVl = [ps(pf, "pVl%d" % i, [128, 512], F32) for i in range(2)]
            pD = [ps(pf, "pD%d" % i, [128, D], F32) for i in range(2)]

            cx.dma("sp", cwt[:], cw_d[:, :, :], writes=["cwt"])
            cx.dma("sp", cbt[:], cb_d[:, :], writes=["cbt"])
            cx.dma("sp", gFb[:], gF_d[:, :], writes=["gFb"])
            h2v = h2_d.rearrange("(c p) s -> p c s", p=128)
            _SUB = int(os.environ.get("KSUB", "9"))
            def load_h2s(st_):
                cx.dma("sp", h2s[st_ % 2][:], h2v[:, :, st_ * 256:st_ * 256 + 258],
                       reads=[("h2_d", t) for t in range(max(0, 2 * st_ - 1), min(NT, 2 * st_ + 3))] + ["h2halo"],
                       writes=["h2s%d" % (st_ % 2)])

            load_h2s(0)
            for st in range(16 if _SUB > 1 else 0):
                hb_ = st % 2
                for tt in range(2):
                    t = st * 2 + tt
                    cx.dma("sp", x1r[t % 2][:], x1_d[t * 128:(t + 1) * 128, :], reads=[("x1_d", t)], writes=["x1r%d" % (t % 2)])
                if st + 1 < 16:
                    load_h2s(st + 1)
                pend = []

                def emit_down(item):
                    f, ab = item
                    mm = []
                    for tt in range(2):
                        for hf in range(2):
                            mm.append(lambda e, tt=tt, hf=hf: e.matmul(pD[tt][:, hf * 512:(hf + 1) * 512],
                                                                       lhsT=aT[ab][:, tt * 128:(tt + 1) * 128],
                                                                       rhs=wdb[:, f, hf * 512:(hf + 1) * 512],
                                                                       start=(f == 0), stop=(f == NF - 1)))
                    cx.op("pe", mm, reads=["aT_%d" % ab, "wdb"], writes=(["pD0", "pD1"] if f in (0, NF - 1) else []))

                for f in range(NF):
                    b = f % 2
                    ab = f % 4
                    fs = slice(f * 128, (f + 1) * 128)
                    cx.op("pe", [(lambda e, c=c: e.matmul(pGt[b][:, 0:258], lhsT=wgb[:, c, fs], rhs=h2s[hb_][:, c, 0:258],
                                                          start=(c == 0), stop=(c == 7))) for c in range(8)],
                          reads=["wgb", "h2s%d" % hb_], writes=["pGt%d" % b])
                    cx.op("pe", [(lambda e, c=c: e.matmul(pVl[b][:, 0:256], lhsT=wvb[:, c, fs], rhs=h2s[hb_][:, c, 1:257],
                                                          start=(c == 0), stop=(c == 7))) for c in range(8)],
                          reads=["wvb", "h2s%d" % hb_], writes=["pVl%d" % b])
                    cx.op("act", lambda e: e.activation(out=t1[b][:], in_=pGt[b][:, 1:257], func=AF.Identity,
                                                        scale=cwt[:, f, 1:2], bias=cbt[:, f:f + 1]),
                          reads=["pGt%d" % b, "cwt", "cbt"], writes=["t1_%d" % b])
                    cx.op("dve", lambda e: e.scalar_tensor_tensor(out=t2[b][:], in0=pGt[b][:, 0:256], scalar=cwt[:, f, 0:1],
                                                                  in1=t1[b][:], op0=ALU.mult, op1=ALU.add),
                          reads=["pGt%d" % b, "cwt", "t1_%d" % b], writes=["t2_%d" % b])
                    cx.op("dve", lambda e: e.scalar_tensor_tensor(out=t1[b][:], in0=pGt[b][:, 2:258], scalar=cwt[:, f, 2:3],
                                                                  in1=t2[b][:], op0=ALU.mult, op1=ALU.add),
                          reads=["pGt%d" % b, "cwt", "t2_%d" % b], writes=["t1_%d" % b])
                    cx.op("act", lambda e: e.activation(out=sg[b][:], in_=t1[b][:], func=AF.Silu),
                          reads=["t1_%d" % b], writes=["sg_%d" % b])
                    cx.op("dve", lambda e: e.tensor_tensor(out=aT[ab][:], in0=pVl[b][:, 0:256], in1=sg[b][:], op=ALU.mult),
                          reads=["pVl%d" % b, "sg_%d" % b], writes=["aT_%d" % ab])
                    if _SUB > 2:
                        pend.append((f, ab))
                    if len(pend) > 2:
                        emit_down(pend.pop(0))
                while pend:
                    emit_down(pend.pop(0))
                for tt in range(2 if _SUB > 3 else 0):
                    t = st * 2 + tt
                    b = t % 2
                    cx.op("dve", lambda e: e.tensor_tensor(out=of_[b][:], in0=pD[tt][:], in1=x1r[b][:], op=ALU.add),
                          reads=["pD%d" % tt, "x1r%d" % b], writes=["of%d" % b])
                    rms_rstd((junkF[:], ssqF[b][:], rstdF[b][:], "junkF", "ssqF%d" % b, "rstdF%d" % b),
                             of_[b][:], "of%d" % b, D, "F")
                    cx.op("dve", lambda e: e.scalar_tensor_tensor(out=yo[b][:], in0=of_[b][:], scalar=rstdF[b][:, 0:1],
                                                                  in1=gFb[:], op0=ALU.mult, op1=ALU.mult),
                          reads=["of%d" % b, "rstdF%d" % b, "gFb"], writes=["yo%d" % b])
                    cx.dma(os.environ.get("KYQ", "sp"), y_d[t * 128:(t + 1) * 128, :], yo[b][:], reads=["yo%d" % b], writes=[("y", t)])
            cx.finish("pool", [("y", t) for t in range(NT)])
            cx.barrier()
    return nc


def _t5_bucket_np(rel):
    nb = 16
    max_exact = 8
    ret = np.where(rel > 0, nb, 0)
    n = np.abs(rel)
    nf = np.maximum(n, 1).astype(np.float32)
    large = max_exact + (np.log(nf / np.float32(max_exact)) / np.float32(math.log(1024 / max_exact))
                         * np.float32(nb - max_exact)).astype(np.int32)
    large = np.minimum(large, nb - 1)
    return ret + np.where(n < max_exact, n, large)


_CONST = {}


def _constants():
    if _CONST:
        return _CONST
    bf = ml_dtypes.bfloat16
    m = np.arange(P_G)
    rel = 191 - m
    oh = np.zeros((32, 3, P_G), np.float32)
    gmask = np.zeros((12, 3, P_G), np.float32)
    for di, d in enumerate(DILS):
        bk = _t5_bucket_np(rel * d)
        valid = np.abs(rel) <= 64
        oh[bk[valid], di, m[valid]] = 1.0
        gmask[:, di, valid] = 1.0
    _CONST["onehot"] = oh
    _CONST["gmask"] = gmask
    c = np.arange(64)
    ang = 2 * np.pi * np.outer(c, c) / 64
    C64 = np.cos(ang) / 512.0
    S64 = -np.sin(ang) / 512.0
    z = np.zeros((64, 64))
    _CONST["c64blk"] = np.block([[C64, z], [z, C64]]).astype(bf)
    _CONST["s64blk"] = np.block([[S64, z], [z, S64]]).astype(bf)
    _CONST["ident"] = np.eye(128, dtype=np.float32).astype(bf)
    s = np.arange(S, dtype=np.int64)
    prod = (s[:, None] * s[None, :]) % S
    angt = prod.astype(np.float64) * (2 * np.pi / S)
    for name, fn in (("dft_cos", np.cos), ("dft_sin", np.sin)):
        t = fn(angt).astype(np.float32)
        t = t.reshape(32, 128, 8, 512).transpose(2, 1, 0, 3)
        _CONST[name] = np.ascontiguousarray(t).astype(bf)
    return _CONST


_NC = {}


def kernel(x, norm_mix_gain, w_in, attn_out_gain, rel_bias_table, fourier_w, fourier_b, fourier_out_gain,
           w_out, norm_ffn_gain, w_gate, w_val, conv_w, conv_b, w_down, final_norm_gain):
    f32 = np.float32
    x = np.asarray(x, f32)
    cst = _constants()
    bc = lambda v, n: np.ascontiguousarray(np.broadcast_to(np.asarray(v, f32).reshape(1, n), (128, n)))
    fw = np.asarray(fourier_w, f32)[0]
    fw_blk = np.zeros((128, 2, 128), f32)
    for cc in range(2):
        fw_blk[0:64, cc, 0:64] = fw[2 * cc]
        fw_blk[64:128, cc, 64:128] = fw[2 * cc + 1]
    shared = {
        "g1b": bc(norm_mix_gain[0], D), "g2b": bc(norm_ffn_gain[0], D), "gFb": bc(final_norm_gain, D),
        "gfb": bc(fourier_out_gain[0], 256),
        "ga_t": np.ascontiguousarray(np.asarray(attn_out_gain, f32)[0].reshape(6, 128).T),
        "w_in": np.ascontiguousarray(np.asarray(w_in, f32)[0]),
        "w_out": np.ascontiguousarray(np.asarray(w_out, f32)[0]),
        "w_gate": np.ascontiguousarray(np.asarray(w_gate, f32)[0]),
        "w_val": np.ascontiguousarray(np.asarray(w_val, f32)[0]),
        "w_down": np.ascontiguousarray(np.asarray(w_down, f32)[0]),
        "cw_t": np.ascontiguousarray(np.asarray(conv_w, f32)[0].reshape(3, NF, 128).transpose(2, 1, 0)),
        "cb_t": np.ascontiguousarray(np.asarray(conv_b, f32)[0].reshape(NF, 128).T),
        "rel_tab": np.ascontiguousarray(np.asarray(rel_bias_table, f32)),
        "onehot": cst["onehot"], "gmask": cst["gmask"],
        "fw_blk": fw_blk,
        "fb_row": np.ascontiguousarray(np.asarray(fourier_b, f32)[0].reshape(1, 256)),
        "c64blk": cst["c64blk"], "s64blk": cst["s64blk"], "ident": cst["ident"],
        "dft_cos": cst["dft_cos"], "dft_sin": cst["dft_sin"],
    }
    n = x.shape[0]
    if "nc" not in _NC:
        _NC["nc"] = build_nc()
    nc = _NC["nc"]
    in_maps = [dict(shared, x=np.ascontiguousarray(x[b])) for b in range(n)]
    res = run_bass_kernel_spmd(nc, in_maps, core_ids=list(range(n)))
    return np.stack([np.asarray(r["y"], f32) for r in res.results], axis=0)
```

```python
import math
import os
from contextlib import ExitStack

import numpy as np
import ml_dtypes
import concourse.bass as bass
import concourse.mybir as mybir
from concourse.bass_utils import run_bass_kernel_spmd

F32 = mybir.dt.float32
BF16 = mybir.dt.bfloat16
AF = mybir.ActivationFunctionType
ALU = mybir.AluOpType

S = 4096
D = 1024
NT = 32
DFF = 2816
NF = 22
EPS = 1e-6
DILS = (1, 4, 16)
P_G = 384
REP = 64


class Ctx:
    def __init__(self, nc, es):
        self.nc = nc
        self.E = {"pe": nc.tensor, "act": nc.scalar, "dve": nc.vector, "pool": nc.gpsimd, "sp": nc.sync}
        self.sem = {k: es.enter_context(nc.semaphore("sem_" + k)) for k in self.E}
        self.cnt = {k: 0 for k in self.E}
        self.seen = {k: {} for k in self.E}
        self.lastw = {}
        self.readers = {}
        nd = 64
        self.dsem = [es.enter_context(nc.semaphore("dsem%d" % i)) for i in range(nd)]
        self.dval = [0] * nd
        self.dnext = 0

    def _wait(self, eng, tok):
        sem, key, val = tok
        if self.seen[eng].get(key, 0) >= val:
            return
        self.E[eng].wait_ge(sem, val)
        self.seen[eng][key] = val

    def _deps(self, eng, reads, writes):
        for k in list(reads) + list(writes):
            t = self.lastw.get(k)
            if t is not None:
                self._wait(eng, t)
        for k in writes:
            for t in self.readers.get(k, {}).values():
                self._wait(eng, t)

    def _commit(self, tok, reads, writes):
        for k in writes:
            self.lastw[k] = tok
            self.readers[k] = {}
        for k in reads:
            d = self.readers.setdefault(k, {})
            if tok[1] not in d or d[tok[1]][2] < tok[2]:
                d[tok[1]] = tok

    def op(self, eng, fns, reads=(), writes=()):
        self._deps(eng, reads, writes)
        if callable(fns):
            fns = [fns]
        ins = None
        for f in fns:
            ins = f(self.E[eng])
        self.cnt[eng] += 1
        ins.then_inc(self.sem[eng], 1)
        tok = (self.sem[eng], eng, self.cnt[eng])
        self._commit(tok, reads, writes)

    def dma(self, q, out, in_, reads=(), writes=(), **kw):
        self._deps(q, reads, writes)
        i = self.dnext
        self.dnext = (i + 1) % len(self.dsem)
        key = "d%d" % i
        if self.dval[i] > 0:
            self._wait(q, (self.dsem[i], key, self.dval[i]))
        self.E[q].dma_start(out=out, in_=in_, **kw).then_inc(self.dsem[i], 16)
        self.dval[i] += 16
        tok = (self.dsem[i], key, self.dval[i])
        self._commit(tok, reads, writes)

    def barrier(self):
        if os.environ.get("KDBG"):
            print("barrier counts", self.cnt, max(self.dval))
        toks = [(self.sem[k], k, self.cnt[k]) for k in self.E if self.cnt[k] > 0]
        toks += [(self.dsem[i], "d%d" % i, self.dval[i]) for i in range(len(self.dsem)) if self.dval[i] > 0]
        for eng in self.E:
            for t in toks:
                if t[1] != eng:
                    self._wait(eng, t)

    def finish(self, eng, keys):
        for k in keys:
            t = self.lastw.get(k)
            if t is not None:
                self._wait(eng, t)


_STOP = int(os.environ.get('KSTOP', '9'))


def build_nc():
    nc = bass.Bass("TRN2", target_bir_lowering=False)

    def din(name, shape, dt=F32):
        return nc.dram_tensor(name, list(shape), dt, kind="ExternalInput").ap()

    def dscr(name, shape, dt):
        return nc.dram_tensor(name, list(shape), dt, kind="Internal").ap()

    x_d = din("x", [S, D])
    g1_d = din("g1b", [128, D])
    g2_d = din("g2b", [128, D])
    gF_d = din("gFb", [128, D])
    gf_d = din("gfb", [128, 256])
    ga_d = din("ga_t", [128, 6])
    win_d = din("w_in", [D, 2560])
    wout_d = din("w_out", [D, D])
    wg_d = din("w_gate", [D, DFF])
    wv_d = din("w_val", [D, DFF])
    wd_d = din("w_down", [DFF, D])
    cw_d = din("cw_t", [128, NF, 3])
    cb_d = din("cb_t", [128, NF])
    tab_d = din("rel_tab", [32, 12])
    oh_d = din("onehot", [32, 3, P_G])
    gm_d = din("gmask", [12, 3, P_G])
    fwb_d = din("fw_blk", [128, 2, 128])
    fb_d = din("fb_row", [1, 256])
    c64_d = din("c64blk", [128, 128], BF16)
    s64_d = din("s64blk", [128, 128], BF16)
    id_d = din("ident", [128, 128], BF16)
    tabc_d = din("dft_cos", [8, 128, 32, 512], BF16)
    tabs_d = din("dft_sin", [8, 128, 32, 512], BF16)
    y_d = nc.dram_tensor("y", [S, D], F32, kind="ExternalOutput").ap()

    mix_d = dscr("mix_scr", [D, S], BF16)
    x1_d = dscr("x1_scr", [S, D], F32)
    h2_d = dscr("h2_scr", [D, S + 2], BF16)
    gr_d = dscr("gr_scr", [12, 3, REP, P_G], BF16)

    with ExitStack() as es:
        cx = Ctx(nc, es)

        def sb(stack, name, shape, dt):
            return stack.enter_context(nc.sbuf_tensor("sb_" + name, list(shape), dt))

        def ps(stack, name, shape, dt):
            return stack.enter_context(nc.psum_tensor("ps_" + name, list(shape), dt))

        ident = sb(es, "ident", [128, 128], BF16)
        ones_b = sb(es, "ones_b", [128, 128], BF16)
        epsb = sb(es, "epsb", [128, 1], F32)
        s2 = es.enter_context(ExitStack())
        hT = sb(s2, "hT", [128, 8, S], BF16)
        cx.dma("sp", ident[:], id_d[:, :], writes=["ident"])
        cx.op("pool", lambda e: e.memset(ones_b[:], 1.0), writes=["ones_b"])
        cx.op("pool", lambda e: e.memset(epsb[:], EPS), writes=["epsb"])

        def rms_rstd(stack_tiles, src_ap, src_key, n, tag):
            junk, ssq, rstd, kj, ks, kr = stack_tiles
            cx.op("act", lambda e: e.activation(out=junk, in_=src_ap, func=AF.Square, accum_out=ssq),
                  reads=[src_key], writes=[kj, ks])
            cx.op("act", lambda e: e.activation(out=rstd, in_=ssq, func=AF.Sqrt, scale=1.0 / n, bias=epsb[:, 0:1]),
                  reads=[ks, "epsb"], writes=[kr])
            cx.op("dve", lambda e: e.reciprocal(out=rstd, in_=rstd), reads=[kr], writes=[kr])

        with ExitStack() as pa:
            g1b = sb(pa, "g1b", [128, D], F32)
            tab = sb(pa, "tab", [32, 12], F32)
            oh = sb(pa, "oh", [32, 3, P_G], F32)
            gm = sb(pa, "gm", [12, 3, P_G], F32)
            gsb = sb(pa, "gsb", [12, 3, P_G], F32)
            gbf = sb(pa, "gbf", [12, 3, P_G], BF16)
            pG = ps(pa, "pG", [12, 3, 512], F32)
            cx.dma("sp", g1b[:], g1_d[:, :], writes=["g1b"])
            xt = [sb(pa, "xt%d" % i, [128, D], F32) for i in range(2)]
            junk = sb(pa, "junkA", [128, D], F32)
            hb = [sb(pa, "hb%d" % i, [128, D], BF16) for i in range(2)]
            ssq = [sb(pa, "ssqA%d" % i, [128, 1], F32) for i in range(2)]
            rstd = [sb(pa, "rstdA%d" % i, [128, 1], F32) for i in range(2)]
            pT = [ps(pa, "pTA%d" % i, [128, 8, 128], BF16) for i in range(2)]
            def a_s2(t):
                b = t % 2
                cx.op("pe", [(lambda e, c=c: e.transpose(out=pT[b][:, c, :], in_=hb[b][:, c * 128:(c + 1) * 128],
                                                         identity=ident[:])) for c in range(8)],
                      reads=["hb%d" % b, "ident"], writes=["pTA%d" % b])
                cx.op("act", lambda e: e.copy(out=hT[:, :, t * 128:(t + 1) * 128], in_=pT[b][:]),
                      reads=["pTA%d" % b], writes=[("hT", t // 4)])

            for t in range(NT):
                b = t % 2
                cx.dma("sp", xt[b][:], x_d[t * 128:(t + 1) * 128, :], writes=["xt%d" % b])
                rms_rstd((junk[:], ssq[b][:], rstd[b][:], "junkA", "ssqA%d" % b, "rstdA%d" % b),
                         xt[b][:], "xt%d" % b, D, "A")
                cx.op("dve", lambda e: e.scalar_tensor_tensor(out=hb[b][:], in0=xt[b][:], scalar=rstd[b][:, 0:1],
                                                              in1=g1b[:], op0=ALU.mult, op1=ALU.mult),
                      reads=["xt%d" % b, "rstdA%d" % b, "g1b"], writes=["hb%d" % b])
                if t >= 1:
                    a_s2(t - 1)
            a_s2(NT - 1)
            cx.dma("sp", tab[:], tab_d[:, :], writes=["tab"])
            cx.dma("sp", oh[:], oh_d[:, :, :], writes=["oh"])
            cx.dma("sp", gm[:], gm_d[:, :, :], writes=["gm"])
            cx.op("pe", [(lambda e, d=d: e.matmul(pG[:, d, 0:P_G], lhsT=tab[:], rhs=oh[:, d, :], start=True, stop=True))
                         for d in range(3)], reads=["tab", "oh"], writes=["pG"])
            cx.op("act", lambda e: e.activation(out=gsb[:], in_=pG[:, :, 0:P_G], func=AF.Exp),
                  reads=["pG"], writes=["gsb"])
            cx.op("dve", lambda e: e.tensor_tensor(out=gbf[:], in0=gsb[:], in1=gm[:], op=ALU.mult),
                  reads=["gsb", "gm"], writes=["gbf"])
            cx.dma("sp", gr_d[:, :, 0, :], gbf[:], reads=["gbf"], writes=["gr_d"])
            n = 1
            while n < REP:
                cx.dma("sp", gr_d[:, :, n:2 * n, :], gr_d[:, :, 0:n, :], reads=["gr_d"], writes=["gr_d"])
                n *= 2
        cx.barrier()
        if _STOP <= 1:
            return nc
        sD = es.enter_context(ExitStack())
        usb = sb(sD, "usb", [128, NT, 256], BF16)
        with ExitStack() as pu:
            wuf = sb(pu, "wuf", [128, 8, 256], F32)
            wub = sb(pu, "wub", [128, 8, 256], BF16)
            pU = [ps(pu, "pUu%d" % i, [128, 512], F32) for i in range(2)]
            cx.dma("sp", wuf[:], win_d.rearrange("(c p) n -> p c n", p=128)[:, :, 2304:2560], writes=["wuf"])
            cx.op("pool", lambda e: e.tensor_copy(out=wub[:], in_=wuf[:]), reads=["wuf"], writes=["wub"])
            for t in range(NT):
                b = t % 2
                cx.op("pe", [(lambda e, c=c: e.matmul(pU[b][:, 0:256], lhsT=hT[:, c, t * 128:(t + 1) * 128], rhs=wub[:, c, :],
                                                      start=(c == 0), stop=(c == 7))) for c in range(8)],
                      reads=[("hT", t // 4), "wub"], writes=["pUu%d" % b])
                cx.op("dve" if b else "act",
                      (lambda e: e.tensor_copy(out=usb[:, t, :], in_=pU[b][:, 0:256])) if b else
                      (lambda e: e.copy(out=usb[:, t, :], in_=pU[b][:, 0:256])),
                      reads=["pUu%d" % b], writes=[("usb", t)])
        cx.barrier()
        if _STOP <= 2:
            return nc
        with ExitStack() as pd:
            ABT = sb(pd, "ABT", [128, 2, 2, S], BF16)
            tabb = [sb(pd, "tabb%d" % i, [128, 32, 512], BF16) for i in range(2)]
            c64 = sb(pd, "c64", [128, 128], BF16)
            s64 = sb(pd, "s64", [128, 128], BF16)
            fwf = sb(pd, "fwf", [128, 2, 128], F32)
            fwb = sb(pd, "fwb", [128, 2, 128], BF16)
            M12 = sb(pd, "M12", [128, 2, 2, 128], BF16)
            fbf = sb(pd, "fbf", [1, 256], F32)
            fbb = sb(pd, "fbb", [1, 256], BF16)
            gfb = sb(pd, "gfb", [128, 256], F32)
            fjunk = sb(pd, "fjunk", [128, 256], F32)
            fssq = [sb(pd, "fssq%d" % i, [128, 1], F32) for i in range(2)]
            frstd = [sb(pd, "frstd%d" % i, [128, 1], F32) for i in range(2)]
            fnb = [sb(pd, "fnb%d" % i, [128, 256], BF16) for i in range(2)]
            fourT = sb(pd, "fourT", [128, 2, S], BF16)
            pU = [ps(pd, "pU%d" % i, [128, 512], F32) for i in range(2)]
            pF = [ps(pd, "pF%d" % i, [128, 512], F32) for i in range(2)]
            pM = ps(pd, "pM", [128, 512], F32)
            pFT = [ps(pd, "pFT%d" % i, [128, 2, 128], BF16) for i in range(2)]

            cx.dma("sp", c64[:], c64_d[:, :], writes=["c64"])
            cx.dma("sp", s64[:], s64_d[:, :], writes=["s64"])
            cx.dma("sp", fwf[:], fwb_d[:, :, :], writes=["fwf"])
            cx.dma("sp", fbf[:], fb_d[:, :], writes=["fbf"])
            cx.dma("sp", gfb[:], gf_d[:, :], writes=["gfb"])
            cx.op("pool", lambda e: e.tensor_copy(out=fwb[:], in_=fwf[:]), reads=["fwf"], writes=["fwb"])
            cx.op("pool", lambda e: e.tensor_copy(out=fbb[:], in_=fbf[:]), reads=["fbf"], writes=["fbb"])
            cx.op("pe", [(lambda e, cs=cs, cc=cc: e.matmul(pM[:, (cs * 2 + cc) * 128:(cs * 2 + cc + 1) * 128],
                                                           lhsT=(c64 if cs == 0 else s64)[:], rhs=fwb[:, cc, :],
                                                           start=True, stop=True)) for cs in range(2) for cc in range(2)],
                  reads=["c64", "s64", "fwb"], writes=["pM"])
            cx.op("act", lambda e: e.copy(out=M12[:].rearrange("p a b e -> p (a b e)"), in_=pM[:]),
                  reads=["pM"], writes=["M12"])
            it = 0
            for sbk in range(8):
                for cs, tsrc in ((0, tabc_d), (1, tabs_d)):
                    tb_ = it % 2
                    it += 1
                    cx.dma("sp", tabb[tb_][:], tsrc[sbk, :, :, :], writes=["tabb%d" % tb_])
                    for cc in range(2):
                        b = cc
                        cx.op("pe", [(lambda e, k=k: e.matmul(pF[b][:], lhsT=usb[:, k, cc * 128:(cc + 1) * 128],
                                                              rhs=tabb[tb_][:, k, :], start=(k == 0), stop=(k == 31)))
                                     for k in range(32)],
                              reads=[("usb", k_) for k_ in range(NT)] + ["tabb%d" % tb_], writes=["pF%d" % b])
                        if cc == 0:
                            cx.op("act", lambda e: e.copy(out=ABT[:, cs, cc, sbk * 512:(sbk + 1) * 512], in_=pF[b][:]),
                                  reads=["pF%d" % b], writes=["ABT"])
                        else:
                            cx.op("dve", lambda e: e.tensor_copy(out=ABT[:, cs, cc, sbk * 512:(sbk + 1) * 512], in_=pF[b][:]),
                                  reads=["pF%d" % b], writes=["ABT"])
            def f_s1(t):
                b = t % 2
                ts_ = slice(t * 128, (t + 1) * 128)
                mm = []
                for cc in range(2):
                    o_ap = pU[b][:, cc * 128:(cc + 1) * 128]
                    mm.append(lambda e, cc=cc, o_ap=o_ap: e.matmul(o_ap, lhsT=ABT[:, 0, cc, ts_], rhs=M12[:, 0, cc, :], start=True, stop=False))
                    mm.append(lambda e, cc=cc, o_ap=o_ap: e.matmul(o_ap, lhsT=ABT[:, 1, cc, ts_], rhs=M12[:, 1, cc, :], start=False, stop=False))
                    mm.append(lambda e, cc=cc, o_ap=o_ap: e.matmul(o_ap, lhsT=ones_b[0:1, :], rhs=fbb[0:1, cc * 128:(cc + 1) * 128], start=False, stop=True))
                cx.op("pe", mm, reads=["ABT", "M12", "ones_b", "fbb"], writes=["pU%d" % b])
                rms_rstd((fjunk[:], fssq[b][:], frstd[b][:], "fjunk", "fssq%d" % b, "frstd%d" % b),
                         pU[b][:, 0:256], "pU%d" % b, 256, "F")
                cx.op("dve", lambda e: e.scalar_tensor_tensor(out=fnb[b][:], in0=pU[b][:, 0:256], scalar=frstd[b][:, 0:1],
                                                              in1=gfb[:], op0=ALU.mult, op1=ALU.mult),
                      reads=["pU%d" % b, "frstd%d" % b, "gfb"], writes=["fnb%d" % b])

            def f_s2(t):
                b = t % 2
                ts_ = slice(t * 128, (t + 1) * 128)
                cx.op("pe", [(lambda e, cc=cc: e.transpose(out=pFT[b][:, cc, :], in_=fnb[b][:, cc * 128:(cc + 1) * 128],
                                                           identity=ident[:])) for cc in range(2)],
                      reads=["fnb%d" % b, "ident"], writes=["pFT%d" % b])
                cx.op("act", lambda e: e.copy(out=fourT[:, :, ts_], in_=pFT[b][:]),
                      reads=["pFT%d" % b], writes=["fourT"])

            for t in range(NT):
                f_s1(t)
                if t >= 1:
                    f_s2(t - 1)
            f_s2(NT - 1)
            for cc in range(2):
                cx.dma("pool", mix_d[768 + cc * 128:768 + (cc + 1) * 128, :], fourT[:, cc, :], reads=["fourT"],
                       writes=[("mix_d", 6 + cc)])

        cx.barrier()
        if _STOP <= 3:
            return nc
        sD.close()
        with ExitStack() as pb:
            EB = sb(pb, "EB", [128, 6, 3, 512], BF16)
            if _STOP <= 4:
                return nc
            wfr = sb(pb, "wfr", [128, 1024], F32)
            wf = [wfr[:].rearrange("p (c n) -> p c n", c=8)]
            wqkv = sb(pb, "wqkv", [128, 3, 8, 128], BF16)
            QT = sb(pb, "QT", [128, 2, S], BF16)
            VT = sb(pb, "VT", [128, S], BF16)
            KPb = [sb(pb, "KP%d" % d, [128, d * (S // d + 128)], BF16) for d in DILS]
            Vs = sb(pb, "Vs", [128, 48, 192], BF16)
            acc = [sb(pb, "acc%d" % i, [128, S], F32) for i in range(2)]
            attn = [sb(pb, "attn%d" % i, [128, 512], BF16) for i in range(2)]
            Eb = [sb(pb, "Eb%d" % i, [128, 512], BF16) for i in range(2)]
            PTb = [sb(pb, "PTb%d" % i, [128, 512], BF16) for i in range(4)]
            pVt2 = [ps(pb, "pVt%d" % i, [128, 8, 128], BF16) for i in range(2)]
            vbi = 0
            pS = [ps(pb, "pS%d" % i, [128, 512], F32) for i in range(2)]
            pI = pS
            pO = [ps(pb, "pO%d" % i, [128, 512], F32) for i in range(4)]

            cx.op("pool", lambda e: e.memset(Vs[:], 1.0), writes=["VsL", "VsU"])
            for i_ in range(3):
                cx.op("pool", lambda e: e.memset(KPb[i_][:], 0.0), writes=[("KP", i_)])
            cx.op("pool", lambda e: e.memset(QT[:], 0.0), writes=[("QT", i) for i in range(8)])
            win_v = win_d.rearrange("(c p) n -> p c n", p=128)
            def load_weights(j):
                for wi, col0 in enumerate((j * 128, 768 + j * 128, 1536 + j * 128)):
                    cx.dma("sp", wf[0], win_v[:, :, col0:col0 + 128], writes=["wfr"])
                    cx.op("pool", lambda e: e.tensor_copy(out=wqkv[:, wi, :, :], in_=wf[0]),
                          reads=["wfr"], writes=[("wqkv", wi)])

            def epilogue_piece(j, cb_):
                cs = slice(cb_ * 512, (cb_ + 1) * 512)
                rden = wfr[:, (cb_ % 2) * 512:(cb_ % 2) * 512 + 512]
                rk = "wfr"
                cx.op("dve", [lambda e: e.tensor_copy(out=rden[0:64, :], in_=acc[0][64:128, cs]),
                              lambda e: e.tensor_copy(out=rden[64:128, :], in_=acc[1][0:64, cs])],
                      reads=["acc0", "acc1"], writes=[rk])
                cx.op("act", lambda e: e.activation(out=rden, in_=rden, func=AF.Ln), reads=[rk], writes=[rk])
                cx.op("act", lambda e: e.activation(out=rden, in_=rden, func=AF.Exp, scale=-1.0), reads=[rk], writes=[rk])
                ab_ = cb_ % 2
                cx.op("dve", [lambda e: e.tensor_tensor(out=attn[ab_][0:64, :], in0=acc[0][0:64, cs], in1=rden[0:64, :], op=ALU.mult),
                              lambda e: e.tensor_tensor(out=attn[ab_][64:128, :], in0=acc[1][64:128, cs], in1=rden[64:128, :], op=ALU.mult)],
                      reads=["acc0", "acc1", rk], writes=["attn%d" % ab_])
                cx.dma("pool", mix_d[j * 128:(j + 1) * 128, cs], attn[ab_][:], reads=["attn%d" % ab_], writes=[("mix_d", j, cb_)])

            load_weights(0)
            for j in range(6):
                if j == 0:
                    for jp in range(6):
                        for d_ in range(3):
                            for (r0, off) in ((0, 63), (64, 127)):
                                for h_ in range(2):
                                    base = ((2 * jp + h_) * 3 + d_) * REP * P_G
                                    src = bass.AP(gr_d.tensor, base + off, [[P_G - 1, 64], [128, 2], [1, 128]])
                                    dst = EB[r0:r0 + 64, jp, d_, :].rearrange("p (k h c) -> p k h c", k=2, h=2)[:, :, h_, :]
                                    q_ = "sp"
                                    cx.dma(q_, dst, src, reads=["gr_d"], writes=[("EBp", jp, d_, r0, h_)])
                it = 0
                pend_epi = [(j - 1, c_) for c_ in range(8)] if j > 0 else []
                QTk = [("QT", i) for i in range(8)] + [("QTb", i) for i in range(8)]
                KTk = [("KT", i) for i in range(8)]
                VTk = [("VT", i) for i in range(8)]
                for wi, dst, key0 in ((0, QT, "QT"), (1, None, "KT"), (2, VT, "VT")):
                    for tb in range(8):
                        b = it % 2
                        it += 1
                        cx.op("pe", [(lambda e, c=c: e.matmul(pI[b][:], lhsT=wqkv[:, wi, c, :],
                                                              rhs=hT[:, c, tb * 512:(tb + 1) * 512],
                                                              start=(c == 0), stop=(c == 7))) for c in range(8)],
                              reads=[("wqkv", wi), ("hT", tb)], writes=["pS%d" % b])
                        key = (key0, tb)
                        tsl = slice(tb * 512, (tb + 1) * 512)
                        if wi == 0:
                            cx.op("act", lambda e: e.copy(out=QT[0:64, 0, tsl], in_=pI[b][0:64, :]),
                                  reads=["pS%d" % b], writes=[key])
                            cx.op("dve", lambda e: e.tensor_copy(out=QT[64:128, 1, tsl], in_=pI[b][64:128, :]),
                                  reads=["pS%d" % b], writes=[(key0 + "b", tb)])
                        elif wi == 2:
                            if tb % 2:
                                cx.op("act", lambda e: e.copy(out=VT[:, tsl], in_=pI[b][:]), reads=["pS%d" % b], writes=[key])
                            else:
                                cx.op("dve", lambda e: e.tensor_copy(out=VT[:, tsl], in_=pI[b][:]), reads=["pS%d" % b], writes=[key])
                        else:
                            src1 = pI[b][:].rearrange("p (m x) -> p m x", x=128)
                            k1 = KPb[0]
                            d0 = k1[:, tb * 512:tb * 512 + 512].rearrange("p (m x) -> p m x", x=128)
                            d1 = k1[:, tb * 512 + 128:tb * 512 + 640].rearrange("p (m x) -> p m x", x=128)
                            src4 = pI[b][:].rearrange("p (l r) -> p r l", r=4)
                            k4 = KPb[1][:].rearrange("p (r l) -> p r l", r=4)
                            src16 = pI[b][:].rearrange("p (l r) -> p r l", r=16)
                            k16 = KPb[2][:].rearrange("p (r l) -> p r l", r=16)
                            pos16 = 32 * tb + (128 if ((tb // 2) % 2) else 0)
                            for eng, pr in (("act", slice(0, 64)), ("dve", slice(64, 128))):
                                cp = (lambda e, o, i_: e.copy(out=o, in_=i_)) if eng == "act" else (lambda e, o, i_: e.tensor_copy(out=o, in_=i_))
                                cx.op(eng, [lambda e: cp(e, d0[pr, :, 0:64], src1[pr, :, 0:64]),
                                            lambda e: cp(e, d1[pr, :, 64:128], src1[pr, :, 64:128]),
                                            lambda e: cp(e, k4[pr, :, 128 * tb:128 * tb + 64], src4[pr, :, 0:64]),
                                            lambda e: cp(e, k4[pr, :, 128 * tb + 192:128 * tb + 256], src4[pr, :, 64:128]),
                                            lambda e: cp(e, k16[pr, :, pos16:pos16 + 32], src16[pr, :, :])],
                                      reads=["pS%d" % b], writes=[("KPa" if eng == "act" else "KPd", tb)])
                        if pend_epi and it % 3 == 0:
                            epilogue_piece(*pend_epi.pop(0))
                while pend_epi:
                    epilogue_piece(*pend_epi.pop(0))
                if j + 1 < 6:
                    load_weights(j + 1)
                KPk = [("KPa", i) for i in range(8)] + [("KPd", i) for i in range(8)] + [("KP", i) for i in range(3)]
                for h in range(2):
                    pass

                oi = 0
                si = 0
                for di, d in enumerate(DILS):
                    L = S // d
                    Lb = L + 128
                    nqb = L // 128
                    nsl = nqb + 1
                    KPv = KPb[di][:].rearrange("p (r l) -> p r l", r=d)
                    VTv = VT[:].rearrange("p (l r) -> p r l", r=d)
                    Vsv = Vs[:, 0:d * nsl, :].rearrange("p (r s) c -> p r s c", r=d)
                    for r in range(d):
                        for t0 in range(0, nqb, 8):
                            nt_ = min(8, nqb - t0)
                            vb = vbi % 2
                            vbi += 1
                            pVt = pVt2[vb]
                            cx.op("pe", [(lambda e, i=i: e.transpose(out=pVt[:, i, :],
                                                                     in_=VTv[:, r, (t0 + i) * 128:(t0 + i + 1) * 128],
                                                                     identity=ident[:])) for i in range(nt_)],
                                  reads=VTk + ["ident"], writes=["pVt%d" % vb])
                            cx.op("dve", [lambda e: e.tensor_copy(out=Vsv[0:64, r, t0:t0 + nt_, 0:64], in_=pVt[0:64, 0:nt_, 0:64]),
                                          lambda e: e.tensor_copy(out=Vsv[0:64, r, t0:t0 + nt_, 128:192], in_=pVt[0:64, 0:nt_, 64:128])],
                                  reads=["pVt%d" % vb], writes=["VsL"])
                            cx.op("act", [lambda e: e.copy(out=Vsv[64:128, r, t0 + 1:t0 + 1 + nt_, 0:64], in_=pVt[64:128, 0:nt_, 0:64]),
                                          lambda e: e.copy(out=Vsv[64:128, r, t0 + 1:t0 + 1 + nt_, 128:192], in_=pVt[64:128, 0:nt_, 64:128])],
                                  reads=["pVt%d" % vb], writes=["VsU"])
                    QTv = QT[:].rearrange("p a (l r) -> p a r l", r=d)
                    accv = [acc[h][:].rearrange("p (l r) -> p r l", r=d) for h in range(2)]
                    vcols = (slice(0, 128), slice(64, 192))
                    groups = []
                    if d == 16:
                        for r0 in range(0, 16, 2):
                            groups.append([(r0, 0), (r0, 1), (r0 + 1, 0), (r0 + 1, 1)])
                    else:
                        for r in range(d):
                            for g0 in range(0, nqb, 4):
                                groups.append([(r, g0 + i) for i in range(4)])
                    pend = []

                    def emit_pv(item):
                        (r, qb, sidx, ob, slot, grp) = item
                        pt = PTb[sidx]
                        lo_ok = qb > 0
                        hi_ok = qb < nqb - 1
                        for h in range(2):
                            vcol = vcols[h]
                            pob = pO[h * 2 + ob]
                            o_ap = pob[:, slot * 128:(slot + 1) * 128]
                            cb2 = slice(h * 128, h * 128 + 128)
                            ca = slice(256 + h * 128, 256 + h * 128 + 128)
                            mm = []
                            if lo_ok:
                                mm.append(lambda e: e.matmul(o_ap, lhsT=Vsv[:, r, qb, vcol], rhs=pt[:, ca], start=(slot == 0), stop=False, skip_group_check=True))
                            else:
                                mm.append(lambda e: e.matmul(o_ap, lhsT=Vsv[0:64, r, qb, vcol], rhs=pt[0:64, ca], start=(slot == 0), stop=False, skip_group_check=True))
                            if hi_ok:
                                mm.append(lambda e: e.matmul(o_ap, lhsT=Vsv[:, r, qb + 1, vcol], rhs=pt[:, cb2], start=False, stop=True, skip_group_check=True))
                            else:
                                mm.append(lambda e: e.matmul(o_ap, lhsT=Vsv[64:128, r, qb + 1, vcol], rhs=pt[64:128, cb2], start=False, stop=True, skip_group_check=True))
                            cx.op("pe", mm, reads=["VsL", "VsU", "PTb%d" % sidx], writes=[("pO", h * 2 + ob, slot)])
                            if slot == 3:
                                if d == 16:
                                    r0 = grp[0][0]
                                    dst = accv[h][:, r0:r0 + 2, :]
                                    srcp = pob[:].rearrange("p (a l) -> p a l", a=2)
                                else:
                                    r_, q0 = grp[0]
                                    dst = accv[h][:, r_, q0 * 128:q0 * 128 + 512]
                                    srcp = pob[:]
                                if di == 0:
                                    cx.op("act", lambda e: e.copy(out=dst, in_=srcp),
                                          reads=[("pO", h * 2 + ob, s_) for s_ in range(4)], writes=["acc%d" % h])
                                else:
                                    cx.op("dve", lambda e: e.tensor_tensor(out=dst, in0=srcp, in1=dst, op=ALU.add),
                                          reads=[("pO", h * 2 + ob, s_) for s_ in range(4)] + ["acc%d" % h], writes=["acc%d" % h])

                    for grp in groups:
                        ob = oi % 2
                        oi += 1
                        for slot, (r, qb) in enumerate(grp):
                            sidx = si % 4
                            ebi = si % 2
                            psi = si % 2
                            si += 1
                            q_ap = QTv[:, :, r, qb * 128:(qb + 1) * 128]
                            kA = KPv[:, r, qb * 128:(qb + 1) * 128]
                            kB = KPv[:, r, (qb + 1) * 128:(qb + 2) * 128]
                            oB = pS[psi][:, 0:256].rearrange("p (a q) -> p a q", a=2)
                            oA = pS[psi][:, 256:512].rearrange("p (a q) -> p a q", a=2)
                            cx.op("pe", [lambda e: e.matmul(oA, lhsT=kA, rhs=q_ap, start=True, stop=True),
                                         lambda e: e.matmul(oB, lhsT=kB, rhs=q_ap, start=True, stop=True)],
                                  reads=QTk + KPk, writes=["pS%d" % psi])
                            cx.op("act", lambda e: e.activation(out=Eb[ebi][:], in_=pS[psi][:], func=AF.Exp, scale=0.125),
                                  reads=["pS%d" % psi], writes=["Eb%d" % ebi])
                            cx.op("dve", lambda e: e.tensor_tensor(out=PTb[sidx][:], in0=Eb[ebi][:], in1=EB[:, j, di, :], op=ALU.mult),
                                  reads=["Eb%d" % ebi] + [("EBp", j, di, r0_, h_) for r0_ in (0, 64) for h_ in range(2)], writes=["PTb%d" % sidx])
                            pend.append((r, qb, sidx, ob, slot, grp))
                            if len(pend) > 2:
                                emit_pv(pend.pop(0))
                    while pend:
                        emit_pv(pend.pop(0))

            for cb_ in range(8):
                epilogue_piece(5, cb_)
        cx.barrier()
        if _STOP <= 5:
            return nc
        s2.close()
        sW = es.enter_context(ExitStack())
        wgb = sb(sW, "wgb", [128, 8, DFF], BF16)
        wvb = sb(sW, "wvb", [128, 8, DFF], BF16)
        wdb = sb(sW, "wdb", [128, NF, D], BF16)
        wst = [sb(sW, "wst%d" % i, [128, 704], F32) for i in range(3)]
        wjobs = []
        for (src_d, dstw, key) in ((wg_d, wgb, "wgb"), (wv_d, wvb, "wvb")):
            for c in range(8):
                for q4 in range(4):
                    cols = slice(q4 * 704, (q4 + 1) * 704)
                    wjobs.append((src_d[c * 128:(c + 1) * 128, cols], dstw[:, c, cols], 704, key))
        for f in range(NF):
            wjobs.append((wd_d[f * 128:(f + 1) * 128, 0:704], wdb[:, f, 0:704], 704, "wdb"))
            wjobs.append((wd_d[f * 128:(f + 1) * 128, 704:D], wdb[:, f, 704:D], D - 704, "wdb"))
        winfl = [None, None, None]
        wstate = {"i": 0}

        def emit_wslot(b):
            if winfl[b] is not None:
                dst, wid, key = winfl[b]
                i = wstate["i"]
                wstate["i"] += 1
                if i % 2:
                    cx.op("act", lambda e: e.copy(out=dst, in_=wst[b][:, 0:wid]), reads=["wst%d" % b], writes=[key])
                else:
                    cx.op("dve", lambda e: e.tensor_copy(out=dst, in_=wst[b][:, 0:wid]), reads=["wst%d" % b], writes=[key])
                winfl[b] = None
            if wjobs:
                src, dst, wid, key = wjobs.pop(0)
                cx.dma("sp", wst[b][:, 0:wid], src, writes=["wst%d" % b])
                winfl[b] = (dst, wid, key)

        with ExitStack() as pe2:
            wob = sb(pe2, "wob", [128, 8, D], BF16)
            gat = sb(pe2, "gat", [128, 6], F32)
            g2b = sb(pe2, "g2b", [128, D], F32)
            mixb = [sb(pe2, "mixb%d" % i, [128, 8, 512], BF16) for i in range(2)]
            sqb = [sb(pe2, "sqb%d" % i, [128, 512], BF16) for i in range(2)]
            rsa = sb(pe2, "rsa", [128, 512], F32)
            x1 = [sb(pe2, "x1_%d" % i, [128, D], F32) for i in range(3)]
            ssqE = [sb(pe2, "ssqE%d" % i, [128, 1], F32) for i in range(2)]
            rstdE = [sb(pe2, "rstdE%d" % i, [128, 1], F32) for i in range(2)]
            h2b = [sb(pe2, "h2b%d" % i, [128, D], BF16) for i in range(3)]
            h2T = [sb(pe2, "h2T%d" % i, [128, 8, 128], BF16) for i in range(2)]
            zcol = sb(pe2, "zcol", [128, 8, 1], BF16)
            pR = ps(pe2, "pR", [128, 512], F32)
            pY = [ps(pe2, "pY%d" % i, [128, D], F32) for i in range(2)]
            pT2 = [ps(pe2, "pT2%d" % i, [128, 8, 128], BF16) for i in range(2)]

            cx.dma("sp", gat[:], ga_d[:, :], writes=["gat"])
            cx.dma("sp", g2b[:], g2_d[:, :], writes=["g2b"])
            for c in range(8):
                b = c % 2
                for (c0_, c1_) in ((0, 704), (704, D)):
                    bb = (2 * c + (c0_ > 0)) % 3
                    cx.dma("sp", wst[bb][:, 0:c1_ - c0_], wout_d[c * 128:(c + 1) * 128, c0_:c1_], writes=["wst%d" % bb])
                    if c % 2:
                        cx.op("act", lambda e: e.copy(out=wob[:, c, c0_:c1_], in_=wst[bb][:, 0:c1_ - c0_]), reads=["wst%d" % bb], writes=["wob"])
                    else:
                        cx.op("dve", lambda e: e.tensor_copy(out=wob[:, c, c0_:c1_], in_=wst[bb][:, 0:c1_ - c0_]), reads=["wst%d" % bb], writes=["wob"])
            cx.op("pool", lambda e: e.memset(zcol[:], 0.0), writes=["zcol"])
            h2v = h2_d.rearrange("(c p) s -> p c s", p=128)
            cx.dma("pool", h2v[:, :, 0:1], zcol[:], reads=["zcol"], writes=["h2halo"], allow_slow_non_contiguous=True)
            cx.dma("pool", h2v[:, :, S + 1:S + 2], zcol[:], reads=["zcol"], writes=["h2halo"], allow_slow_non_contiguous=True)
            mixv = mix_d.rearrange("(c p) s -> p c s", p=128)
            def e_load(blk):
                mb = blk % 2
                bs = slice(blk * 512, (blk + 1) * 512)
                cx.dma("sp", mixb[mb][:], mixv[:, :, bs], reads=[("mix_d", i, blk) for i in range(6)] + [("mix_d", 6), ("mix_d", 7)], writes=["mixb%d" % mb])

            def e_prologue(blk):
                mb = blk % 2
                for jj in range(6):
                    q = jj % 2
                    cx.op("act", lambda e: e.activation(out=sqb[q][:], in_=mixb[mb][:, jj, :], func=AF.Square),
                          reads=["mixb%d" % mb], writes=["sqb%d" % q])
                    cx.op("pe", lambda e: e.matmul(pR[:], lhsT=ones_b[:], rhs=sqb[q][:], start=(jj == 0), stop=(jj == 5)),
                          reads=["ones_b", "sqb%d" % q], writes=["pR"])
                cx.op("act", lambda e: e.activation(out=rsa[:], in_=pR[:], func=AF.Sqrt, scale=1.0 / 768, bias=epsb[:, 0:1]),
                      reads=["pR", "epsb"], writes=["rsa"])
                cx.op("dve", lambda e: e.reciprocal(out=rsa[:], in_=rsa[:]), reads=["rsa"], writes=["rsa"])
                for jj in range(6):
                    cx.op("dve", lambda e: e.scalar_tensor_tensor(out=mixb[mb][:, jj, :], in0=mixb[mb][:, jj, :],
                                                                  scalar=gat[:, jj:jj + 1], in1=rsa[:],
                                                                  op0=ALU.mult, op1=ALU.mult),
                          reads=["mixb%d" % mb, "gat", "rsa"], writes=["mixb%d" % mb])

            def e_s1(t):
                blk, tt = t // 4, t % 4
                mb = blk % 2
                b = t % 2
                b3 = t % 3
                xk = "x1_%d" % b3
                mm = []
                for hf in range(2):
                    for c in range(8):
                        mm.append(lambda e, hf=hf, c=c: e.matmul(pY[b][:, hf * 512:(hf + 1) * 512],
                                                                 lhsT=mixb[mb][:, c, tt * 128:(tt + 1) * 128],
                                                                 rhs=wob[:, c, hf * 512:(hf + 1) * 512],
                                                                 start=(c == 0), stop=(c == 7)))
                cx.op("pe", mm, reads=["mixb%d" % mb, "wob"], writes=["pY%d" % b])
                cx.op("dve", lambda e: e.tensor_tensor(out=x1[b3][:], in0=pY[b][:], in1=x1[b3][:], op=ALU.add),
                      reads=["pY%d" % b, xk], writes=[xk])
                cx.dma("pool", x1_d[t * 128:(t + 1) * 128, :], x1[b3][:], reads=[xk], writes=[("x1_d", t)])
                rms_rstd((h2b[b3][:], ssqE[b][:], rstdE[b][:], "h2b%d" % b3, "ssqE%d" % b, "rstdE%d" % b),
                         x1[b3][:], xk, D, "E")
                cx.op("dve", lambda e: e.scalar_tensor_tensor(out=h2b[b3][:], in0=x1[b3][:], scalar=rstdE[b][:, 0:1],
                                                              in1=g2b[:], op0=ALU.mult, op1=ALU.mult),
                      reads=[xk, "rstdE%d" % b, "g2b"], writes=["h2b%d" % b3])
                for b_w in range(3):
                    emit_wslot(b_w)

            def e_s2(t):
                b = t % 2
                b3 = t % 3
                cx.op("pe", [(lambda e, c=c: e.transpose(out=pT2[b][:, c, :], in_=h2b[b3][:, c * 128:(c + 1) * 128],
                                                         identity=ident[:])) for c in range(8)],
                      reads=["h2b%d" % b3, "ident"], writes=["pT2%d" % b])
                cx.op("act", lambda e: e.copy(out=h2T[b][:], in_=pT2[b][:]), reads=["pT2%d" % b], writes=["h2T%d" % b])
                cx.dma("pool", h2v[:, :, 1 + t * 128:1 + (t + 1) * 128], h2T[b][:], reads=["h2T%d" % b],
                       writes=[("h2_d", t)])

            def load_x(t):
                cx.dma("sp", x1[t % 3][:], x_d[t * 128:(t + 1) * 128, :], writes=["x1_%d" % (t % 3)])

            e_load(0)
            e_prologue(0)
            load_x(0)
            for t in range(NT):
                if t % 4 == 0 and t // 4 + 1 < 8:
                    e_load(t // 4 + 1)
                if t + 1 < NT:
                    load_x(t + 1)
                e_s1(t)
                if t % 4 == 3 and t // 4 + 1 < 8:
                    e_prologue(t // 4 + 1)
                if t >= 2:
                    e_s2(t - 2)
            e_s2(NT - 2)
            e_s2(NT - 1)
            while wjobs or any(w is not None for w in winfl):
                for b_w in range(3):
                    emit_wslot(b_w)
        cx.barrier()
        if _STOP <= 6:
            return nc
        with ExitStack() as pf:
            cwt = sb(pf, "cwt", [128, NF, 3], F32)
            cbt = sb(pf, "cbt", [128, NF], F32)
            gFb = sb(pf, "gFb", [128, D], F32)
            h2s = [sb(pf, "h2s%d" % i, [128, 8, 258], BF16) for i in range(2)]
            t1 = [sb(pf, "t1_%d" % i, [128, 256], F32) for i in range(2)]
            t2 = [sb(pf, "t2_%d" % i, [128, 256], F32) for i in range(2)]
            sg = [sb(pf, "sg_%d" % i, [128, 256], F32) for i in range(2)]
            aT = [sb(pf, "aT_%d" % i, [128, 256], BF16) for i in range(4)]
            x1r = [sb(pf, "x1r%d" % i, [128, D], F32) for i in range(2)]
            of_ = [sb(pf, "of%d" % i, [128, D], F32) for i in range(2)]
            junkF = sb(pf, "junkF", [128, D], F32)
            ssqF = [sb(pf, "ssqF%d" % i, [128, 1], F32) for i in range(2)]
            rstdF = [sb(pf, "rstdF%d" % i, [128, 1], F32) for i in range(2)]
            yo = [sb(pf, "yo%d" % i, [128, D], F32) for i in range(2)]
            pGt = [ps(pf, "pGt%d" % i, [128, 512], F32) for i in range(2)]
            pVl = [ps(pf, "pVl%d" % i, [128, 512], F32) for i in range(2)]
            pD = [ps(pf, "pD%d" % i, [128, D], F32) for i in range(2)]

            cx.dma("sp", cwt[:], cw_d[:, :, :], writes=["cwt"])
            cx.dma("sp", cbt[:], cb_d[:, :], writes=["cbt"])
            cx.dma("sp", gFb[:], gF_d[:, :], writes=["gFb"])
            h2v = h2_d.rearrange("(c p) s -> p c s", p=128)
            _SUB = int(os.environ.get("KSUB", "9"))
            def load_h2s(st_):
                cx.dma("sp", h2s[st_ % 2][:], h2v[:, :, st_ * 256:st_ * 256 + 258],
                       reads=[("h2_d", t) for t in range(max(0, 2 * st_ - 1), min(NT, 2 * st_ + 3))] + ["h2halo"],
                       writes=["h2s%d" % (st_ % 2)])

            load_h2s(0)
            for st in range(16 if _SUB > 1 else 0):
                hb_ = st % 2
                for tt in range(2):
                    t = st * 2 + tt
                    cx.dma("sp", x1r[t % 2][:], x1_d[t * 128:(t + 1) * 128, :], reads=[("x1_d", t)], writes=["x1r%d" % (t % 2)])
                if st + 1 < 16:
                    load_h2s(st + 1)
                pend = []

                def emit_down(item):
                    f, ab = item
                    mm = []
                    for tt in range(2):
                        for hf in range(2):
                            mm.append(lambda e, tt=tt, hf=hf: e.matmul(pD[tt][:, hf * 512:(hf + 1) * 512],
                                                                       lhsT=aT[ab][:, tt * 128:(tt + 1) * 128],
                                                                       rhs=wdb[:, f, hf * 512:(hf + 1) * 512],
                                                                       start=(f == 0), stop=(f == NF - 1)))
                    cx.op("pe", mm, reads=["aT_%d" % ab, "wdb"], writes=(["pD0", "pD1"] if f in (0, NF - 1) else []))

                for f in range(NF):
                    b = f % 2
                    ab = f % 4
                    fs = slice(f * 128, (f + 1) * 128)
                    cx.op("pe", [(lambda e, c=c: e.matmul(pGt[b][:, 0:258], lhsT=wgb[:, c, fs], rhs=h2s[hb_][:, c, 0:258],
                                                          start=(c == 0), stop=(c == 7))) for c in range(8)],
                          reads=["wgb", "h2s%d" % hb_], writes=["pGt%d" % b])
                    cx.op("pe", [(lambda e, c=c: e.matmul(pVl[b][:, 0:256], lhsT=wvb[:, c, fs], rhs=h2s[hb_][:, c, 1:257],
                                                          start=(c == 0), stop=(c == 7))) for c in range(8)],
                          reads=["wvb", "h2s%d" % hb_], writes=["pVl%d" % b])
                    cx.op("act", lambda e: e.activation(out=t1[b][:], in_=pGt[b][:, 1:257], func=AF.Identity,
                                                        scale=cwt[:, f, 1:2], bias=cbt[:, f:f + 1]),
                          reads=["pGt%d" % b, "cwt", "cbt"], writes=["t1_%d" % b])
                    cx.op("dve", lambda e: e.scalar_tensor_tensor(out=t2[b][:], in0=pGt[b][:, 0:256], scalar=cwt[:, f, 0:1],
                                                                  in1=t1[b][:], op0=ALU.mult, op1=ALU.add),
                          reads=["pGt%d" % b, "cwt", "t1_%d" % b], writes=["t2_%d" % b])
                    cx.op("dve", lambda e: e.scalar_tensor_tensor(out=t1[b][:], in0=pGt[b][:, 2:258], scalar=cwt[:, f, 2:3],
                                                                  in1=t2[b][:], op0=ALU.mult, op1=ALU.add),
                          reads=["pGt%d" % b, "cwt", "t2_%d" % b], writes=["t1_%d" % b])
                    cx.op("act", lambda e: e.activation(out=sg[b][:], in_=t1[b][:], func=AF.Silu),
                          reads=["t1_%d" % b], writes=["sg_%d" % b])
                    cx.op("dve", lambda e: e.tensor_tensor(out=aT[ab][:], in0=pVl[b][:, 0:256], in1=sg[b][:], op=ALU.mult),
                          reads=["pVl%d" % b, "sg_%d" % b], writes=["aT_%d" % ab])
                    if _SUB > 2:
                        pend.append((f, ab))
                    if len(pend) > 2:
                        emit_down(pend.pop(0))
                while pend:
                    emit_down(pend.pop(0))
                for tt in range(2 if _SUB > 3 else 0):
                    t = st * 2 + tt
                    b = t % 2
                    cx.op("act", lambda e: e.copy(out=of_[b][:], in_=pD[tt][:]), reads=["pD%d" % tt], writes=["of%d" % b])
                    cx.op("dve", lambda e: e.tensor_tensor(out=of_[b][:], in0=of_[b][:], in1=x1r[b][:], op=ALU.add),
                          reads=["of%d" % b, "x1r%d" % b], writes=["of%d" % b])
                    rms_rstd((junkF[:], ssqF[b][:], rstdF[b][:], "junkF", "ssqF%d" % b, "rstdF%d" % b),
                             of_[b][:], "of%d" % b, D, "F")
                    cx.op("dve", lambda e: e.scalar_tensor_tensor(out=yo[b][:], in0=of_[b][:], scalar=rstdF[b][:, 0:1],
                                                                  in1=gFb[:], op0=ALU.mult, op1=ALU.mult),
                          reads=["of%d" % b, "rstdF%d" % b, "gFb"], writes=["yo%d" % b])
                    cx.dma(os.environ.get("KYQ", "sp"), y_d[t * 128:(t + 1) * 128, :], yo[b][:], reads=["yo%d" % b], writes=[("y", t)])
            cx.finish("pool", [("y", t) for t in range(NT)])
            cx.barrier()
    return nc


def _t5_bucket_np(rel):
    nb = 16
    max_exact = 8
    ret = np.where(rel > 0, nb, 0)
    n = np.abs(rel)
    nf = np.maximum(n, 1).astype(np.float32)
    large = max_exact + (np.log(nf / np.float32(max_exact)) / np.float32(math.log(1024 / max_exact))
                         * np.float32(nb - max_exact)).astype(np.int32)
    large = np.minimum(large, nb - 1)
    return ret + np.where(n < max_exact, n, large)


_CONST = {}


def _constants():
    if _CONST:
        return _CONST
    bf = ml_dtypes.bfloat16
    m = np.arange(P_G)
    rel = 191 - m
    oh = np.zeros((32, 3, P_G), np.float32)
    gmask = np.zeros((12, 3, P_G), np.float32)
    for di, d in enumerate(DILS):
        bk = _t5_bucket_np(rel * d)
        valid = np.abs(rel) <= 64
        oh[bk[valid], di, m[valid]] = 1.0
        gmask[:, di, valid] = 1.0
    _CONST["onehot"] = oh
    _CONST["gmask"] = gmask
    c = np.arange(64)
    ang = 2 * np.pi * np.outer(c, c) / 64
    C64 = np.cos(ang) / 512.0
    S64 = -np.sin(ang) / 512.0
    z = np.zeros((64, 64))
    _CONST["c64blk"] = np.block([[C64, z], [z, C64]]).astype(bf)
    _CONST["s64blk"] = np.block([[S64, z], [z, S64]]).astype(bf)
    _CONST["ident"] = np.eye(128, dtype=np.float32).astype(bf)
    s = np.arange(S, dtype=np.int64)
    prod = (s[:, None] * s[None, :]) % S
    angt = prod.astype(np.float64) * (2 * np.pi / S)
    for name, fn in (("dft_cos", np.cos), ("dft_sin", np.sin)):
        t = fn(angt).astype(np.float32)
        t = t.reshape(32, 128, 8, 512).transpose(2, 1, 0, 3)
        _CONST[name] = np.ascontiguousarray(t).astype(bf)
    return _CONST


_NC = {}


def kernel(x, norm_mix_gain, w_in, attn_out_gain, rel_bias_table, fourier_w, fourier_b, fourier_out_gain,
           w_out, norm_ffn_gain, w_gate, w_val, conv_w, conv_b, w_down, final_norm_gain):
    f32 = np.float32
    x = np.asarray(x, f32)
    cst = _constants()
    bc = lambda v, n: np.ascontiguousarray(np.broadcast_to(np.asarray(v, f32).reshape(1, n), (128, n)))
    fw = np.asarray(fourier_w, f32)[0]
    fw_blk = np.zeros((128, 2, 128), f32)
    for cc in range(2):
        fw_blk[0:64, cc, 0:64] = fw[2 * cc]
        fw_blk[64:128, cc, 64:128] = fw[2 * cc + 1]
    shared = {
        "g1b": bc(norm_mix_gain[0], D), "g2b": bc(norm_ffn_gain[0], D), "gFb": bc(final_norm_gain, D),
        "gfb": bc(fourier_out_gain[0], 256),
        "ga_t": np.ascontiguousarray(np.asarray(attn_out_gain, f32)[0].reshape(6, 128).T),
        "w_in": np.ascontiguousarray(np.asarray(w_in, f32)[0]),
        "w_out": np.ascontiguousarray(np.asarray(w_out, f32)[0]),
        "w_gate": np.ascontiguousarray(np.asarray(w_gate, f32)[0]),
        "w_val": np.ascontiguousarray(np.asarray(w_val, f32)[0]),
        "w_down": np.ascontiguousarray(np.asarray(w_down, f32)[0]),
        "cw_t": np.ascontiguousarray(np.asarray(conv_w, f32)[0].reshape(3, NF, 128).transpose(2, 1, 0)),
        "cb_t": np.ascontiguousarray(np.asarray(conv_b, f32)[0].reshape(NF, 128).T),
        "rel_tab": np.ascontiguousarray(np.asarray(rel_bias_table, f32)),
        "onehot": cst["onehot"], "gmask": cst["gmask"],
        "fw_blk": fw_blk,
        "fb_row": np.ascontiguousarray(np.asarray(fourier_b, f32)[0].reshape(1, 256)),
        "c64blk": cst["c64blk"], "s64blk": cst["s64blk"], "ident": cst["ident"],
        "dft_cos": cst["dft_cos"], "dft_sin": cst["dft_sin"],
    }
    n = x.shape[0]
    if "nc" not in _NC:
        _NC["nc"] = build_nc()
    nc = _NC["nc"]
    in_maps = [dict(shared, x=np.ascontiguousarray(x[b])) for b in range(n)]
    res = run_bass_kernel_spmd(nc, in_maps, core_ids=list(range(n)))
    return np.stack([np.asarray(r["y"], f32) for r in res.results], axis=0)
```

```python
import math
import os
from contextlib import ExitStack

import numpy as np
import ml_dtypes
import concourse.bass as bass
import concourse.mybir as mybir
from concourse.bass_utils import run_bass_kernel_spmd

F32 = mybir.dt.float32
BF16 = mybir.dt.bfloat16
AF = mybir.ActivationFunctionType
ALU = mybir.AluOpType

S = 4096
D = 1024
NT = 32
DFF = 2816
NF = 22
EPS = 1e-6
DILS = (1, 4, 16)
P_G = 384
REP = 64


class Ctx:
    def __init__(self, nc, es):
        self.nc = nc
        self.E = {"pe": nc.tensor, "act": nc.scalar, "dve": nc.vector, "pool": nc.gpsimd, "sp": nc.sync}
        self.sem = {k: es.enter_context(nc.semaphore("sem_" + k)) for k in self.E}
        self.cnt = {k: 0 for k in self.E}
        self.seen = {k: {} for k in self.E}
        self.lastw = {}
        self.readers = {}
        nd = 64
        self.dsem = [es.enter_context(nc.semaphore("dsem%d" % i)) for i in range(nd)]
        self.dval = [0] * nd
        self.dnext = 0

    def _wait(self, eng, tok):
        sem, key, val = tok
        if self.seen[eng].get(key, 0) >= val:
            return
        self.E[eng].wait_ge(sem, val)
        self.seen[eng][key] = val

    def _deps(self, eng, reads, writes):
        for k in list(reads) + list(writes):
            t = self.lastw.get(k)
            if t is not None:
                self._wait(eng, t)
        for k in writes:
            for t in self.readers.get(k, {}).values():
                self._wait(eng, t)

    def _commit(self, tok, reads, writes):
        for k in writes:
            self.lastw[k] = tok
            self.readers[k] = {}
        for k in reads:
            d = self.readers.setdefault(k, {})
            if tok[1] not in d or d[tok[1]][2] < tok[2]:
                d[tok[1]] = tok

    def op(self, eng, fns, reads=(), writes=()):
        self._deps(eng, reads, writes)
        if callable(fns):
            fns = [fns]
        ins = None
        for f in fns:
            ins = f(self.E[eng])
        self.cnt[eng] += 1
        ins.then_inc(self.sem[eng], 1)
        tok = (self.sem[eng], eng, self.cnt[eng])
        self._commit(tok, reads, writes)

    def dma(self, q, out, in_, reads=(), writes=(), **kw):
        self._deps(q, reads, writes)
        i = self.dnext
        self.dnext = (i + 1) % len(self.dsem)
        key = "d%d" % i
        if self.dval[i] > 0:
            self._wait(q, (self.dsem[i], key, self.dval[i]))
        self.E[q].dma_start(out=out, in_=in_, **kw).then_inc(self.dsem[i], 16)
        self.dval[i] += 16
        tok = (self.dsem[i], key, self.dval[i])
        self._commit(tok, reads, writes)

    def barrier(self):
        if os.environ.get("KDBG"):
            print("barrier counts", self.cnt, max(self.dval))
        toks = [(self.sem[k], k, self.cnt[k]) for k in self.E if self.cnt[k] > 0]
        toks += [(self.dsem[i], "d%d" % i, self.dval[i]) for i in range(len(self.dsem)) if self.dval[i] > 0]
        for eng in self.E:
            for t in toks:
                if t[1] != eng:
                    self._wait(eng, t)

    def finish(self, eng, keys):
        for k in keys:
            t = self.lastw.get(k)
            if t is not None:
                self._wait(eng, t)


_STOP = int(os.environ.get('KSTOP', '9'))


def build_nc():
    nc = bass.Bass("TRN2", target_bir_lowering=False)

    def din(name, shape, dt=F32):
        return nc.dram_tensor(name, list(shape), dt, kind="ExternalInput").ap()

    def dscr(name, shape, dt):
        return nc.dram_tensor(name, list(shape), dt, kind="Internal").ap()

    x_d = din("x", [S, D])
    g1_d = din("g1b", [128, D])
    g2_d = din("g2b", [128, D])
    gF_d = din("gFb", [128, D])
    gf_d = din("gfb", [128, 256])
    ga_d = din("ga_t", [128, 6])
    win_d = din("w_in", [D, 2560])
    wout_d = din("w_out", [D, D])
    wg_d = din("w_gate", [D, DFF])
    wv_d = din("w_val", [D, DFF])
    wd_d = din("w_down", [DFF, D])
    cw_d = din("cw_t", [128, NF, 3])
    cb_d = din("cb_t", [128, NF])
    tab_d = din("rel_tab", [32, 12])
    oh_d = din("onehot", [32, 3, P_G])
    gm_d = din("gmask", [12, 3, P_G])
    fwb_d = din("fw_blk", [128, 2, 128])
    fb_d = din("fb_row", [1, 256])
    c64_d = din("c64blk", [128, 128], BF16)
    s64_d = din("s64blk", [128, 128], BF16)
    id_d = din("ident", [128, 128], BF16)
    tabc_d = din("dft_cos", [8, 128, 32, 512], BF16)
    tabs_d = din("dft_sin", [8, 128, 32, 512], BF16)
    y_d = nc.dram_tensor("y", [S, D], F32, kind="ExternalOutput").ap()

    mix_d = dscr("mix_scr", [D, S], BF16)
    x1_d = dscr("x1_scr", [S, D], F32)
    h2_d = dscr("h2_scr", [D, S + 2], BF16)
    gr_d = dscr("gr_scr", [12, 3, REP, P_G], BF16)

    with ExitStack() as es:
        cx = Ctx(nc, es)

        def sb(stack, name, shape, dt):
            return stack.enter_context(nc.sbuf_tensor("sb_" + name, list(shape), dt))

        def ps(stack, name, shape, dt):
            return stack.enter_context(nc.psum_tensor("ps_" + name, list(shape), dt))

        ident = sb(es, "ident", [128, 128], BF16)
        ones_b = sb(es, "ones_b", [128, 128], BF16)
        epsb = sb(es, "epsb", [128, 1], F32)
        s2 = es.enter_context(ExitStack())
        hT = sb(s2, "hT", [128, 8, S], BF16)
        cx.dma("sp", ident[:], id_d[:, :], writes=["ident"])
        cx.op("pool", lambda e: e.memset(ones_b[:], 1.0), writes=["ones_b"])
        cx.op("pool", lambda e: e.memset(epsb[:], EPS), writes=["epsb"])

        def rms_rstd(stack_tiles, src_ap, src_key, n, tag):
            junk, ssq, rstd, kj, ks, kr = stack_tiles
            cx.op("act", lambda e: e.activation(out=junk, in_=src_ap, func=AF.Square, accum_out=ssq),
                  reads=[src_key], writes=[kj, ks])
            cx.op("act", lambda e: e.activation(out=rstd, in_=ssq, func=AF.Sqrt, scale=1.0 / n, bias=epsb[:, 0:1]),
                  reads=[ks, "epsb"], writes=[kr])
            cx.op("dve", lambda e: e.reciprocal(out=rstd, in_=rstd), reads=[kr], writes=[kr])

        with ExitStack() as pa:
            g1b = sb(pa, "g1b", [128, D], F32)
            tab = sb(pa, "tab", [32, 12], F32)
            oh = sb(pa, "oh", [32, 3, P_G], F32)
            gm = sb(pa, "gm", [12, 3, P_G], F32)
            gsb = sb(pa, "gsb", [12, 3, P_G], F32)
            gbf = sb(pa, "gbf", [12, 3, P_G], BF16)
            pG = ps(pa, "pG", [12, 3, 512], F32)
            cx.dma("sp", g1b[:], g1_d[:, :], writes=["g1b"])
            xt = [sb(pa, "xt%d" % i, [128, D], F32) for i in range(2)]
            junk = sb(pa, "junkA", [128, D], F32)
            hb = [sb(pa, "hb%d" % i, [128, D], BF16) for i in range(2)]
            ssq = [sb(pa, "ssqA%d" % i, [128, 1], F32) for i in range(2)]
            rstd = [sb(pa, "rstdA%d" % i, [128, 1], F32) for i in range(2)]
            pT = [ps(pa, "pTA%d" % i, [128, 8, 128], BF16) for i in range(2)]
            def a_s2(t):
                b = t % 2
                cx.op("pe", [(lambda e, c=c: e.transpose(out=pT[b][:, c, :], in_=hb[b][:, c * 128:(c + 1) * 128],
                                                         identity=ident[:])) for c in range(8)],
                      reads=["hb%d" % b, "ident"], writes=["pTA%d" % b])
                cx.op("act", lambda e: e.copy(out=hT[:, :, t * 128:(t + 1) * 128], in_=pT[b][:]),
                      reads=["pTA%d" % b], writes=[("hT", t // 4)])

            for t in range(NT):
                b = t % 2
                cx.dma("sp", xt[b][:], x_d[t * 128:(t + 1) * 128, :], writes=["xt%d" % b])
                rms_rstd((junk[:], ssq[b][:], rstd[b][:], "junkA", "ssqA%d" % b, "rstdA%d" % b),
                         xt[b][:], "xt%d" % b, D, "A")
                cx.op("dve", lambda e: e.scalar_tensor_tensor(out=hb[b][:], in0=xt[b][:], scalar=rstd[b][:, 0:1],
                                                              in1=g1b[:], op0=ALU.mult, op1=ALU.mult),
                      reads=["xt%d" % b, "rstdA%d" % b, "g1b"], writes=["hb%d" % b])
                if t >= 1:
                    a_s2(t - 1)
            a_s2(NT - 1)
            cx.dma("sp", tab[:], tab_d[:, :], writes=["tab"])
            cx.dma("sp", oh[:], oh_d[:, :, :], writes=["oh"])
            cx.dma("sp", gm[:], gm_d[:, :, :], writes=["gm"])
            cx.op("pe", [(lambda e, d=d: e.matmul(pG[:, d, 0:P_G], lhsT=tab[:], rhs=oh[:, d, :], start=True, stop=True))
                         for d in range(3)], reads=["tab", "oh"], writes=["pG"])
            cx.op("act", lambda e: e.activation(out=gsb[:], in_=pG[:, :, 0:P_G], func=AF.Exp),
                  reads=["pG"], writes=["gsb"])
            cx.op("dve", lambda e: e.tensor_tensor(out=gbf[:], in0=gsb[:], in1=gm[:], op=ALU.mult),
                  reads=["gsb", "gm"], writes=["gbf"])
            cx.dma("sp", gr_d[:, :, 0, :], gbf[:], reads=["gbf"], writes=["gr_d"])
            n = 1
            while n < REP:
                cx.dma("sp", gr_d[:, :, n:2 * n, :], gr_d[:, :, 0:n, :], reads=["gr_d"], writes=["gr_d"])
                n *= 2
        cx.barrier()
        if _STOP <= 1:
            return nc
        sD = es.enter_context(ExitStack())
        usb = sb(sD, "usb", [128, NT, 256], BF16)
        with ExitStack() as pu:
            wuf = sb(pu, "wuf", [128, 8, 256], F32)
            wub = sb(pu, "wub", [128, 8, 256], BF16)
            pU = [ps(pu, "pUu%d" % i, [128, 512], F32) for i in range(2)]
            cx.dma("sp", wuf[:], win_d.rearrange("(c p) n -> p c n", p=128)[:, :, 2304:2560], writes=["wuf"])
            cx.op("pool", lambda e: e.tensor_copy(out=wub[:], in_=wuf[:]), reads=["wuf"], writes=["wub"])
            for t in range(NT):
                b = t % 2
                cx.op("pe", [(lambda e, c=c: e.matmul(pU[b][:, 0:256], lhsT=hT[:, c, t * 128:(t + 1) * 128], rhs=wub[:, c, :],
                                                      start=(c == 0), stop=(c == 7))) for c in range(8)],
                      reads=[("hT", t // 4), "wub"], writes=["pUu%d" % b])
                cx.op("dve" if b else "act",
                      (lambda e: e.tensor_copy(out=usb[:, t, :], in_=pU[b][:, 0:256])) if b else
                      (lambda e: e.copy(out=usb[:, t, :], in_=pU[b][:, 0:256])),
                      reads=["pUu%d" % b], writes=[("usb", t)])
        cx.barrier()
        if _STOP <= 2:
            return nc
        with ExitStack() as pd:
            ABT = sb(pd, "ABT", [128, 2, 2, S], BF16)
            tabb = [sb(pd, "tabb%d" % i, [128, 32, 512], BF16) for i in range(2)]
            c64 = sb(pd, "c64", [128, 128], BF16)
            s64 = sb(pd, "s64", [128, 128], BF16)
            fwf = sb(pd, "fwf", [128, 2, 128], F32)
            fwb = sb(pd, "fwb", [128, 2, 128], BF16)
            M12 = sb(pd, "M12", [128, 2, 2, 128], BF16)
            fbf = sb(pd, "fbf", [1, 256], F32)
            fbb = sb(pd, "fbb", [1, 256], BF16)
            gfb = sb(pd, "gfb", [128, 256], F32)
            fjunk = sb(pd, "fjunk", [128, 256], F32)
            fssq = [sb(pd, "fssq%d" % i, [128, 1], F32) for i in range(2)]
            frstd = [sb(pd, "frstd%d" % i, [128, 1], F32) for i in range(2)]
            fnb = [sb(pd, "fnb%d" % i, [128, 256], BF16) for i in range(2)]
            fourT = sb(pd, "fourT", [128, 2, S], BF16)
            pU = [ps(pd, "pU%d" % i, [128, 512], F32) for i in range(2)]
            pF = [ps(pd, "pF%d" % i, [128, 512], F32) for i in range(2)]
            pM = ps(pd, "pM", [128, 512], F32)
            pFT = [ps(pd, "pFT%d" % i, [128, 2, 128], BF16) for i in range(2)]

            cx.dma("sp", c64[:], c64_d[:, :], writes=["c64"])
            cx.dma("sp", s64[:], s64_d[:, :], writes=["s64"])
            cx.dma("sp", fwf[:], fwb_d[:, :, :], writes=["fwf"])
            cx.dma("sp", fbf[:], fb_d[:, :], writes=["fbf"])
            cx.dma("sp", gfb[:], gf_d[:, :], writes=["gfb"])
            cx.op("pool", lambda e: e.tensor_copy(out=fwb[:], in_=fwf[:]), reads=["fwf"], writes=["fwb"])
            cx.op("pool", lambda e: e.tensor_copy(out=fbb[:], in_=fbf[:]), reads=["fbf"], writes=["fbb"])
            cx.op("pe", [(lambda e, cs=cs, cc=cc: e.matmul(pM[:, (cs * 2 + cc) * 128:(cs * 2 + cc + 1) * 128],
                                                           lhsT=(c64 if cs == 0 else s64)[:], rhs=fwb[:, cc, :],
                                                           start=True, stop=True)) for cs in range(2) for cc in range(2)],
                  reads=["c64", "s64", "fwb"], writes=["pM"])
            cx.op("act", lambda e: e.copy(out=M12[:].rearrange("p a b e -> p (a b e)"), in_=pM[:]),
                  reads=["pM"], writes=["M12"])
            it = 0
            for sbk in range(8):
                for cs, tsrc in ((0, tabc_d), (1, tabs_d)):
                    tb_ = it % 2
                    it += 1
                    cx.dma("sp", tabb[tb_][:], tsrc[sbk, :, :, :], writes=["tabb%d" % tb_])
                    for cc in range(2):
                        b = cc
                        cx.op("pe", [(lambda e, k=k: e.matmul(pF[b][:], lhsT=usb[:, k, cc * 128:(cc + 1) * 128],
                                                              rhs=tabb[tb_][:, k, :], start=(k == 0), stop=(k == 31)))
                                     for k in range(32)],
                              reads=[("usb", k_) for k_ in range(NT)] + ["tabb%d" % tb_], writes=["pF%d" % b])
                        if cc == 0:
                            cx.op("act", lambda e: e.copy(out=ABT[:, cs, cc, sbk * 512:(sbk + 1) * 512], in_=pF[b][:]),
                                  reads=["pF%d" % b], writes=["ABT"])
                        else:
                            cx.op("dve", lambda e: e.tensor_copy(out=ABT[:, cs, cc, sbk * 512:(sbk + 1) * 512], in_=pF[b][:]),
                                  reads=["pF%d" % b], writes=["ABT"])
            def f_s1(t):
                b = t % 2
                ts_ = slice(t * 128, (t + 1) * 128)
                mm = []
                for cc in range(2):
                    o_ap = pU[b][:, cc * 128:(cc + 1) * 128]
                    mm.append(lambda e, cc=cc, o_ap=o_ap: e.matmul(o_ap, lhsT=ABT[:, 0, cc, ts_], rhs=M12[:, 0, cc, :], start=True, stop=False))
                    mm.append(lambda e, cc=cc, o_ap=o_ap: e.matmul(o_ap, lhsT=ABT[:, 1, cc, ts_], rhs=M12[:, 1, cc, :], start=False, stop=False))
                    mm.append(lambda e, cc=cc, o_ap=o_ap: e.matmul(o_ap, lhsT=ones_b[0:1, :], rhs=fbb[0:1, cc * 128:(cc + 1) * 128], start=False, stop=True))
                cx.op("pe", mm, reads=["ABT", "M12", "ones_b", "fbb"], writes=["pU%d" % b])
                rms_rstd((fjunk[:], fssq[b][:], frstd[b][:], "fjunk", "fssq%d" % b, "frstd%d" % b),
                         pU[b][:, 0:256], "pU%d" % b, 256, "F")
                cx.op("dve", lambda e: e.scalar_tensor_tensor(out=fnb[b][:], in0=pU[b][:, 0:256], scalar=frstd[b][:, 0:1],
                                                              in1=gfb[:], op0=ALU.mult, op1=ALU.mult),
                      reads=["pU%d" % b, "frstd%d" % b, "gfb"], writes=["fnb%d" % b])

            def f_s2(t):
                b = t % 2
                ts_ = slice(t * 128, (t + 1) * 128)
                cx.op("pe", [(lambda e, cc=cc: e.transpose(out=pFT[b][:, cc, :], in_=fnb[b][:, cc * 128:(cc + 1) * 128],
                                                           identity=ident[:])) for cc in range(2)],
                      reads=["fnb%d" % b, "ident"], writes=["pFT%d" % b])
                cx.op("act", lambda e: e.copy(out=fourT[:, :, ts_], in_=pFT[b][:]),
                      reads=["pFT%d" % b], writes=["fourT"])

            for t in range(NT):
                f_s1(t)
                if t >= 1:
                    f_s2(t - 1)
            f_s2(NT - 1)
            for cc in range(2):
                cx.dma("act", mix_d[768 + cc * 128:768 + (cc + 1) * 128, :], fourT[:, cc, :], reads=["fourT"],
                       writes=[("mix_d", 6 + cc)])

        cx.barrier()
        if _STOP <= 3:
            return nc
        sD.close()
        with ExitStack() as pb:
            EB = sb(pb, "EB", [128, 6, 3, 512], BF16)
            if _STOP <= 4:
                return nc
            wfr = sb(pb, "wfr", [128, 1024], F32)
            wf = [wfr[:].rearrange("p (c n) -> p c n", c=8)]
            wqkv = sb(pb, "wqkv", [128, 3, 8, 128], BF16)
            QT = sb(pb, "QT", [128, 2, S], BF16)
            VT = sb(pb, "VT", [128, S], BF16)
            KPb = [sb(pb, "KP%d" % d, [128, d * (S // d + 128)], BF16) for d in DILS]
            Vs = sb(pb, "Vs", [128, 48, 192], BF16)
            acc = [sb(pb, "acc%d" % i, [128, S], F32) for i in range(2)]
            attn = [sb(pb, "attn%d" % i, [128, 512], BF16) for i in range(2)]
            Eb = [sb(pb, "Eb%d" % i, [128, 512], BF16) for i in range(2)]
            PTb = [sb(pb, "PTb%d" % i, [128, 512], BF16) for i in range(4)]
            pVt2 = [ps(pb, "pVt%d" % i, [128, 8, 128], BF16) for i in range(2)]
            vbi = 0
            pS = [ps(pb, "pS%d" % i, [128, 512], F32) for i in range(2)]
            pI = pS
            pO = [ps(pb, "pO%d" % i, [128, 512], F32) for i in range(4)]

            cx.op("pool", lambda e: e.memset(Vs[:], 1.0), writes=["VsL", "VsU"])
            for i_ in range(3):
                cx.op("pool", lambda e: e.memset(KPb[i_][:], 0.0), writes=[("KP", i_)])
            cx.op("pool", lambda e: e.memset(QT[:], 0.0), writes=[("QT", i) for i in range(8)])
            win_v = win_d.rearrange("(c p) n -> p c n", p=128)
            def load_weights(j):
                for wi, col0 in enumerate((j * 128, 768 + j * 128, 1536 + j * 128)):
                    cx.dma("sp", wf[0], win_v[:, :, col0:col0 + 128], writes=["wfr"])
                    cx.op("pool", lambda e: e.tensor_copy(out=wqkv[:, wi, :, :], in_=wf[0]),
                          reads=["wfr"], writes=[("wqkv", wi)])

            def epilogue_piece(j, cb_):
                cs = slice(cb_ * 512, (cb_ + 1) * 512)
                rden = wfr[:, (cb_ % 2) * 512:(cb_ % 2) * 512 + 512]
                rk = "wfr"
                cx.op("dve", [lambda e: e.tensor_copy(out=rden[0:64, :], in_=acc[0][64:128, cs]),
                              lambda e: e.tensor_copy(out=rden[64:128, :], in_=acc[1][0:64, cs])],
                      reads=["acc0", "acc1"], writes=[rk])
                cx.op("act", lambda e: e.activation(out=rden, in_=rden, func=AF.Ln), reads=[rk], writes=[rk])
                cx.op("act", lambda e: e.activation(out=rden, in_=rden, func=AF.Exp, scale=-1.0), reads=[rk], writes=[rk])
                ab_ = cb_ % 2
                cx.op("dve", [lambda e: e.tensor_tensor(out=attn[ab_][0:64, :], in0=acc[0][0:64, cs], in1=rden[0:64, :], op=ALU.mult),
                              lambda e: e.tensor_tensor(out=attn[ab_][64:128, :], in0=acc[1][64:128, cs], in1=rden[64:128, :], op=ALU.mult)],
                      reads=["acc0", "acc1", rk], writes=["attn%d" % ab_])
                cx.dma("act", mix_d[j * 128:(j + 1) * 128, cs], attn[ab_][:], reads=["attn%d" % ab_], writes=[("mix_d", j, cb_)])

            load_weights(0)
            for j in range(6):
                if j == 0:
                    for jp in range(6):
                        for d_ in range(3):
                            for (r0, off) in ((0, 63), (64, 127)):
                                for h_ in range(2):
                                    base = ((2 * jp + h_) * 3 + d_) * REP * P_G
                                    src = bass.AP(gr_d.tensor, base + off, [[P_G - 1, 64], [128, 2], [1, 128]])
                                    dst = EB[r0:r0 + 64, jp, d_, :].rearrange("p (k h c) -> p k h c", k=2, h=2)[:, :, h_, :]
                                    q_ = "sp"
                                    cx.dma(q_, dst, src, reads=["gr_d"], writes=[("EBp", jp, d_, r0, h_)])
                it = 0
                pend_epi = [(j - 1, c_) for c_ in range(8)] if j > 0 else []
                QTk = [("QT", i) for i in range(8)] + [("QTb", i) for i in range(8)]
                KTk = [("KT", i) for i in range(8)]
                VTk = [("VT", i) for i in range(8)]
                for wi, dst, key0 in ((0, QT, "QT"), (1, None, "KT"), (2, VT, "VT")):
                    for tb in range(8):
                        b = it % 2
                        it += 1
                        cx.op("pe", [(lambda e, c=c: e.matmul(pI[b][:], lhsT=wqkv[:, wi, c, :],
                                                              rhs=hT[:, c, tb * 512:(tb + 1) * 512],
                                                              start=(c == 0), stop=(c == 7))) for c in range(8)],
                              reads=[("wqkv", wi), ("hT", tb)], writes=["pS%d" % b])
                        key = (key0, tb)
                        tsl = slice(tb * 512, (tb + 1) * 512)
                        if wi == 0:
                            cx.op("act", lambda e: e.copy(out=QT[0:64, 0, tsl], in_=pI[b][0:64, :]),
                                  reads=["pS%d" % b], writes=[key])
                            cx.op("dve", lambda e: e.tensor_copy(out=QT[64:128, 1, tsl], in_=pI[b][64:128, :]),
                                  reads=["pS%d" % b], writes=[(key0 + "b", tb)])
                        elif wi == 2:
                            if tb % 2:
                                cx.op("act", lambda e: e.copy(out=VT[:, tsl], in_=pI[b][:]), reads=["pS%d" % b], writes=[key])
                            else:
                                cx.op("dve", lambda e: e.tensor_copy(out=VT[:, tsl], in_=pI[b][:]), reads=["pS%d" % b], writes=[key])
                        else:
                            src1 = pI[b][:].rearrange("p (m x) -> p m x", x=128)
                            k1 = KPb[0]
                            d0 = k1[:, tb * 512:tb * 512 + 512].rearrange("p (m x) -> p m x", x=128)
                            d1 = k1[:, tb * 512 + 128:tb * 512 + 640].rearrange("p (m x) -> p m x", x=128)
                            src4 = pI[b][:].rearrange("p (l r) -> p r l", r=4)
                            k4 = KPb[1][:].rearrange("p (r l) -> p r l", r=4)
                            src16 = pI[b][:].rearrange("p (l r) -> p r l", r=16)
                            k16 = KPb[2][:].rearrange("p (r l) -> p r l", r=16)
                            pos16 = 32 * tb + (128 if ((tb // 2) % 2) else 0)
                            for eng, pr in (("act", slice(0, 64)), ("dve", slice(64, 128))):
                                cp = (lambda e, o, i_: e.copy(out=o, in_=i_)) if eng == "act" else (lambda e, o, i_: e.tensor_copy(out=o, in_=i_))
                                cx.op(eng, [lambda e: cp(e, d0[pr, :, 0:64], src1[pr, :, 0:64]),
                                            lambda e: cp(e, d1[pr, :, 64:128], src1[pr, :, 64:128]),
                                            lambda e: cp(e, k4[pr, :, 128 * tb:128 * tb + 64], src4[pr, :, 0:64]),
                                            lambda e: cp(e, k4[pr, :, 128 * tb + 192:128 * tb + 256], src4[pr, :, 64:128]),
                                            lambda e: cp(e, k16[pr, :, pos16:pos16 + 32], src16[pr, :, :])],
                                      reads=["pS%d" % b], writes=[("KPa" if eng == "act" else "KPd", tb)])
                        if pend_epi and it % 3 == 0:
                            epilogue_piece(*pend_epi.pop(0))
                while pend_epi:
                    epilogue_piece(*pend_epi.pop(0))
                if j + 1 < 6:
                    load_weights(j + 1)
                KPk = [("KPa", i) for i in range(8)] + [("KPd", i) for i in range(8)] + [("KP", i) for i in range(3)]
                for h in range(2):
                    pass

                oi = 0
                si = 0
                for di, d in enumerate(DILS):
                    L = S // d
                    Lb = L + 128
                    nqb = L // 128
                    nsl = nqb + 1
                    KPv = KPb[di][:].rearrange("p (r l) -> p r l", r=d)
                    VTv = VT[:].rearrange("p (l r) -> p r l", r=d)
                    Vsv = Vs[:, 0:d * nsl, :].rearrange("p (r s) c -> p r s c", r=d)
                    for r in range(d):
                        for t0 in range(0, nqb, 8):
                            nt_ = min(8, nqb - t0)
                            vb = vbi % 2
                            vbi += 1
                            pVt = pVt2[vb]
                            cx.op("pe", [(lambda e, i=i: e.transpose(out=pVt[:, i, :],
                                                                     in_=VTv[:, r, (t0 + i) * 128:(t0 + i + 1) * 128],
                                                                     identity=ident[:])) for i in range(nt_)],
                                  reads=VTk + ["ident"], writes=["pVt%d" % vb])
                            cx.op("dve", [lambda e: e.tensor_copy(out=Vsv[0:64, r, t0:t0 + nt_, 0:64], in_=pVt[0:64, 0:nt_, 0:64]),
                                          lambda e: e.tensor_copy(out=Vsv[0:64, r, t0:t0 + nt_, 128:192], in_=pVt[0:64, 0:nt_, 64:128])],
                                  reads=["pVt%d" % vb], writes=["VsL"])
                            cx.op("act", [lambda e: e.copy(out=Vsv[64:128, r, t0 + 1:t0 + 1 + nt_, 0:64], in_=pVt[64:128, 0:nt_, 0:64]),
                                          lambda e: e.copy(out=Vsv[64:128, r, t0 + 1:t0 + 1 + nt_, 128:192], in_=pVt[64:128, 0:nt_, 64:128])],
                                  reads=["pVt%d" % vb], writes=["VsU"])
                    QTv = QT[:].rearrange("p a (l r) -> p a r l", r=d)
                    accv = [acc[h][:].rearrange("p (l r) -> p r l", r=d) for h in range(2)]
                    vcols = (slice(0, 128), slice(64, 192))
                    groups = []
                    if d == 16:
                        for r0 in range(0, 16, 2):
                            groups.append([(r0, 0), (r0, 1), (r0 + 1, 0), (r0 + 1, 1)])
                    else:
                        for r in range(d):
                            for g0 in range(0, nqb, 4):
                                groups.append([(r, g0 + i) for i in range(4)])
                    pend = []

                    def emit_pv(item):
                        (r, qb, sidx, ob, slot, grp) = item
                        pt = PTb[sidx]
                        lo_ok = qb > 0
                        hi_ok = qb < nqb - 1
                        for h in range(2):
                            vcol = vcols[h]
                            pob = pO[h * 2 + ob]
                            o_ap = pob[:, slot * 128:(slot + 1) * 128]
                            cb2 = slice(h * 128, h * 128 + 128)
                            ca = slice(256 + h * 128, 256 + h * 128 + 128)
                            mm = []
                            if lo_ok:
                                mm.append(lambda e: e.matmul(o_ap, lhsT=Vsv[:, r, qb, vcol], rhs=pt[:, ca], start=(slot == 0), stop=False, skip_group_check=True))
                            else:
                                mm.append(lambda e: e.matmul(o_ap, lhsT=Vsv[0:64, r, qb, vcol], rhs=pt[0:64, ca], start=(slot == 0), stop=False, skip_group_check=True))
                            if hi_ok:
                                mm.append(lambda e: e.matmul(o_ap, lhsT=Vsv[:, r, qb + 1, vcol], rhs=pt[:, cb2], start=False, stop=True, skip_group_check=True))
                            else:
                                mm.append(lambda e: e.matmul(o_ap, lhsT=Vsv[64:128, r, qb + 1, vcol], rhs=pt[64:128, cb2], start=False, stop=True, skip_group_check=True))
                            cx.op("pe", mm, reads=["VsL", "VsU", "PTb%d" % sidx], writes=[("pO", h * 2 + ob, slot)])
                            if slot == 3:
                                if d == 16:
                                    r0 = grp[0][0]
                                    dst = accv[h][:, r0:r0 + 2, :]
                                    srcp = pob[:].rearrange("p (a l) -> p a l", a=2)
                                else:
                                    r_, q0 = grp[0]
                                    dst = accv[h][:, r_, q0 * 128:q0 * 128 + 512]
                                    srcp = pob[:]
                                if di == 0:
                                    cx.op("act", lambda e: e.copy(out=dst, in_=srcp),
                                          reads=[("pO", h * 2 + ob, s_) for s_ in range(4)], writes=["acc%d" % h])
                                else:
                                    cx.op("dve", lambda e: e.tensor_tensor(out=dst, in0=srcp, in1=dst, op=ALU.add),
                                          reads=[("pO", h * 2 + ob, s_) for s_ in range(4)] + ["acc%d" % h], writes=["acc%d" % h])

                    for grp in groups:
                        ob = oi % 2
                        oi += 1
                        for slot, (r, qb) in enumerate(grp):
                            sidx = si % 4
                            ebi = si % 2
                            psi = si % 2
                            si += 1
                            q_ap = QTv[:, :, r, qb * 128:(qb + 1) * 128]
                            kA = KPv[:, r, qb * 128:(qb + 1) * 128]
                            kB = KPv[:, r, (qb + 1) * 128:(qb + 2) * 128]
                            oB = pS[psi][:, 0:256].rearrange("p (a q) -> p a q", a=2)
                            oA = pS[psi][:, 256:512].rearrange("p (a q) -> p a q", a=2)
                            cx.op("pe", [lambda e: e.matmul(oA, lhsT=kA, rhs=q_ap, start=True, stop=True),
                                         lambda e: e.matmul(oB, lhsT=kB, rhs=q_ap, start=True, stop=True)],
                                  reads=QTk + KPk, writes=["pS%d" % psi])
                            cx.op("act", lambda e: e.activation(out=Eb[ebi][:], in_=pS[psi][:], func=AF.Exp, scale=0.125),
                                  reads=["pS%d" % psi], writes=["Eb%d" % ebi])
                            cx.op("dve", lambda e: e.tensor_tensor(out=PTb[sidx][:], in0=Eb[ebi][:], in1=EB[:, j, di, :], op=ALU.mult),
                                  reads=["Eb%d" % ebi] + [("EBp", j, di, r0_, h_) for r0_ in (0, 64) for h_ in range(2)], writes=["PTb%d" % sidx])
                            pend.append((r, qb, sidx, ob, slot, grp))
                            if len(pend) > 2:
                                emit_pv(pend.pop(0))
                    while pend:
                        emit_pv(pend.pop(0))

            for cb_ in range(8):
                epilogue_piece(5, cb_)
        cx.barrier()
        if _STOP <= 5:
            return nc
        s2.close()
        sW = es.enter_context(ExitStack())
        wgb = sb(sW, "wgb", [128, 8, DFF], BF16)
        wvb = sb(sW, "wvb", [128, 8, DFF], BF16)
        wdb = sb(sW, "wdb", [128, NF, D], BF16)
        wst = [sb(sW, "wst%d" % i, [128, 704], F32) for i in range(3)]
        wjobs = []
        for (src_d, dstw, key) in ((wg_d, wgb, "wgb"), (wv_d, wvb, "wvb")):
            for c in range(8):
                for q4 in range(4):
                    cols = slice(q4 * 704, (q4 + 1) * 704)
                    wjobs.append((src_d[c * 128:(c + 1) * 128, cols], dstw[:, c, cols], 704, key))
        for f in range(NF):
            wjobs.append((wd_d[f * 128:(f + 1) * 128, 0:704], wdb[:, f, 0:704], 704, "wdb"))
            wjobs.append((wd_d[f * 128:(f + 1) * 128, 704:D], wdb[:, f, 704:D], D - 704, "wdb"))
        winfl = [None, None, None]
        wstate = {"i": 0}

        def emit_wslot(b):
            if winfl[b] is not None:
                dst, wid, key = winfl[b]
                i = wstate["i"]
                wstate["i"] += 1
                if i % 2:
                    cx.op("act", lambda e: e.copy(out=dst, in_=wst[b][:, 0:wid]), reads=["wst%d" % b], writes=[key])
                else:
                    cx.op("dve", lambda e: e.tensor_copy(out=dst, in_=wst[b][:, 0:wid]), reads=["wst%d" % b], writes=[key])
                winfl[b] = None
            if wjobs:
                src, dst, wid, key = wjobs.pop(0)
                cx.dma("sp", wst[b][:, 0:wid], src, writes=["wst%d" % b])
                winfl[b] = (dst, wid, key)

        with ExitStack() as pe2:
            wob = sb(pe2, "wob", [128, 8, D], BF16)
            gat = sb(pe2, "gat", [128, 6], F32)
            g2b = sb(pe2, "g2b", [128, D], F32)
            mixb = [sb(pe2, "mixb%d" % i, [128, 8, 512], BF16) for i in range(2)]
            sqb = [sb(pe2, "sqb%d" % i, [128, 512], BF16) for i in range(2)]
            rsa = sb(pe2, "rsa", [128, 512], F32)
            x1 = [sb(pe2, "x1_%d" % i, [128, D], F32) for i in range(3)]
            ssqE = [sb(pe2, "ssqE%d" % i, [128, 1], F32) for i in range(2)]
            rstdE = [sb(pe2, "rstdE%d" % i, [128, 1], F32) for i in range(2)]
            h2b = [sb(pe2, "h2b%d" % i, [128, D], BF16) for i in range(3)]
            h2T = [sb(pe2, "h2T%d" % i, [128, 8, 128], BF16) for i in range(2)]
            zcol = sb(pe2, "zcol", [128, 8, 1], BF16)
            pR = ps(pe2, "pR", [128, 512], F32)
            pY = [ps(pe2, "pY%d" % i, [128, D], F32) for i in range(2)]
            pT2 = [ps(pe2, "pT2%d" % i, [128, 8, 128], BF16) for i in range(2)]

            cx.dma("sp", gat[:], ga_d[:, :], writes=["gat"])
            cx.dma("sp", g2b[:], g2_d[:, :], writes=["g2b"])
            for c in range(8):
                b = c % 2
                for (c0_, c1_) in ((0, 704), (704, D)):
                    bb = (2 * c + (c0_ > 0)) % 3
                    cx.dma("sp", wst[bb][:, 0:c1_ - c0_], wout_d[c * 128:(c + 1) * 128, c0_:c1_], writes=["wst%d" % bb])
                    if c % 2:
                        cx.op("act", lambda e: e.copy(out=wob[:, c, c0_:c1_], in_=wst[bb][:, 0:c1_ - c0_]), reads=["wst%d" % bb], writes=["wob"])
                    else:
                        cx.op("dve", lambda e: e.tensor_copy(out=wob[:, c, c0_:c1_], in_=wst[bb][:, 0:c1_ - c0_]), reads=["wst%d" % bb], writes=["wob"])
            cx.op("pool", lambda e: e.memset(zcol[:], 0.0), writes=["zcol"])
            h2v = h2_d.rearrange("(c p) s -> p c s", p=128)
            cx.dma("act", h2v[:, :, 0:1], zcol[:], reads=["zcol"], writes=["h2halo"], allow_slow_non_contiguous=True)
            cx.dma("act", h2v[:, :, S + 1:S + 2], zcol[:], reads=["zcol"], writes=["h2halo"], allow_slow_non_contiguous=True)
            mixv = mix_d.rearrange("(c p) s -> p c s", p=128)
            def e_load(blk):
                mb = blk % 2
                bs = slice(blk * 512, (blk + 1) * 512)
                cx.dma("sp", mixb[mb][:], mixv[:, :, bs], reads=[("mix_d", i, blk) for i in range(6)] + [("mix_d", 6), ("mix_d", 7)], writes=["mixb%d" % mb])

            def e_prologue(blk):
                mb = blk % 2
                for jj in range(6):
                    q = jj % 2
                    cx.op("act", lambda e: e.activation(out=sqb[q][:], in_=mixb[mb][:, jj, :], func=AF.Square),
                          reads=["mixb%d" % mb], writes=["sqb%d" % q])
                    cx.op("pe", lambda e: e.matmul(pR[:], lhsT=ones_b[:], rhs=sqb[q][:], start=(jj == 0), stop=(jj == 5)),
                          reads=["ones_b", "sqb%d" % q], writes=["pR"])
                cx.op("act", lambda e: e.activation(out=rsa[:], in_=pR[:], func=AF.Sqrt, scale=1.0 / 768, bias=epsb[:, 0:1]),
                      reads=["pR", "epsb"], writes=["rsa"])
                cx.op("dve", lambda e: e.reciprocal(out=rsa[:], in_=rsa[:]), reads=["rsa"], writes=["rsa"])
                for jj in range(6):
                    cx.op("dve", lambda e: e.scalar_tensor_tensor(out=mixb[mb][:, jj, :], in0=mixb[mb][:, jj, :],
                                                                  scalar=gat[:, jj:jj + 1], in1=rsa[:],
                                                                  op0=ALU.mult, op1=ALU.mult),
                          reads=["mixb%d" % mb, "gat", "rsa"], writes=["mixb%d" % mb])

            def e_s1(t):
                blk, tt = t // 4, t % 4
                mb = blk % 2
                b = t % 2
                b3 = t % 3
                xk = "x1_%d" % b3
                mm = []
                for hf in range(2):
                    for c in range(8):
                        mm.append(lambda e, hf=hf, c=c: e.matmul(pY[b][:, hf * 512:(hf + 1) * 512],
                                                                 lhsT=mixb[mb][:, c, tt * 128:(tt + 1) * 128],
                                                                 rhs=wob[:, c, hf * 512:(hf + 1) * 512],
                                                                 start=(c == 0), stop=(c == 7)))
                cx.op("pe", mm, reads=["mixb%d" % mb, "wob"], writes=["pY%d" % b])
                cx.op("dve", lambda e: e.tensor_tensor(out=x1[b3][:], in0=pY[b][:], in1=x1[b3][:], op=ALU.add),
                      reads=["pY%d" % b, xk], writes=[xk])
                cx.dma("act", x1_d[t * 128:(t + 1) * 128, :], x1[b3][:], reads=[xk], writes=[("x1_d", t)])
                rms_rstd((h2b[b3][:], ssqE[b][:], rstdE[b][:], "h2b%d" % b3, "ssqE%d" % b, "rstdE%d" % b),
                         x1[b3][:], xk, D, "E")
                cx.op("dve", lambda e: e.scalar_tensor_tensor(out=h2b[b3][:], in0=x1[b3][:], scalar=rstdE[b][:, 0:1],
                                                              in1=g2b[:], op0=ALU.mult, op1=ALU.mult),
                      reads=[xk, "rstdE%d" % b, "g2b"], writes=["h2b%d" % b3])
                for b_w in range(3):
                    emit_wslot(b_w)

            def e_s2(t):
                b = t % 2
                b3 = t % 3
                cx.op("pe", [(lambda e, c=c: e.transpose(out=pT2[b][:, c, :], in_=h2b[b3][:, c * 128:(c + 1) * 128],
                                                         identity=ident[:])) for c in range(8)],
                      reads=["h2b%d" % b3, "ident"], writes=["pT2%d" % b])
                cx.op("act", lambda e: e.copy(out=h2T[b][:], in_=pT2[b][:]), reads=["pT2%d" % b], writes=["h2T%d" % b])
                cx.dma("act", h2v[:, :, 1 + t * 128:1 + (t + 1) * 128], h2T[b][:], reads=["h2T%d" % b],
                       writes=[("h2_d", t)])

            def load_x(t):
                cx.dma("sp", x1[t % 3][:], x_d[t * 128:(t + 1) * 128, :], writes=["x1_%d" % (t % 3)])

            e_load(0)
            e_prologue(0)
            load_x(0)
            for t in range(NT):
                if t % 4 == 0 and t // 4 + 1 < 8:
                    e_load(t // 4 + 1)
                if t + 1 < NT:
                    load_x(t + 1)
                e_s1(t)
                if t % 4 == 3 and t // 4 + 1 < 8:
                    e_prologue(t // 4 + 1)
                if t >= 2:
                    e_s2(t - 2)
            e_s2(NT - 2)
            e_s2(NT - 1)
            while wjobs or any(w is not None for w in winfl):
                for b_w in range(3):
                    emit_wslot(b_w)
        cx.barrier()
        if _STOP <= 6:
            return nc
        with ExitStack() as pf:
            cwt = sb(pf, "cwt", [128, NF, 3], F32)
            cbt = sb(pf, "cbt", [128, NF], F32)
            gFb = sb(pf, "gFb", [128, D], F32)
            h2s = [sb(pf, "h2s%d" % i, [128, 8, 258], BF16) for i in range(2)]
            t1 = [sb(pf, "t1_%d" % i, [128, 256], F32) for i in range(2)]
            t2 = [sb(pf, "t2_%d" % i, [128, 256], F32) for i in range(2)]
            sg = [sb(pf, "sg_%d" % i, [128, 256], F32) for i in range(2)]
            aT = [sb(pf, "aT_%d" % i, [128, 256], BF16) for i in range(4)]
            x1r = [sb(pf, "x1r%d" % i, [128, D], F32) for i in range(2)]
            of_ = [sb(pf, "of%d" % i, [128, D], F32) for i in range(2)]
            junkF = sb(pf, "junkF", [128, D], F32)
            ssqF = [sb(pf, "ssqF%d" % i, [128, 1], F32) for i in range(2)]
            rstdF = [sb(pf, "rstdF%d" % i, [128, 1], F32) for i in range(2)]
            yo = [sb(pf, "yo%d" % i, [128, D], F32) for i in range(2)]
            pGt = [ps(pf, "pGt%d" % i, [128, 512], F32) for i in range(2)]
            pVl = [ps(pf, "pVl%d" % i, [128, 512], F32) for i in range(2)]
            pD = [ps(pf, "pD%d" % i, [128, D], F32) for i in range(2)]

            cx.dma("sp", cwt[:], cw_d[:, :, :], writes=["cwt"])
            cx.dma("sp", cbt[:], cb_d[:, :], writes=["cbt"])
            cx.dma("sp", gFb[:], gF_d[:, :], writes=["gFb"])
            h2v = h2_d.rearrange("(c p) s -> p c s", p=128)
            _SUB = int(os.environ.get("KSUB", "9"))
            def load_h2s(st_):
                cx.dma("sp", h2s[st_ % 2][:], h2v[:, :, st_ * 256:st_ * 256 + 258],
                       reads=[("h2_d", t) for t in range(max(0, 2 * st_ - 1), min(NT, 2 * st_ + 3))] + ["h2halo"],
                       writes=["h2s%d" % (st_ % 2)])

            load_h2s(0)
            for st in range(16 if _SUB > 1 else 0):
                hb_ = st % 2
                for tt in range(2):
                    t = st * 2 + tt
                    cx.dma("sp", x1r[t % 2][:], x1_d[t * 128:(t + 1) * 128, :], reads=[("x1_d", t)], writes=["x1r%d" % (t % 2)])
                if st + 1 < 16:
                    load_h2s(st + 1)
                pend = []

                def emit_down(item):
                    f, ab = item
                    mm = []
                    for tt in range(2):
                        for hf in range(2):
                            mm.append(lambda e, tt=tt, hf=hf: e.matmul(pD[tt][:, hf * 512:(hf + 1) * 512],
                                                                       lhsT=aT[ab][:, tt * 128:(tt + 1) * 128],
                                                                       rhs=wdb[:, f, hf * 512:(hf + 1) * 512],
                                                                       start=(f == 0), stop=(f == NF - 1)))
                    cx.op("pe", mm, reads=["aT_%d" % ab, "wdb"], writes=(["pD0", "pD1"] if f in (0, NF - 1) else []))

                for f in range(NF):
                    b = f % 2
                    ab = f % 4
                    fs = slice(f * 128, (f + 1) * 128)
                    cx.op("pe", [(lambda e, c=c: e.matmul(pGt[b][:, 0:258], lhsT=wgb[:, c, fs], rhs=h2s[hb_][:, c, 0:258],
                                                          start=(c == 0), stop=(c == 7))) for c in range(8)],
                          reads=["wgb", "h2s%d" % hb_], writes=["pGt%d" % b])
                    cx.op("pe", [(lambda e, c=c: e.matmul(pVl[b][:, 0:256], lhsT=wvb[:, c, fs], rhs=h2s[hb_][:, c, 1:257],
                                                          start=(c == 0), stop=(c == 7))) for c in range(8)],
                          reads=["wvb", "h2s%d" % hb_], writes=["pVl%d" % b])
                    cx.op("act", lambda e: e.activation(out=t1[b][:], in_=pGt[b][:, 1:257], func=AF.Identity,
                                                        scale=cwt[:, f, 1:2], bias=cbt[:, f:f + 1]),
                          reads=["pGt%d" % b, "cwt", "cbt"], writes=["t1_%d" % b])
                    cx.op("dve", lambda e: e.scalar_tensor_tensor(out=t2[b][:], in0=pGt[b][:, 0:256], scalar=cwt[:, f, 0:1],
                                                                  in1=t1[b][:], op0=ALU.mult, op1=ALU.add),
                          reads=["pGt%d" % b, "cwt", "t1_%d" % b], writes=["t2_%d" % b])
                    cx.op("dve", lambda e: e.scalar_tensor_tensor(out=t1[b][:], in0=pGt[b][:, 2:258], scalar=cwt[:, f, 2:3],
                                                                  in1=t2[b][:], op0=ALU.mult, op1=ALU.add),
                          reads=["pGt%d" % b, "cwt", "t2_%d" % b], writes=["t1_%d" % b])
                    cx.op("act", lambda e: e.activation(out=sg[b][:], in_=t1[b][:], func=AF.Silu),
                          reads=["t1_%d" % b], writes=["sg_%d" % b])
                    cx.op("dve", lambda e: e.tensor_tensor(out=aT[ab][:], in0=pVl[b][:, 0:256], in1=sg[b][:], op=ALU.mult),
                          reads=["pVl%d" % b, "sg_%d" % b], writes=["aT_%d" % ab])
                    if _SUB > 2:
                        pend.append((f, ab))
                    if len(pend) > 2:
                        emit_down(pend.pop(0))
                while pend:
                    emit_down(pend.pop(0))
                for tt in range(2 if _SUB > 3 else 0):
                    t = st * 2 + tt
                    b = t % 2
                    cx.op("act", lambda e: e.copy(out=of_[b][:], in_=pD[tt][:]), reads=["pD%d" % tt], writes=["of%d" % b])
                    cx.op("dve", lambda e: e.tensor_tensor(out=of_[b][:], in0=of_[b][:], in1=x1r[b][:], op=ALU.add),
                          reads=["of%d" % b, "x1r%d" % b], writes=["of%d" % b])
                    rms_rstd((junkF[:], ssqF[b][:], rstdF[b][:], "junkF", "ssqF%d" % b, "rstdF%d" % b),
                             of_[b][:], "of%d" % b, D, "F")
                    cx.op("dve", lambda e: e.scalar_tensor_tensor(out=yo[b][:], in0=of_[b][:], scalar=rstdF[b][:, 0:1],
                                                                  in1=gFb[:], op0=ALU.mult, op1=ALU.mult),
                          reads=["of%d" % b, "rstdF%d" % b, "gFb"], writes=["yo%d" % b])
                    cx.dma(os.environ.get("KYQ", "sp"), y_d[t * 128:(t + 1) * 128, :], yo[b][:], reads=["yo%d" % b], writes=[("y", t)])
            cx.finish("sp", [("y", t) for t in range(NT)])
            cx.barrier()
    return nc


def _t5_bucket_np(rel):
    nb = 16
    max_exact = 8
    ret = np.where(rel > 0, nb, 0)
    n = np.abs(rel)
    nf = np.maximum(n, 1).astype(np.float32)
    large = max_exact + (np.log(nf / np.float32(max_exact)) / np.float32(math.log(1024 / max_exact))
                         * np.float32(nb - max_exact)).astype(np.int32)
    large = np.minimum(large, nb - 1)
    return ret + np.where(n < max_exact, n, large)


_CONST = {}


def _constants():
    if _CONST:
        return _CONST
    bf = ml_dtypes.bfloat16
    m = np.arange(P_G)
    rel = 191 - m
    oh = np.zeros((32, 3, P_G), np.float32)
    gmask = np.zeros((12, 3, P_G), np.float32)
    for di, d in enumerate(DILS):
        bk = _t5_bucket_np(rel * d)
        valid = np.abs(rel) <= 64
        oh[bk[valid], di, m[valid]] = 1.0
        gmask[:, di, valid] = 1.0
    _CONST["onehot"] = oh
    _CONST["gmask"] = gmask
    c = np.arange(64)
    ang = 2 * np.pi * np.outer(c, c) / 64
    C64 = np.cos(ang) / 512.0
    S64 = -np.sin(ang) / 512.0
    z = np.zeros((64, 64))
    _CONST["c64blk"] = np.block([[C64, z], [z, C64]]).astype(bf)
    _CONST["s64blk"] = np.block([[S64, z], [z, S64]]).astype(bf)
    _CONST["ident"] = np.eye(128, dtype=np.float32).astype(bf)
    s = np.arange(S, dtype=np.int64)
    prod = (s[:, None] * s[None, :]) % S
    angt = prod.astype(np.float64) * (2 * np.pi / S)
    for name, fn in (("dft_cos", np.cos), ("dft_sin", np.sin)):
        t = fn(angt).astype(np.float32)
        t = t.reshape(32, 128, 8, 512).transpose(2, 1, 0, 3)
        _CONST[name] = np.ascontiguousarray(t).astype(bf)
    return _CONST


_NC = {}


def kernel(x, norm_mix_gain, w_in, attn_out_gain, rel_bias_table, fourier_w, fourier_b, fourier_out_gain,
           w_out, norm_ffn_gain, w_gate, w_val, conv_w, conv_b, w_down, final_norm_gain):
    f32 = np.float32
    x = np.asarray(x, f32)
    cst = _constants()
    bc = lambda v, n: np.ascontiguousarray(np.broadcast_to(np.asarray(v, f32).reshape(1, n), (128, n)))
    fw = np.asarray(fourier_w, f32)[0]
    fw_blk = np.zeros((128, 2, 128), f32)
    for cc in range(2):
        fw_blk[0:64, cc, 0:64] = fw[2 * cc]
        fw_blk[64:128, cc, 64:128] = fw[2 * cc + 1]
    shared = {
        "g1b": bc(norm_mix_gain[0], D), "g2b": bc(norm_ffn_gain[0], D), "gFb": bc(final_norm_gain, D),
        "gfb": bc(fourier_out_gain[0], 256),
        "ga_t": np.ascontiguousarray(np.asarray(attn_out_gain, f32)[0].reshape(6, 128).T),
        "w_in": np.ascontiguousarray(np.asarray(w_in, f32)[0]),
        "w_out": np.ascontiguousarray(np.asarray(w_out, f32)[0]),
        "w_gate": np.ascontiguousarray(np.asarray(w_gate, f32)[0]),
        "w_val": np.ascontiguousarray(np.asarray(w_val, f32)[0]),
        "w_down": np.ascontiguousarray(np.asarray(w_down, f32)[0]),
        "cw_t": np.ascontiguousarray(np.asarray(conv_w, f32)[0].reshape(3, NF, 128).transpose(2, 1, 0)),
        "cb_t": np.ascontiguousarray(np.asarray(conv_b, f32)[0].reshape(NF, 128).T),
        "rel_tab": np.ascontiguousarray(np.asarray(rel_bias_table, f32)),
        "onehot": cst["onehot"], "gmask": cst["gmask"],
        "fw_blk": fw_blk,
        "fb_row": np.ascontiguousarray(np.asarray(fourier_b, f32)[0].reshape(1, 256)),
        "c64blk": cst["c64blk"], "s64blk": cst["s64blk"], "ident": cst["ident"],
        "dft_cos": cst["dft_cos"], "dft_sin": cst["dft_sin"],
    }
    n = x.shape[0]
    if "nc" not in _NC:
        _NC["nc"] = build_nc()
    nc = _NC["nc"]
    in_maps = [dict(shared, x=np.ascontiguousarray(x[b])) for b in range(n)]
    res = run_bass_kernel_spmd(nc, in_maps, core_ids=list(range(n)))
    return np.stack([np.asarray(r["y"], f32) for r in res.results], axis=0)
```

```python
import math
import os
from contextlib import ExitStack

import numpy as np
import ml_dtypes
import concourse.bass as bass
import concourse.mybir as mybir
from concourse.bass_utils import run_bass_kernel_spmd

F32 = mybir.dt.float32
BF16 = mybir.dt.bfloat16
AF = mybir.ActivationFunctionType
ALU = mybir.AluOpType

S = 4096
D = 1024
NT = 32
DFF = 2816
NF = 22
EPS = 1e-6
DILS = (1, 4, 16)
P_G = 384
REP = 64


class Ctx:
    def __init__(self, nc, es):
        self.nc = nc
        self.E = {"pe": nc.tensor, "act": nc.scalar, "dve": nc.vector, "pool": nc.gpsimd, "sp": nc.sync}
        self.sem = {k: es.enter_context(nc.semaphore("sem_" + k)) for k in self.E}
        self.cnt = {k: 0 for k in self.E}
        self.seen = {k: {} for k in self.E}
        self.lastw = {}
        self.readers = {}
        nd = 64
        self.dsem = [es.enter_context(nc.semaphore("dsem%d" % i)) for i in range(nd)]
        self.dval = [0] * nd
        self.dnext = 0

    def _wait(self, eng, tok):
        sem, key, val = tok
        if self.seen[eng].get(key, 0) >= val:
            return
        self.E[eng].wait_ge(sem, val)
        self.seen[eng][key] = val

    def _deps(self, eng, reads, writes):
        for k in list(reads) + list(writes):
            t = self.lastw.get(k)
            if t is not None:
                self._wait(eng, t)
        for k in writes:
            for t in self.readers.get(k, {}).values():
                self._wait(eng, t)

    def _commit(self, tok, reads, writes):
        for k in writes:
            self.lastw[k] = tok
            self.readers[k] = {}
        for k in reads:
            d = self.readers.setdefault(k, {})
            if tok[1] not in d or d[tok[1]][2] < tok[2]:
                d[tok[1]] = tok

    def op(self, eng, fns, reads=(), writes=()):
        self._deps(eng, reads, writes)
        if callable(fns):
            fns = [fns]
        ins = None
        for f in fns:
            ins = f(self.E[eng])
        self.cnt[eng] += 1
        ins.then_inc(self.sem[eng], 1)
        tok = (self.sem[eng], eng, self.cnt[eng])
        self._commit(tok, reads, writes)

    def dma(self, q, out, in_, reads=(), writes=(), **kw):
        self._deps(q, reads, writes)
        i = self.dnext
        self.dnext = (i + 1) % len(self.dsem)
        key = "d%d" % i
        if self.dval[i] > 0:
            self._wait(q, (self.dsem[i], key, self.dval[i]))
        self.E[q].dma_start(out=out, in_=in_, **kw).then_inc(self.dsem[i], 16)
        self.dval[i] += 16
        tok = (self.dsem[i], key, self.dval[i])
        self._commit(tok, reads, writes)

    def barrier(self):
        if os.environ.get("KDBG"):
            print("barrier counts", self.cnt, max(self.dval))
        toks = [(self.sem[k], k, self.cnt[k]) for k in self.E if self.cnt[k] > 0]
        toks += [(self.dsem[i], "d%d" % i, self.dval[i]) for i in range(len(self.dsem)) if self.dval[i] > 0]
        for eng in self.E:
            for t in toks:
                if t[1] != eng:
                    self._wait(eng, t)

    def finish(self, eng, keys):
        for k in keys:
            t = self.lastw.get(k)
            if t is not None:
                self._wait(eng, t)


_STOP = int(os.environ.get('KSTOP', '9'))


def build_nc():
    nc = bass.Bass("TRN2", target_bir_lowering=False)

    def din(name, shape, dt=F32):
        return nc.dram_tensor(name, list(shape), dt, kind="ExternalInput").ap()

    def dscr(name, shape, dt):
        return nc.dram_tensor(name, list(shape), dt, kind="Internal").ap()

    x_d = din("x", [S, D])
    g1_d = din("g1b", [128, D])
    g2_d = din("g2b", [128, D])
    gF_d = din("gFb", [128, D])
    gf_d = din("gfb", [128, 256])
    ga_d = din("ga_t", [128, 6])
    win_d = din("w_in", [D, 2560])
    wout_d = din("w_out", [D, D])
    wg_d = din("w_gate", [D, DFF])
    wv_d = din("w_val", [D, DFF])
    wd_d = din("w_down", [DFF, D])
    cw_d = din("cw_t", [128, NF, 3])
    cb_d = din("cb_t", [128, NF])
    tab_d = din("rel_tab", [32, 12])
    oh_d = din("onehot", [32, 3, P_G])
    gm_d = din("gmask", [12, 3, P_G])
    fwb_d = din("fw_blk", [128, 2, 128])
    fb_d = din("fb_row", [1, 256])
    c64_d = din("c64blk", [128, 128], BF16)
    s64_d = din("s64blk", [128, 128], BF16)
    id_d = din("ident", [128, 128], BF16)
    tabc_d = din("dft_cos", [8, 128, 32, 512], BF16)
    tabs_d = din("dft_sin", [8, 128, 32, 512], BF16)
    y_d = nc.dram_tensor("y", [S, D], F32, kind="ExternalOutput").ap()

    mix_d = dscr("mix_scr", [D, S], BF16)
    x1_d = dscr("x1_scr", [S, D], F32)
    h2_d = dscr("h2_scr", [D, S + 2], BF16)
    gr_d = dscr("gr_scr", [12, 3, REP, P_G], BF16)

    with ExitStack() as es:
        cx = Ctx(nc, es)

        def sb(stack, name, shape, dt):
            return stack.enter_context(nc.sbuf_tensor("sb_" + name, list(shape), dt))

        def ps(stack, name, shape, dt):
            return stack.enter_context(nc.psum_tensor("ps_" + name, list(shape), dt))

        ident = sb(es, "ident", [128, 128], BF16)
        ones_b = sb(es, "ones_b", [128, 128], BF16)
        epsb = sb(es, "epsb", [128, 1], F32)
        s2 = es.enter_context(ExitStack())
        hT = sb(s2, "hT", [128, 8, S], BF16)
        cx.dma("sp", ident[:], id_d[:, :], writes=["ident"])
        cx.op("pool", lambda e: e.memset(ones_b[:], 1.0), writes=["ones_b"])
        cx.op("pool", lambda e: e.memset(epsb[:], EPS), writes=["epsb"])

        def rms_rstd(stack_tiles, src_ap, src_key, n, tag):
            junk, ssq, rstd, kj, ks, kr = stack_tiles
            cx.op("act", lambda e: e.activation(out=junk, in_=src_ap, func=AF.Square, accum_out=ssq),
                  reads=[src_key], writes=[kj, ks])
            cx.op("act", lambda e: e.activation(out=rstd, in_=ssq, func=AF.Sqrt, scale=1.0 / n, bias=epsb[:, 0:1]),
                  reads=[ks, "epsb"], writes=[kr])
            cx.op("dve", lambda e: e.reciprocal(out=rstd, in_=rstd), reads=[kr], writes=[kr])

        with ExitStack() as pa:
            g1b = sb(pa, "g1b", [128, D], F32)
            tab = sb(pa, "tab", [32, 12], F32)
            oh = sb(pa, "oh", [32, 3, P_G], F32)
            gm = sb(pa, "gm", [12, 3, P_G], F32)
            gsb = sb(pa, "gsb", [12, 3, P_G], F32)
            gbf = sb(pa, "gbf", [12, 3, P_G], BF16)
            pG = ps(pa, "pG", [12, 3, 512], F32)
            cx.dma("sp", g1b[:], g1_d[:, :], writes=["g1b"])
            xt = [sb(pa, "xt%d" % i, [128, D], F32) for i in range(2)]
            junk = sb(pa, "junkA", [128, D], F32)
            hb = [sb(pa, "hb%d" % i, [128, D], BF16) for i in range(2)]
            ssq = [sb(pa, "ssqA%d" % i, [128, 1], F32) for i in range(2)]
            rstd = [sb(pa, "rstdA%d" % i, [128, 1], F32) for i in range(2)]
            pT = [ps(pa, "pTA%d" % i, [128, 8, 128], BF16) for i in range(2)]
            def a_s2(t):
                b = t % 2
                cx.op("pe", [(lambda e, c=c: e.transpose(out=pT[b][:, c, :], in_=hb[b][:, c * 128:(c + 1) * 128],
                                                         identity=ident[:])) for c in range(8)],
                      reads=["hb%d" % b, "ident"], writes=["pTA%d" % b])
                cx.op("act", lambda e: e.copy(out=hT[:, :, t * 128:(t + 1) * 128], in_=pT[b][:]),
                      reads=["pTA%d" % b], writes=[("hT", t // 4)])

            for t in range(NT):
                b = t % 2
                cx.dma("sp", xt[b][:], x_d[t * 128:(t + 1) * 128, :], writes=["xt%d" % b])
                rms_rstd((junk[:], ssq[b][:], rstd[b][:], "junkA", "ssqA%d" % b, "rstdA%d" % b),
                         xt[b][:], "xt%d" % b, D, "A")
                cx.op("dve", lambda e: e.scalar_tensor_tensor(out=hb[b][:], in0=xt[b][:], scalar=rstd[b][:, 0:1],
                                                              in1=g1b[:], op0=ALU.mult, op1=ALU.mult),
                      reads=["xt%d" % b, "rstdA%d" % b, "g1b"], writes=["hb%d" % b])
                if t >= 1:
                    a_s2(t - 1)
            a_s2(NT - 1)
            cx.dma("sp", tab[:], tab_d[:, :], writes=["tab"])
            cx.dma("sp", oh[:], oh_d[:, :, :], writes=["oh"])
            cx.dma("sp", gm[:], gm_d[:, :, :], writes=["gm"])
            cx.op("pe", [(lambda e, d=d: e.matmul(pG[:, d, 0:P_G], lhsT=tab[:], rhs=oh[:, d, :], start=True, stop=True))
                         for d in range(3)], reads=["tab", "oh"], writes=["pG"])
            cx.op("act", lambda e: e.activation(out=gsb[:], in_=pG[:, :, 0:P_G], func=AF.Exp),
                  reads=["pG"], writes=["gsb"])
            cx.op("dve", lambda e: e.tensor_tensor(out=gbf[:], in0=gsb[:], in1=gm[:], op=ALU.mult),
                  reads=["gsb", "gm"], writes=["gbf"])
            cx.dma("sp", gr_d[:, :, 0, :], gbf[:], reads=["gbf"], writes=["gr_d"])
            n = 1
            while n < REP:
                cx.dma("sp", gr_d[:, :, n:2 * n, :], gr_d[:, :, 0:n, :], reads=["gr_d"], writes=["gr_d"])
                n *= 2
        cx.barrier()
        if _STOP <= 1:
            return nc
        sD = es.enter_context(ExitStack())
        usb = sb(sD, "usb", [128, NT, 256], BF16)
        with ExitStack() as pu:
            wuf = sb(pu, "wuf", [128, 8, 256], F32)
            wub = sb(pu, "wub", [128, 8, 256], BF16)
            pU = [ps(pu, "pUu%d" % i, [128, 512], F32) for i in range(2)]
            cx.dma("sp", wuf[:], win_d.rearrange("(c p) n -> p c n", p=128)[:, :, 2304:2560], writes=["wuf"])
            cx.op("pool", lambda e: e.tensor_copy(out=wub[:], in_=wuf[:]), reads=["wuf"], writes=["wub"])
            for t in range(NT):
                b = t % 2
                cx.op("pe", [(lambda e, c=c: e.matmul(pU[b][:, 0:256], lhsT=hT[:, c, t * 128:(t + 1) * 128], rhs=wub[:, c, :],
                                                      start=(c == 0), stop=(c == 7))) for c in range(8)],
                      reads=[("hT", t // 4), "wub"], writes=["pUu%d" % b])
                cx.op("dve" if b else "act",
                      (lambda e: e.tensor_copy(out=usb[:, t, :], in_=pU[b][:, 0:256])) if b else
                      (lambda e: e.copy(out=usb[:, t, :], in_=pU[b][:, 0:256])),
                      reads=["pUu%d" % b], writes=[("usb", t)])
        cx.barrier()
        if _STOP <= 2:
            return nc
        with ExitStack() as pd:
            ABT = sb(pd, "ABT", [128, 2, 2, S], BF16)
            tabb = [sb(pd, "tabb%d" % i, [128, 32, 512], BF16) for i in range(2)]
            c64 = sb(pd, "c64", [128, 128], BF16)
            s64 = sb(pd, "s64", [128, 128], BF16)
            fwf = sb(pd, "fwf", [128, 2, 128], F32)
            fwb = sb(pd, "fwb", [128, 2, 128], BF16)
            M12 = sb(pd, "M12", [128, 2, 2, 128], BF16)
            fbf = sb(pd, "fbf", [1, 256], F32)
            fbb = sb(pd, "fbb", [1, 256], BF16)
            gfb = sb(pd, "gfb", [128, 256], F32)
            fjunk = sb(pd, "fjunk", [128, 256], F32)
            fssq = [sb(pd, "fssq%d" % i, [128, 1], F32) for i in range(2)]
            frstd = [sb(pd, "frstd%d" % i, [128, 1], F32) for i in range(2)]
            fnb = [sb(pd, "fnb%d" % i, [128, 256], BF16) for i in range(2)]
            fourT = sb(pd, "fourT", [128, 2, S], BF16)
            pU = [ps(pd, "pU%d" % i, [128, 512], F32) for i in range(2)]
            pF = [ps(pd, "pF%d" % i, [128, 512], F32) for i in range(2)]
            pM = ps(pd, "pM", [128, 512], F32)
            pFT = [ps(pd, "pFT%d" % i, [128, 2, 128], BF16) for i in range(2)]

            cx.dma("sp", c64[:], c64_d[:, :], writes=["c64"])
            cx.dma("sp", s64[:], s64_d[:, :], writes=["s64"])
            cx.dma("sp", fwf[:], fwb_d[:, :, :], writes=["fwf"])
            cx.dma("sp", fbf[:], fb_d[:, :], writes=["fbf"])
            cx.dma("sp", gfb[:], gf_d[:, :], writes=["gfb"])
            cx.op("pool", lambda e: e.tensor_copy(out=fwb[:], in_=fwf[:]), reads=["fwf"], writes=["fwb"])
            cx.op("pool", lambda e: e.tensor_copy(out=fbb[:], in_=fbf[:]), reads=["fbf"], writes=["fbb"])
            cx.op("pe", [(lambda e, cs=cs, cc=cc: e.matmul(pM[:, (cs * 2 + cc) * 128:(cs * 2 + cc + 1) * 128],
                                                           lhsT=(c64 if cs == 0 else s64)[:], rhs=fwb[:, cc, :],
                                                           start=True, stop=True)) for cs in range(2) for cc in range(2)],
                  reads=["c64", "s64", "fwb"], writes=["pM"])
            cx.op("act", lambda e: e.copy(out=M12[:].rearrange("p a b e -> p (a b e)"), in_=pM[:]),
                  reads=["pM"], writes=["M12"])
            it = 0
            for sbk in range(8):
                for cs, tsrc in ((0, tabc_d), (1, tabs_d)):
                    tb_ = it % 2
                    it += 1
                    cx.dma("sp", tabb[tb_][:], tsrc[sbk, :, :, :], writes=["tabb%d" % tb_])
                    for cc in range(2):
                        b = cc
                        cx.op("pe", [(lambda e, k=k: e.matmul(pF[b][:], lhsT=usb[:, k, cc * 128:(cc + 1) * 128],
                                                              rhs=tabb[tb_][:, k, :], start=(k == 0), stop=(k == 31)))
                                     for k in range(32)],
                              reads=[("usb", k_) for k_ in range(NT)] + ["tabb%d" % tb_], writes=["pF%d" % b])
                        if cc == 0:
                            cx.op("act", lambda e: e.copy(out=ABT[:, cs, cc, sbk * 512:(sbk + 1) * 512], in_=pF[b][:]),
                                  reads=["pF%d" % b], writes=["ABT"])
                        else:
                            cx.op("dve", lambda e: e.tensor_copy(out=ABT[:, cs, cc, sbk * 512:(sbk + 1) * 512], in_=pF[b][:]),
                                  reads=["pF%d" % b], writes=["ABT"])
            def f_s1(t):
                b = t % 2
                ts_ = slice(t * 128, (t + 1) * 128)
                mm = []
                for cc in range(2):
                    o_ap = pU[b][:, cc * 128:(cc + 1) * 128]
                    mm.append(lambda e, cc=cc, o_ap=o_ap: e.matmul(o_ap, lhsT=ABT[:, 0, cc, ts_], rhs=M12[:, 0, cc, :], start=True, stop=False))
                    mm.append(lambda e, cc=cc, o_ap=o_ap: e.matmul(o_ap, lhsT=ABT[:, 1, cc, ts_], rhs=M12[:, 1, cc, :], start=False, stop=False))
                    mm.append(lambda e, cc=cc, o_ap=o_ap: e.matmul(o_ap, lhsT=ones_b[0:1, :], rhs=fbb[0:1, cc * 128:(cc + 1) * 128], start=False, stop=True))
                cx.op("pe", mm, reads=["ABT", "M12", "ones_b", "fbb"], writes=["pU%d" % b])
                rms_rstd((fjunk[:], fssq[b][:], frstd[b][:], "fjunk", "fssq%d" % b, "frstd%d" % b),
                         pU[b][:, 0:256], "pU%d" % b, 256, "F")
                cx.op("dve", lambda e: e.scalar_tensor_tensor(out=fnb[b][:], in0=pU[b][:, 0:256], scalar=frstd[b][:, 0:1],
                                                              in1=gfb[:], op0=ALU.mult, op1=ALU.mult),
                      reads=["pU%d" % b, "frstd%d" % b, "gfb"], writes=["fnb%d" % b])

            def f_s2(t):
                b = t % 2
                ts_ = slice(t * 128, (t + 1) * 128)
                cx.op("pe", [(lambda e, cc=cc: e.transpose(out=pFT[b][:, cc, :], in_=fnb[b][:, cc * 128:(cc + 1) * 128],
                                                           identity=ident[:])) for cc in range(2)],
                      reads=["fnb%d" % b, "ident"], writes=["pFT%d" % b])
                cx.op("act", lambda e: e.copy(out=fourT[:, :, ts_], in_=pFT[b][:]),
                      reads=["pFT%d" % b], writes=["fourT"])

            for t in range(NT):
                f_s1(t)
                if t >= 1:
                    f_s2(t - 1)
            f_s2(NT - 1)
            for cc in range(2):
                cx.dma("sp", mix_d[768 + cc * 128:768 + (cc + 1) * 128, :], fourT[:, cc, :], reads=["fourT"],
                       writes=[("mix_d", 6 + cc)])

        cx.barrier()
        if _STOP <= 3:
            return nc
        sD.close()
        with ExitStack() as pb:
            EB = sb(pb, "EB", [128, 6, 3, 512], BF16)
            if _STOP <= 4:
                return nc
            wfr = sb(pb, "wfr", [128, 1024], F32)
            wf = [wfr[:].rearrange("p (c n) -> p c n", c=8)]
            wqkv = sb(pb, "wqkv", [128, 3, 8, 128], BF16)
            QT = sb(pb, "QT", [128, 2, S], BF16)
            VT = sb(pb, "VT", [128, S], BF16)
            KPb = [sb(pb, "KP%d" % d, [128, d * (S // d + 128)], BF16) for d in DILS]
            Vs = sb(pb, "Vs", [128, 48, 192], BF16)
            acc = [sb(pb, "acc%d" % i, [128, S], F32) for i in range(2)]
            attn = [sb(pb, "attn%d" % i, [128, 512], BF16) for i in range(2)]
            Eb = [sb(pb, "Eb%d" % i, [128, 512], BF16) for i in range(2)]
            PTb = [sb(pb, "PTb%d" % i, [128, 512], BF16) for i in range(4)]
            pVt2 = [ps(pb, "pVt%d" % i, [128, 8, 128], BF16) for i in range(2)]
            vbi = 0
            pS = [ps(pb, "pS%d" % i, [128, 512], F32) for i in range(2)]
            pI = pS
            pO = [ps(pb, "pO%d" % i, [128, 512], F32) for i in range(4)]

            cx.op("pool", lambda e: e.memset(Vs[:], 1.0), writes=["VsL", "VsU"])
            for i_ in range(3):
                cx.op("pool", lambda e: e.memset(KPb[i_][:], 0.0), writes=[("KP", i_)])
            cx.op("pool", lambda e: e.memset(QT[:], 0.0), writes=[("QT", i) for i in range(8)])
            win_v = win_d.rearrange("(c p) n -> p c n", p=128)
            def load_weights(j):
                for wi, col0 in enumerate((j * 128, 768 + j * 128, 1536 + j * 128)):
                    cx.dma("sp", wf[0], win_v[:, :, col0:col0 + 128], writes=["wfr"])
                    cx.op("pool", lambda e: e.tensor_copy(out=wqkv[:, wi, :, :], in_=wf[0]),
                          reads=["wfr"], writes=[("wqkv", wi)])

            def epilogue_piece(j, cb_):
                cs = slice(cb_ * 512, (cb_ + 1) * 512)
                rden = wfr[:, (cb_ % 2) * 512:(cb_ % 2) * 512 + 512]
                rk = "wfr"
                cx.op("dve", [lambda e: e.tensor_copy(out=rden[0:64, :], in_=acc[0][64:128, cs]),
                              lambda e: e.tensor_copy(out=rden[64:128, :], in_=acc[1][0:64, cs])],
                      reads=["acc0", "acc1"], writes=[rk])
                cx.op("act", lambda e: e.activation(out=rden, in_=rden, func=AF.Ln), reads=[rk], writes=[rk])
                cx.op("act", lambda e: e.activation(out=rden, in_=rden, func=AF.Exp, scale=-1.0), reads=[rk], writes=[rk])
                ab_ = cb_ % 2
                cx.op("dve", [lambda e: e.tensor_tensor(out=attn[ab_][0:64, :], in0=acc[0][0:64, cs], in1=rden[0:64, :], op=ALU.mult),
                              lambda e: e.tensor_tensor(out=attn[ab_][64:128, :], in0=acc[1][64:128, cs], in1=rden[64:128, :], op=ALU.mult)],
                      reads=["acc0", "acc1", rk], writes=["attn%d" % ab_])
                cx.dma("sp", mix_d[j * 128:(j + 1) * 128, cs], attn[ab_][:], reads=["attn%d" % ab_], writes=[("mix_d", j, cb_)])

            load_weights(0)
            for j in range(6):
                if j == 0:
                    for jp in range(6):
                        for d_ in range(3):
                            for (r0, off) in ((0, 63), (64, 127)):
                                for h_ in range(2):
                                    base = ((2 * jp + h_) * 3 + d_) * REP * P_G
                                    src = bass.AP(gr_d.tensor, base + off, [[P_G - 1, 64], [128, 2], [1, 128]])
                                    dst = EB[r0:r0 + 64, jp, d_, :].rearrange("p (k h c) -> p k h c", k=2, h=2)[:, :, h_, :]
                                    q_ = "sp"
                                    cx.dma(q_, dst, src, reads=["gr_d"], writes=[("EBp", jp, d_, r0, h_)])
                it = 0
                pend_epi = [(j - 1, c_) for c_ in range(8)] if j > 0 else []
                QTk = [("QT", i) for i in range(8)] + [("QTb", i) for i in range(8)]
                KTk = [("KT", i) for i in range(8)]
                VTk = [("VT", i) for i in range(8)]
                for wi, dst, key0 in ((0, QT, "QT"), (1, None, "KT"), (2, VT, "VT")):
                    for tb in range(8):
                        b = it % 2
                        it += 1
                        cx.op("pe", [(lambda e, c=c: e.matmul(pI[b][:], lhsT=wqkv[:, wi, c, :],
                                                              rhs=hT[:, c, tb * 512:(tb + 1) * 512],
                                                              start=(c == 0), stop=(c == 7))) for c in range(8)],
                              reads=[("wqkv", wi), ("hT", tb)], writes=["pS%d" % b])
                        key = (key0, tb)
                        tsl = slice(tb * 512, (tb + 1) * 512)
                        if wi == 0:
                            cx.op("act", lambda e: e.copy(out=QT[0:64, 0, tsl], in_=pI[b][0:64, :]),
                                  reads=["pS%d" % b], writes=[key])
                            cx.op("dve", lambda e: e.tensor_copy(out=QT[64:128, 1, tsl], in_=pI[b][64:128, :]),
                                  reads=["pS%d" % b], writes=[(key0 + "b", tb)])
                        elif wi == 2:
                            if tb % 2:
                                cx.op("act", lambda e: e.copy(out=VT[:, tsl], in_=pI[b][:]), reads=["pS%d" % b], writes=[key])
                            else:
                                cx.op("dve", lambda e: e.tensor_copy(out=VT[:, tsl], in_=pI[b][:]), reads=["pS%d" % b], writes=[key])
                        else:
                            src1 = pI[b][:].rearrange("p (m x) -> p m x", x=128)
                            k1 = KPb[0]
                            d0 = k1[:, tb * 512:tb * 512 + 512].rearrange("p (m x) -> p m x", x=128)
                            d1 = k1[:, tb * 512 + 128:tb * 512 + 640].rearrange("p (m x) -> p m x", x=128)
                            src4 = pI[b][:].rearrange("p (l r) -> p r l", r=4)
                            k4 = KPb[1][:].rearrange("p (r l) -> p r l", r=4)
                            src16 = pI[b][:].rearrange("p (l r) -> p r l", r=16)
                            k16 = KPb[2][:].rearrange("p (r l) -> p r l", r=16)
                            pos16 = 32 * tb + (128 if ((tb // 2) % 2) else 0)
                            for eng, pr in (("act", slice(0, 64)), ("dve", slice(64, 128))):
                                cp = (lambda e, o, i_: e.copy(out=o, in_=i_)) if eng == "act" else (lambda e, o, i_: e.tensor_copy(out=o, in_=i_))
                                cx.op(eng, [lambda e: cp(e, d0[pr, :, 0:64], src1[pr, :, 0:64]),
                                            lambda e: cp(e, d1[pr, :, 64:128], src1[pr, :, 64:128]),
                                            lambda e: cp(e, k4[pr, :, 128 * tb:128 * tb + 64], src4[pr, :, 0:64]),
                                            lambda e: cp(e, k4[pr, :, 128 * tb + 192:128 * tb + 256], src4[pr, :, 64:128]),
                                            lambda e: cp(e, k16[pr, :, pos16:pos16 + 32], src16[pr, :, :])],
                                      reads=["pS%d" % b], writes=[("KPa" if eng == "act" else "KPd", tb)])
                        if pend_epi and it % 3 == 0:
                            epilogue_piece(*pend_epi.pop(0))
                while pend_epi:
                    epilogue_piece(*pend_epi.pop(0))
                if j + 1 < 6:
                    load_weights(j + 1)
                KPk = [("KPa", i) for i in range(8)] + [("KPd", i) for i in range(8)] + [("KP", i) for i in range(3)]
                for h in range(2):
                    pass

                oi = 0
                si = 0
                for di, d in enumerate(DILS):
                    L = S // d
                    Lb = L + 128
                    nqb = L // 128
                    nsl = nqb + 1
                    KPv = KPb[di][:].rearrange("p (r l) -> p r l", r=d)
                    VTv = VT[:].rearrange("p (l r) -> p r l", r=d)
                    Vsv = Vs[:, 0:d * nsl, :].rearrange("p (r s) c -> p r s c", r=d)
                    for r in range(d):
                        for t0 in range(0, nqb, 8):
                            nt_ = min(8, nqb - t0)
                            vb = vbi % 2
                            vbi += 1
                            pVt = pVt2[vb]
                            cx.op("pe", [(lambda e, i=i: e.transpose(out=pVt[:, i, :],
                                                                     in_=VTv[:, r, (t0 + i) * 128:(t0 + i + 1) * 128],
                                                                     identity=ident[:])) for i in range(nt_)],
                                  reads=VTk + ["ident"], writes=["pVt%d" % vb])
                            cx.op("dve", [lambda e: e.tensor_copy(out=Vsv[0:64, r, t0:t0 + nt_, 0:64], in_=pVt[0:64, 0:nt_, 0:64]),
                                          lambda e: e.tensor_copy(out=Vsv[0:64, r, t0:t0 + nt_, 128:192], in_=pVt[0:64, 0:nt_, 64:128])],
                                  reads=["pVt%d" % vb], writes=["VsL"])
                            cx.op("act", [lambda e: e.copy(out=Vsv[64:128, r, t0 + 1:t0 + 1 + nt_, 0:64], in_=pVt[64:128, 0:nt_, 0:64]),
                                          lambda e: e.copy(out=Vsv[64:128, r, t0 + 1:t0 + 1 + nt_, 128:192], in_=pVt[64:128, 0:nt_, 64:128])],
                                  reads=["pVt%d" % vb], writes=["VsU"])
                    QTv = QT[:].rearrange("p a (l r) -> p a r l", r=d)
                    accv = [acc[h][:].rearrange("p (l r) -> p r l", r=d) for h in range(2)]
                    vcols = (slice(0, 128), slice(64, 192))
                    groups = []
                    if d == 16:
                        for r0 in range(0, 16, 2):
                            groups.append([(r0, 0), (r0, 1), (r0 + 1, 0), (r0 + 1, 1)])
                    else:
                        for r in range(d):
                            for g0 in range(0, nqb, 4):
                                groups.append([(r, g0 + i) for i in range(4)])
                    pend = []

                    def emit_pv(item):
                        (r, qb, sidx, ob, slot, grp) = item
                        pt = PTb[sidx]
                        lo_ok = qb > 0
                        hi_ok = qb < nqb - 1
                        for h in range(2):
                            vcol = vcols[h]
                            pob = pO[h * 2 + ob]
                            o_ap = pob[:, slot * 128:(slot + 1) * 128]
                            cb2 = slice(h * 128, h * 128 + 128)
                            ca = slice(256 + h * 128, 256 + h * 128 + 128)
                            mm = []
                            if lo_ok:
                                mm.append(lambda e: e.matmul(o_ap, lhsT=Vsv[:, r, qb, vcol], rhs=pt[:, ca], start=(slot == 0), stop=False, skip_group_check=True))
                            else:
                                mm.append(lambda e: e.matmul(o_ap, lhsT=Vsv[0:64, r, qb, vcol], rhs=pt[0:64, ca], start=(slot == 0), stop=False, skip_group_check=True))
                            if hi_ok:
                                mm.append(lambda e: e.matmul(o_ap, lhsT=Vsv[:, r, qb + 1, vcol], rhs=pt[:, cb2], start=False, stop=True, skip_group_check=True))
                            else:
                                mm.append(lambda e: e.matmul(o_ap, lhsT=Vsv[64:128, r, qb + 1, vcol], rhs=pt[64:128, cb2], start=False, stop=True, skip_group_check=True))
                            cx.op("pe", mm, reads=["VsL", "VsU", "PTb%d" % sidx], writes=[("pO", h * 2 + ob, slot)])
                            if slot == 3:
                                if d == 16:
                                    r0 = grp[0][0]
                                    dst = accv[h][:, r0:r0 + 2, :]
                                    srcp = pob[:].rearrange("p (a l) -> p a l", a=2)
                                else:
                                    r_, q0 = grp[0]
                                    dst = accv[h][:, r_, q0 * 128:q0 * 128 + 512]
                                    srcp = pob[:]
                                if di == 0:
                                    cx.op("act", lambda e: e.copy(out=dst, in_=srcp),
                                          reads=[("pO", h * 2 + ob, s_) for s_ in range(4)], writes=["acc%d" % h])
                                else:
                                    cx.op("dve", lambda e: e.tensor_tensor(out=dst, in0=srcp, in1=dst, op=ALU.add),
                                          reads=[("pO", h * 2 + ob, s_) for s_ in range(4)] + ["acc%d" % h], writes=["acc%d" % h])

                    for grp in groups:
                        ob = oi % 2
                        oi += 1
                        for slot, (r, qb) in enumerate(grp):
                            sidx = si % 4
                            ebi = si % 2
                            psi = si % 2
                            si += 1
                            q_ap = QTv[:, :, r, qb * 128:(qb + 1) * 128]
                            kA = KPv[:, r, qb * 128:(qb + 1) * 128]
                            kB = KPv[:, r, (qb + 1) * 128:(qb + 2) * 128]
                            oB = pS[psi][:, 0:256].rearrange("p (a q) -> p a q", a=2)
                            oA = pS[psi][:, 256:512].rearrange("p (a q) -> p a q", a=2)
                            cx.op("pe", [lambda e: e.matmul(oA, lhsT=kA, rhs=q_ap, start=True, stop=True),
                                         lambda e: e.matmul(oB, lhsT=kB, rhs=q_ap, start=True, stop=True)],
                                  reads=QTk + KPk, writes=["pS%d" % psi])
                            cx.op("act", lambda e: e.activation(out=Eb[ebi][:], in_=pS[psi][:], func=AF.Exp, scale=0.125),
                                  reads=["pS%d" % psi], writes=["Eb%d" % ebi])
                            cx.op("dve", lambda e: e.tensor_tensor(out=PTb[sidx][:], in0=Eb[ebi][:], in1=EB[:, j, di, :], op=ALU.mult),
                                  reads=["Eb%d" % ebi] + [("EBp", j, di, r0_, h_) for r0_ in (0, 64) for h_ in range(2)], writes=["PTb%d" % sidx])
                            pend.append((r, qb, sidx, ob, slot, grp))
                            if len(pend) > 2:
                                emit_pv(pend.pop(0))
                    while pend:
                        emit_pv(pend.pop(0))

            for cb_ in range(8):
                epilogue_piece(5, cb_)
        cx.barrier()
        if _STOP <= 5:
            return nc
        s2.close()
        sW = es.enter_context(ExitStack())
        wgb = sb(sW, "wgb", [128, 8, DFF], BF16)
        wvb = sb(sW, "wvb", [128, 8, DFF], BF16)
        wdb = sb(sW, "wdb", [128, NF, D], BF16)
        wst = [sb(sW, "wst%d" % i, [128, 704], F32) for i in range(3)]
        wjobs = []
        for (src_d, dstw, key) in ((wg_d, wgb, "wgb"), (wv_d, wvb, "wvb")):
            for c in range(8):
                for q4 in range(4):
                    cols = slice(q4 * 704, (q4 + 1) * 704)
                    wjobs.append((src_d[c * 128:(c + 1) * 128, cols], dstw[:, c, cols], 704, key))
        for f in range(NF):
            wjobs.append((wd_d[f * 128:(f + 1) * 128, 0:704], wdb[:, f, 0:704], 704, "wdb"))
            wjobs.append((wd_d[f * 128:(f + 1) * 128, 704:D], wdb[:, f, 704:D], D - 704, "wdb"))
        winfl = [None, None, None]
        wstate = {"i": 0}

        def emit_wslot(b):
            if winfl[b] is not None:
                dst, wid, key = winfl[b]
                i = wstate["i"]
                wstate["i"] += 1
                if i % 2:
                    cx.op("act", lambda e: e.copy(out=dst, in_=wst[b][:, 0:wid]), reads=["wst%d" % b], writes=[key])
                else:
                    cx.op("dve", lambda e: e.tensor_copy(out=dst, in_=wst[b][:, 0:wid]), reads=["wst%d" % b], writes=[key])
                winfl[b] = None
            if wjobs:
                src, dst, wid, key = wjobs.pop(0)
                cx.dma("sp", wst[b][:, 0:wid], src, writes=["wst%d" % b])
                winfl[b] = (dst, wid, key)

        with ExitStack() as pe2:
            wob = sb(pe2, "wob", [128, 8, D], BF16)
            gat = sb(pe2, "gat", [128, 6], F32)
            g2b = sb(pe2, "g2b", [128, D], F32)
            mixb = [sb(pe2, "mixb%d" % i, [128, 8, 512], BF16) for i in range(2)]
            sqb = [sb(pe2, "sqb%d" % i, [128, 512], BF16) for i in range(2)]
            rsa = sb(pe2, "rsa", [128, 512], F32)
            x1 = [sb(pe2, "x1_%d" % i, [128, D], F32) for i in range(3)]
            ssqE = [sb(pe2, "ssqE%d" % i, [128, 1], F32) for i in range(2)]
            rstdE = [sb(pe2, "rstdE%d" % i, [128, 1], F32) for i in range(2)]
            h2b = [sb(pe2, "h2b%d" % i, [128, D], BF16) for i in range(3)]
            h2T = [sb(pe2, "h2T%d" % i, [128, 8, 128], BF16) for i in range(2)]
            zcol = sb(pe2, "zcol", [128, 8, 1], BF16)
            pR = ps(pe2, "pR", [128, 512], F32)
            pY = [ps(pe2, "pY%d" % i, [128, D], F32) for i in range(2)]
            pT2 = [ps(pe2, "pT2%d" % i, [128, 8, 128], BF16) for i in range(2)]

            cx.dma("sp", gat[:], ga_d[:, :], writes=["gat"])
            cx.dma("sp", g2b[:], g2_d[:, :], writes=["g2b"])
            for c in range(8):
                b = c % 2
                for (c0_, c1_) in ((0, 704), (704, D)):
                    bb = (2 * c + (c0_ > 0)) % 3
                    cx.dma("sp", wst[bb][:, 0:c1_ - c0_], wout_d[c * 128:(c + 1) * 128, c0_:c1_], writes=["wst%d" % bb])
                    if c % 2:
                        cx.op("act", lambda e: e.copy(out=wob[:, c, c0_:c1_], in_=wst[bb][:, 0:c1_ - c0_]), reads=["wst%d" % bb], writes=["wob"])
                    else:
                        cx.op("dve", lambda e: e.tensor_copy(out=wob[:, c, c0_:c1_], in_=wst[bb][:, 0:c1_ - c0_]), reads=["wst%d" % bb], writes=["wob"])
            cx.op("pool", lambda e: e.memset(zcol[:], 0.0), writes=["zcol"])
            h2v = h2_d.rearrange("(c p) s -> p c s", p=128)
            cx.dma("act", h2v[:, :, 0:1], zcol[:], reads=["zcol"], writes=["h2halo"], allow_slow_non_contiguous=True)
            cx.dma("act", h2v[:, :, S + 1:S + 2], zcol[:], reads=["zcol"], writes=["h2halo"], allow_slow_non_contiguous=True)
            mixv = mix_d.rearrange("(c p) s -> p c s", p=128)
            def e_load(blk):
                mb = blk % 2
                bs = slice(blk * 512, (blk + 1) * 512)
                cx.dma("sp", mixb[mb][:], mixv[:, :, bs], reads=[("mix_d", i, blk) for i in range(6)] + [("mix_d", 6), ("mix_d", 7)], writes=["mixb%d" % mb])

            def e_prologue(blk):
                mb = blk % 2
                for jj in range(6):
                    q = jj % 2
                    cx.op("act", lambda e: e.activation(out=sqb[q][:], in_=mixb[mb][:, jj, :], func=AF.Square),
                          reads=["mixb%d" % mb], writes=["sqb%d" % q])
                    cx.op("pe", lambda e: e.matmul(pR[:], lhsT=ones_b[:], rhs=sqb[q][:], start=(jj == 0), stop=(jj == 5)),
                          reads=["ones_b", "sqb%d" % q], writes=["pR"])
                cx.op("act", lambda e: e.activation(out=rsa[:], in_=pR[:], func=AF.Sqrt, scale=1.0 / 768, bias=epsb[:, 0:1]),
                      reads=["pR", "epsb"], writes=["rsa"])
                cx.op("dve", lambda e: e.reciprocal(out=rsa[:], in_=rsa[:]), reads=["rsa"], writes=["rsa"])
                for jj in range(6):
                    cx.op("dve", lambda e: e.scalar_tensor_tensor(out=mixb[mb][:, jj, :], in0=mixb[mb][:, jj, :],
                                                                  scalar=gat[:, jj:jj + 1], in1=rsa[:],
                                                                  op0=ALU.mult, op1=ALU.mult),
                          reads=["mixb%d" % mb, "gat", "rsa"], writes=["mixb%d" % mb])

            def e_s1(t):
                blk, tt = t // 4, t % 4
                mb = blk % 2
                b = t % 2
                b3 = t % 3
                xk = "x1_%d" % b3
                mm = []
                for hf in range(2):
                    for c in range(8):
                        mm.append(lambda e, hf=hf, c=c: e.matmul(pY[b][:, hf * 512:(hf + 1) * 512],
                                                                 lhsT=mixb[mb][:, c, tt * 128:(tt + 1) * 128],
                                                                 rhs=wob[:, c, hf * 512:(hf + 1) * 512],
                                                                 start=(c == 0), stop=(c == 7)))
                cx.op("pe", mm, reads=["mixb%d" % mb, "wob"], writes=["pY%d" % b])
                cx.op("dve", lambda e: e.tensor_tensor(out=x1[b3][:], in0=pY[b][:], in1=x1[b3][:], op=ALU.add),
                      reads=["pY%d" % b, xk], writes=[xk])
                cx.dma("act", x1_d[t * 128:(t + 1) * 128, :], x1[b3][:], reads=[xk], writes=[("x1_d", t)])
                rms_rstd((h2b[b3][:], ssqE[b][:], rstdE[b][:], "h2b%d" % b3, "ssqE%d" % b, "rstdE%d" % b),
                         x1[b3][:], xk, D, "E")
                cx.op("dve", lambda e: e.scalar_tensor_tensor(out=h2b[b3][:], in0=x1[b3][:], scalar=rstdE[b][:, 0:1],
                                                              in1=g2b[:], op0=ALU.mult, op1=ALU.mult),
                      reads=[xk, "rstdE%d" % b, "g2b"], writes=["h2b%d" % b3])
                for b_w in range(3):
                    emit_wslot(b_w)

            def e_s2(t):
                b = t % 2
                b3 = t % 3
                cx.op("pe", [(lambda e, c=c: e.transpose(out=pT2[b][:, c, :], in_=h2b[b3][:, c * 128:(c + 1) * 128],
                                                         identity=ident[:])) for c in range(8)],
                      reads=["h2b%d" % b3, "ident"], writes=["pT2%d" % b])
                cx.op("act", lambda e: e.copy(out=h2T[b][:], in_=pT2[b][:]), reads=["pT2%d" % b], writes=["h2T%d" % b])
                cx.dma("act", h2v[:, :, 1 + t * 128:1 + (t + 1) * 128], h2T[b][:], reads=["h2T%d" % b],
                       writes=[("h2_d", t)])

            def load_x(t):
                cx.dma("sp", x1[t % 3][:], x_d[t * 128:(t + 1) * 128, :], writes=["x1_%d" % (t % 3)])

            e_load(0)
            e_prologue(0)
            load_x(0)
            for t in range(NT):
                if t % 4 == 0 and t // 4 + 1 < 8:
                    e_load(t // 4 + 1)
                if t + 1 < NT:
                    load_x(t + 1)
                e_s1(t)
                if t % 4 == 3 and t // 4 + 1 < 8:
                    e_prologue(t // 4 + 1)
                if t >= 2:
                    e_s2(t - 2)
            e_s2(NT - 2)
            e_s2(NT - 1)
            while wjobs or any(w is not None for w in winfl):
                for b_w in range(3):
                    emit_wslot(b_w)
        cx.barrier()
        if _STOP <= 6:
            return nc
        with ExitStack() as pf:
            cwt = sb(pf, "cwt", [128, NF, 3], F32)
            cbt = sb(pf, "cbt", [128, NF], F32)
            gFb = sb(pf, "gFb", [128, D], F32)
            h2s = [sb(pf, "h2s%d" % i, [128, 8, 258], BF16) for i in range(2)]
            t1 = [sb(pf, "t1_%d" % i, [128, 256], F32) for i in range(2)]
            t2 = [sb(pf, "t2_%d" % i, [128, 256], F32) for i in range(2)]
            sg = [sb(pf, "sg_%d" % i, [128, 256], F32) for i in range(2)]
            aT = [sb(pf, "aT_%d" % i, [128, 256], BF16) for i in range(4)]
            x1r = [sb(pf, "x1r%d" % i, [128, D], F32) for i in range(2)]
            of_ = [sb(pf, "of%d" % i, [128, D], F32) for i in range(2)]
            junkF = sb(pf, "junkF", [128, D], F32)
            ssqF = [sb(pf, "ssqF%d" % i, [128, 1], F32) for i in range(2)]
            rstdF = [sb(pf, "rstdF%d" % i, [128, 1], F32) for i in range(2)]
            yo = [sb(pf, "yo%d" % i, [128, D], F32) for i in range(2)]
            pGt = [ps(pf, "pGt%d" % i, [128, 512], F32) for i in range(2)]
            pVl = [ps(pf, "pVl%d" % i, [128, 512], F32) for i in range(2)]
            pD = [ps(pf, "pD%d" % i, [128, D], F32) for i in range(2)]

            cx.dma("sp", cwt[:], cw_d[:, :, :], writes=["cwt"])
            cx.dma("sp", cbt[:], cb_d[:, :], writes=["cbt"])
            cx.dma("sp", gFb[:], gF_d[:, :], writes=["gFb"])
            h2v = h2_d.rearrange("(c p) s -> p c s", p=128)
            _SUB = int(os.environ.get("KSUB", "9"))
            def load_h2s(st_):
                cx.dma("sp", h2s[st_ % 2][:], h2v[:, :, st_ * 256:st_ * 256 + 258],
                       reads=[("h2_d", t) for t in range(max(0, 2 * st_ - 1), min(NT, 2 * st_ + 3))] + ["h2halo"],
                       writes=["h2s%d" % (st_ % 2)])

            load_h2s(0)
            for st in range(16 if _SUB > 1 else 0):
                hb_ = st % 2
                for tt in range(2):
                    t = st * 2 + tt
                    cx.dma("sp", x1r[t % 2][:], x1_d[t * 128:(t + 1) * 128, :], reads=[("x1_d", t)], writes=["x1r%d" % (t % 2)])
                if st + 1 < 16:
                    load_h2s(st + 1)
                pend = []

                def emit_down(item):
                    f, ab = item
                    mm = []
                    for tt in range(2):
                        for hf in range(2):
                            mm.append(lambda e, tt=tt, hf=hf: e.matmul(pD[tt][:, hf * 512:(hf + 1) * 512],
                                                                       lhsT=aT[ab][:, tt * 128:(tt + 1) * 128],
                                                                       rhs=wdb[:, f, hf * 512:(hf + 1) * 512],
                                                                       start=(f == 0), stop=(f == NF - 1)))
                    cx.op("pe", mm, reads=["aT_%d" % ab, "wdb"], writes=(["pD0", "pD1"] if f in (0, NF - 1) else []))

                for f in range(NF):
                    b = f % 2
                    ab = f % 4
                    fs = slice(f * 128, (f + 1) * 128)
                    cx.op("pe", [(lambda e, c=c: e.matmul(pGt[b][:, 0:258], lhsT=wgb[:, c, fs], rhs=h2s[hb_][:, c, 0:258],
                                                          start=(c == 0), stop=(c == 7))) for c in range(8)],
                          reads=["wgb", "h2s%d" % hb_], writes=["pGt%d" % b])
                    cx.op("pe", [(lambda e, c=c: e.matmul(pVl[b][:, 0:256], lhsT=wvb[:, c, fs], rhs=h2s[hb_][:, c, 1:257],
                                                          start=(c == 0), stop=(c == 7))) for c in range(8)],
                          reads=["wvb", "h2s%d" % hb_], writes=["pVl%d" % b])
                    cx.op("act", lambda e: e.activation(out=t1[b][:], in_=pGt[b][:, 1:257], func=AF.Identity,
                                                        scale=cwt[:, f, 1:2], bias=cbt[:, f:f + 1]),
                          reads=["pGt%d" % b, "cwt", "cbt"], writes=["t1_%d" % b])
                    cx.op("dve", lambda e: e.scalar_tensor_tensor(out=t2[b][:], in0=pGt[b][:, 0:256], scalar=cwt[:, f, 0:1],
                                                                  in1=t1[b][:], op0=ALU.mult, op1=ALU.add),
                          reads=["pGt%d" % b, "cwt", "t1_%d" % b], writes=["t2_%d" % b])
                    cx.op("dve", lambda e: e.scalar_tensor_tensor(out=t1[b][:], in0=pGt[b][:, 2:258], scalar=cwt[:, f, 2:3],
                                                                  in1=t2[b][:], op0=ALU.mult, op1=ALU.add),
                          reads=["pGt%d" % b, "cwt", "t2_%d" % b], writes=["t1_%d" % b])
                    cx.op("act", lambda e: e.activation(out=sg[b][:], in_=t1[b][:], func=AF.Silu),
                          reads=["t1_%d" % b], writes=["sg_%d" % b])
                    cx.op("dve", lambda e: e.tensor_tensor(out=aT[ab][:], in0=pVl[b][:, 0:256], in1=sg[b][:], op=ALU.mult),
                          reads=["pVl%d" % b, "sg_%d" % b], writes=["aT_%d" % ab])
                    if _SUB > 2:
                        pend.append((f, ab))
                    if len(pend) > 2:
                        emit_down(pend.pop(0))
                while pend:
                    emit_down(pend.pop(0))
                for tt in range(2 if _SUB > 3 else 0):
                    t = st * 2 + tt
                    b = t % 2
                    cx.op("act", lambda e: e.copy(out=of_[b][:], in_=pD[tt][:]), reads=["pD%d" % tt], writes=["of%d" % b])
                    cx.op("dve", lambda e: e.tensor_tensor(out=of_[b][:], in0=of_[b][:], in1=x1r[b][:], op=ALU.add),
                          reads=["of%d" % b, "x1r%d" % b], writes=["of%d" % b])
                    rms_rstd((junkF[:], ssqF[b][:], rstdF[b][:], "junkF", "ssqF%d" % b, "rstdF%d" % b),
                             of_[b][:], "of%d" % b, D, "F")
                    cx.op("dve", lambda e: e.scalar_tensor_tensor(out=yo[b][:], in0=of_[b][:], scalar=rstdF[b][:, 0:1],
                                                                  in1=gFb[:], op0=ALU.mult, op1=ALU.mult),
                          reads=["of%d" % b, "rstdF%d" % b, "gFb"], writes=["yo%d" % b])
                    cx.dma(os.environ.get("KYQ", "sp"), y_d[t * 128:(t + 1) * 128, :], yo[b][:], reads=["yo%d" % b], writes=[("y", t)])
            cx.finish("sp", [("y", t) for t in range(NT)])
            cx.barrier()
    return nc


def _t5_bucket_np(rel):
    nb = 16
    max_exact = 8
    ret = np.where(rel > 0, nb, 0)
    n = np.abs(rel)
    nf = np.maximum(n, 1).astype(np.float32)
    large = max_exact + (np.log(nf / np.float32(max_exact)) / np.float32(math.log(1024 / max_exact))
                         * np.float32(nb - max_exact)).astype(np.int32)
    large = np.minimum(large, nb - 1)
    return ret + np.where(n < max_exact, n, large)


_CONST = {}


def _constants():
    if _CONST:
        return _CONST
    bf = ml_dtypes.bfloat16
    m = np.arange(P_G)
    rel = 191 - m
    oh = np.zeros((32, 3, P_G), np.float32)
    gmask = np.zeros((12, 3, P_G), np.float32)
    for di, d in enumerate(DILS):
        bk = _t5_bucket_np(rel * d)
        valid = np.abs(rel) <= 64
        oh[bk[valid], di, m[valid]] = 1.0
        gmask[:, di, valid] = 1.0
    _CONST["onehot"] = oh
    _CONST["gmask"] = gmask
    c = np.arange(64)
    ang = 2 * np.pi * np.outer(c, c) / 64
    C64 = np.cos(ang) / 512.0
    S64 = -np.sin(ang) / 512.0
    z = np.zeros((64, 64))
    _CONST["c64blk"] = np.block([[C64, z], [z, C64]]).astype(bf)
    _CONST["s64blk"] = np.block([[S64, z], [z, S64]]).astype(bf)
    _CONST["ident"] = np.eye(128, dtype=np.float32).astype(bf)
    s = np.arange(S, dtype=np.int64)
    prod = (s[:, None] * s[None, :]) % S
    angt = prod.astype(np.float64) * (2 * np.pi / S)
    for name, fn in (("dft_cos", np.cos), ("dft_sin", np.sin)):
        t = fn(angt).astype(np.float32)
        t = t.reshape(32, 128, 8, 512).transpose(2, 1, 0, 3)
        _CONST[name] = np.ascontiguousarray(t).astype(bf)
    return _CONST


_NC = {}


def kernel(x, norm_mix_gain, w_in, attn_out_gain, rel_bias_table, fourier_w, fourier_b, fourier_out_gain,
           w_out, norm_ffn_gain, w_gate, w_val, conv_w, conv_b, w_down, final_norm_gain):
    f32 = np.float32
    x = np.asarray(x, f32)
    cst = _constants()
    bc = lambda v, n: np.ascontiguousarray(np.broadcast_to(np.asarray(v, f32).reshape(1, n), (128, n)))
    fw = np.asarray(fourier_w, f32)[0]
    fw_blk = np.zeros((128, 2, 128), f32)
    for cc in range(2):
        fw_blk[0:64, cc, 0:64] = fw[2 * cc]
        fw_blk[64:128, cc, 64:128] = fw[2 * cc + 1]
    shared = {
        "g1b": bc(norm_mix_gain[0], D), "g2b": bc(norm_ffn_gain[0], D), "gFb": bc(final_norm_gain, D),
        "gfb": bc(fourier_out_gain[0], 256),
        "ga_t": np.ascontiguousarray(np.asarray(attn_out_gain, f32)[0].reshape(6, 128).T),
        "w_in": np.ascontiguousarray(np.asarray(w_in, f32)[0]),
        "w_out": np.ascontiguousarray(np.asarray(w_out, f32)[0]),
        "w_gate": np.ascontiguousarray(np.asarray(w_gate, f32)[0]),
        "w_val": np.ascontiguousarray(np.asarray(w_val, f32)[0]),
        "w_down": np.ascontiguousarray(np.asarray(w_down, f32)[0]),
        "cw_t": np.ascontiguousarray(np.asarray(conv_w, f32)[0].reshape(3, NF, 128).transpose(2, 1, 0)),
        "cb_t": np.ascontiguousarray(np.asarray(conv_b, f32)[0].reshape(NF, 128).T),
        "rel_tab": np.ascontiguousarray(np.asarray(rel_bias_table, f32)),
        "onehot": cst["onehot"], "gmask": cst["gmask"],
        "fw_blk": fw_blk,
        "fb_row": np.ascontiguousarray(np.asarray(fourier_b, f32)[0].reshape(1, 256)),
        "c64blk": cst["c64blk"], "s64blk": cst["s64blk"], "ident": cst["ident"],
        "dft_cos": cst["dft_cos"], "dft_sin": cst["dft_sin"],
    }
    n = x.shape[0]
    if "nc" not in _NC:
        _NC["nc"] = build_nc()
    nc = _NC["nc"]
    in_maps = [dict(shared, x=np.ascontiguousarray(x[b])) for b in range(n)]
    res = run_bass_kernel_spmd(nc, in_maps, core_ids=list(range(n)))
    return np.stack([np.asarray(r["y"], f32) for r in res.results], axis=0)
```

```python
import math
import os
from contextlib import ExitStack

import numpy as np
import ml_dtypes
import concourse.bass as bass
import concourse.mybir as mybir
from concourse.bass_utils import run_bass_kernel_spmd

F32 = mybir.dt.float32
BF16 = mybir.dt.bfloat16
AF = mybir.ActivationFunctionType
ALU = mybir.AluOpType

S = 4096
D = 1024
NT = 32
DFF = 2816
NF = 22
EPS = 1e-6
DILS = (1, 4, 16)
P_G = 384
REP = 64


class Ctx:
    def __init__(self, nc, es):
        self.nc = nc
        self.E = {"pe": nc.tensor, "act": nc.scalar, "dve": nc.vector, "pool": nc.gpsimd, "sp": nc.sync}
        self.sem = {k: es.enter_context(nc.semaphore("sem_" + k)) for k in self.E}
        self.cnt = {k: 0 for k in self.E}
        self.seen = {k: {} for k in self.E}
        self.lastw = {}
        self.readers = {}
        nd = 64
        self.dsem = [es.enter_context(nc.semaphore("dsem%d" % i)) for i in range(nd)]
        self.dval = [0] * nd
        self.dnext = 0

    def _wait(self, eng, tok):
        sem, key, val = tok
        if self.seen[eng].get(key, 0) >= val:
            return
        self.E[eng].wait_ge(sem, val)
        self.seen[eng][key] = val

    def _deps(self, eng, reads, writes):
        for k in list(reads) + list(writes):
            t = self.lastw.get(k)
            if t is not None:
                self._wait(eng, t)
        for k in writes:
            for t in self.readers.get(k, {}).values():
                self._wait(eng, t)

    def _commit(self, tok, reads, writes):
        for k in writes:
            self.lastw[k] = tok
            self.readers[k] = {}
        for k in reads:
            d = self.readers.setdefault(k, {})
            if tok[1] not in d or d[tok[1]][2] < tok[2]:
                d[tok[1]] = tok

    def op(self, eng, fns, reads=(), writes=()):
        self._deps(eng, reads, writes)
        if callable(fns):
            fns = [fns]
        ins = None
        for f in fns:
            ins = f(self.E[eng])
        self.cnt[eng] += 1
        ins.then_inc(self.sem[eng], 1)
        tok = (self.sem[eng], eng, self.cnt[eng])
        self._commit(tok, reads, writes)

    def dma(self, q, out, in_, reads=(), writes=(), **kw):
        self._deps(q, reads, writes)
        i = self.dnext
        self.dnext = (i + 1) % len(self.dsem)
        key = "d%d" % i
        if self.dval[i] > 0:
            self._wait(q, (self.dsem[i], key, self.dval[i]))
        self.E[q].dma_start(out=out, in_=in_, **kw).then_inc(self.dsem[i], 16)
        self.dval[i] += 16
        tok = (self.dsem[i], key, self.dval[i])
        self._commit(tok, reads, writes)

    def barrier(self):
        if os.environ.get("KDBG"):
            print("barrier counts", self.cnt, max(self.dval))
        toks = [(self.sem[k], k, self.cnt[k]) for k in self.E if self.cnt[k] > 0]
        toks += [(self.dsem[i], "d%d" % i, self.dval[i]) for i in range(len(self.dsem)) if self.dval[i] > 0]
        for eng in self.E:
            for t in toks:
                if t[1] != eng:
                    self._wait(eng, t)

    def finish(self, eng, keys):
        for k in keys:
            t = self.lastw.get(k)
            if t is not None:
                self._wait(eng, t)


_STOP = int(os.environ.get('KSTOP', '9'))


def build_nc():
    nc = bass.Bass("TRN2", target_bir_lowering=False)

    def din(name, shape, dt=F32):
        return nc.dram_tensor(name, list(shape), dt, kind="ExternalInput").ap()

    def dscr(name, shape, dt):
        return nc.dram_tensor(name, list(shape), dt, kind="Internal").ap()

    x_d = din("x", [S, D])
    g1_d = din("g1b", [128, D])
    g2_d = din("g2b", [128, D])
    gF_d = din("gFb", [128, D])
    gf_d = din("gfb", [128, 256])
    ga_d = din("ga_t", [128, 6])
    win_d = din("w_in", [D, 2560])
    wout_d = din("w_out", [D, D])
    wg_d = din("w_gate", [D, DFF])
    wv_d = din("w_val", [D, DFF])
    wd_d = din("w_down", [DFF, D])
    cw_d = din("cw_t", [128, NF, 3])
    cb_d = din("cb_t", [128, NF])
    tab_d = din("rel_tab", [32, 12])
    oh_d = din("onehot", [32, 3, P_G])
    gm_d = din("gmask", [12, 3, P_G])
    fwb_d = din("fw_blk", [128, 2, 128])
    fb_d = din("fb_row", [1, 256])
    c64_d = din("c64blk", [128, 128], BF16)
    s64_d = din("s64blk", [128, 128], BF16)
    id_d = din("ident", [128, 128], BF16)
    tabc_d = din("dft_cos", [8, 128, 32, 512], BF16)
    tabs_d = din("dft_sin", [8, 128, 32, 512], BF16)
    y_d = nc.dram_tensor("y", [S, D], F32, kind="ExternalOutput").ap()

    mix_d = dscr("mix_scr", [D, S], BF16)
    x1_d = dscr("x1_scr", [S, D], F32)
    h2_d = dscr("h2_scr", [D, S + 2], BF16)
    gr_d = dscr("gr_scr", [12, 3, REP, P_G], BF16)

    with ExitStack() as es:
        cx = Ctx(nc, es)

        def sb(stack, name, shape, dt):
            return stack.enter_context(nc.sbuf_tensor("sb_" + name, list(shape), dt))

        def ps(stack, name, shape, dt):
            return stack.enter_context(nc.psum_tensor("ps_" + name, list(shape), dt))

        ident = sb(es, "ident", [128, 128], BF16)
        ones_b = sb(es, "ones_b", [128, 128], BF16)
        epsb = sb(es, "epsb", [128, 1], F32)
        s2 = es.enter_context(ExitStack())
        hT = sb(s2, "hT", [128, 8, S], BF16)
        cx.dma("sp", ident[:], id_d[:, :], writes=["ident"])
        cx.op("pool", lambda e: e.memset(ones_b[:], 1.0), writes=["ones_b"])
        cx.op("pool", lambda e: e.memset(epsb[:], EPS), writes=["epsb"])

        def rms_rstd(stack_tiles, src_ap, src_key, n, tag):
            junk, ssq, rstd, kj, ks, kr = stack_tiles
            cx.op("act", lambda e: e.activation(out=junk, in_=src_ap, func=AF.Square, accum_out=ssq),
                  reads=[src_key], writes=[kj, ks])
            cx.op("act", lambda e: e.activation(out=rstd, in_=ssq, func=AF.Sqrt, scale=1.0 / n, bias=epsb[:, 0:1]),
                  reads=[ks, "epsb"], writes=[kr])
            cx.op("dve", lambda e: e.reciprocal(out=rstd, in_=rstd), reads=[kr], writes=[kr])

        with ExitStack() as pa:
            g1b = sb(pa, "g1b", [128, D], F32)
            tab = sb(pa, "tab", [32, 12], F32)
            oh = sb(pa, "oh", [32, 3, P_G], F32)
            gm = sb(pa, "gm", [12, 3, P_G], F32)
            gsb = sb(pa, "gsb", [12, 3, P_G], F32)
            gbf = sb(pa, "gbf", [12, 3, P_G], BF16)
            pG = ps(pa, "pG", [12, 3, 512], F32)
            cx.dma("sp", g1b[:], g1_d[:, :], writes=["g1b"])
            xt = [sb(pa, "xt%d" % i, [128, D], F32) for i in range(2)]
            junk = sb(pa, "junkA", [128, D], F32)
            hb = [sb(pa, "hb%d" % i, [128, D], BF16) for i in range(2)]
            ssq = [sb(pa, "ssqA%d" % i, [128, 1], F32) for i in range(2)]
            rstd = [sb(pa, "rstdA%d" % i, [128, 1], F32) for i in range(2)]
            pT = [ps(pa, "pTA%d" % i, [128, 8, 128], BF16) for i in range(2)]
            def a_s2(t):
                b = t % 2
                cx.op("pe", [(lambda e, c=c: e.transpose(out=pT[b][:, c, :], in_=hb[b][:, c * 128:(c + 1) * 128],
                                                         identity=ident[:])) for c in range(8)],
                      reads=["hb%d" % b, "ident"], writes=["pTA%d" % b])
                cx.op("act", lambda e: e.copy(out=hT[:, :, t * 128:(t + 1) * 128], in_=pT[b][:]),
                      reads=["pTA%d" % b], writes=[("hT", t // 4)])

            for t in range(NT):
                b = t % 2
                cx.dma("sp", xt[b][:], x_d[t * 128:(t + 1) * 128, :], writes=["xt%d" % b])
                rms_rstd((junk[:], ssq[b][:], rstd[b][:], "junkA", "ssqA%d" % b, "rstdA%d" % b),
                         xt[b][:], "xt%d" % b, D, "A")
                cx.op("dve", lambda e: e.scalar_tensor_tensor(out=hb[b][:], in0=xt[b][:], scalar=rstd[b][:, 0:1],
                                                              in1=g1b[:], op0=ALU.mult, op1=ALU.mult),
                      reads=["xt%d" % b, "rstdA%d" % b, "g1b"], writes=["hb%d" % b])
                if t >= 1:
                    a_s2(t - 1)
            a_s2(NT - 1)
            cx.dma("sp", tab[:], tab_d[:, :], writes=["tab"])
            cx.dma("sp", oh[:], oh_d[:, :, :], writes=["oh"])
            cx.dma("sp", gm[:], gm_d[:, :, :], writes=["gm"])
            cx.op("pe", [(lambda e, d=d: e.matmul(pG[:, d, 0:P_G], lhsT=tab[:], rhs=oh[:, d, :], start=True, stop=True))
                         for d in range(3)], reads=["tab", "oh"], writes=["pG"])
            cx.op("act", lambda e: e.activation(out=gsb[:], in_=pG[:, :, 0:P_G], func=AF.Exp),
                  reads=["pG"], writes=["gsb"])
            cx.op("dve", lambda e: e.tensor_tensor(out=gbf[:], in0=gsb[:], in1=gm[:], op=ALU.mult),
                  reads=["gsb", "gm"], writes=["gbf"])
            cx.dma("sp", gr_d[:, :, 0, :], gbf[:], reads=["gbf"], writes=["gr_d"])
            n = 1
            while n < REP:
                cx.dma("sp", gr_d[:, :, n:2 * n, :], gr_d[:, :, 0:n, :], reads=["gr_d"], writes=["gr_d"])
                n *= 2
        cx.barrier()
        if _STOP <= 1:
            return nc
        sD = es.enter_context(ExitStack())
        usb = sb(sD, "usb", [128, NT, 256], BF16)
        with ExitStack() as pu:
            wuf = sb(pu, "wuf", [128, 8, 256], F32)
            wub = sb(pu, "wub", [128, 8, 256], BF16)
            pU = [ps(pu, "pUu%d" % i, [128, 512], F32) for i in range(2)]
            cx.dma("sp", wuf[:], win_d.rearrange("(c p) n -> p c n", p=128)[:, :, 2304:2560], writes=["wuf"])
            cx.op("pool", lambda e: e.tensor_copy(out=wub[:], in_=wuf[:]), reads=["wuf"], writes=["wub"])
            for t in range(NT):
                b = t % 2
                cx.op("pe", [(lambda e, c=c: e.matmul(pU[b][:, 0:256], lhsT=hT[:, c, t * 128:(t + 1) * 128], rhs=wub[:, c, :],
                                                      start=(c == 0), stop=(c == 7))) for c in range(8)],
                      reads=[("hT", t // 4), "wub"], writes=["pUu%d" % b])
                cx.op("dve" if b else "act",
                      (lambda e: e.tensor_copy(out=usb[:, t, :], in_=pU[b][:, 0:256])) if b else
                      (lambda e: e.copy(out=usb[:, t, :], in_=pU[b][:, 0:256])),
                      reads=["pUu%d" % b], writes=[("usb", t)])
        cx.barrier()
        if _STOP <= 2:
            return nc
        with ExitStack() as pd:
            ABT = sb(pd, "ABT", [128, 2, 2, S], BF16)
            tabb = [sb(pd, "tabb%d" % i, [128, 32, 512], BF16) for i in range(2)]
            c64 = sb(pd, "c64", [128, 128], BF16)
            s64 = sb(pd, "s64", [128, 128], BF16)
            fwf = sb(pd, "fwf", [128, 2, 128], F32)
            fwb = sb(pd, "fwb", [128, 2, 128], BF16)
            M12 = sb(pd, "M12", [128, 2, 2, 128], BF16)
            fbf = sb(pd, "fbf", [1, 256], F32)
            fbb = sb(pd, "fbb", [1, 256], BF16)
            gfb = sb(pd, "gfb", [128, 256], F32)
            fjunk = sb(pd, "fjunk", [128, 256], F32)
            fssq = [sb(pd, "fssq%d" % i, [128, 1], F32) for i in range(2)]
            frstd = [sb(pd, "frstd%d" % i, [128, 1], F32) for i in range(2)]
            fnb = [sb(pd, "fnb%d" % i, [128, 256], BF16) for i in range(2)]
            fourT = sb(pd, "fourT", [128, 2, S], BF16)
            pU = [ps(pd, "pU%d" % i, [128, 512], F32) for i in range(2)]
            pF = [ps(pd, "pF%d" % i, [128, 512], F32) for i in range(2)]
            pM = ps(pd, "pM", [128, 512], F32)
            pFT = [ps(pd, "pFT%d" % i, [128, 2, 128], BF16) for i in range(2)]

            cx.dma("sp", c64[:], c64_d[:, :], writes=["c64"])
            cx.dma("sp", s64[:], s64_d[:, :], writes=["s64"])
            cx.dma("sp", fwf[:], fwb_d[:, :, :], writes=["fwf"])
            cx.dma("sp", fbf[:], fb_d[:, :], writes=["fbf"])
            cx.dma("sp", gfb[:], gf_d[:, :], writes=["gfb"])
            cx.op("pool", lambda e: e.tensor_copy(out=fwb[:], in_=fwf[:]), reads=["fwf"], writes=["fwb"])
            cx.op("pool", lambda e: e.tensor_copy(out=fbb[:], in_=fbf[:]), reads=["fbf"], writes=["fbb"])
            cx.op("pe", [(lambda e, cs=cs, cc=cc: e.matmul(pM[:, (cs * 2 + cc) * 128:(cs * 2 + cc + 1) * 128],
                                                           lhsT=(c64 if cs == 0 else s64)[:], rhs=fwb[:, cc, :],
                                                           start=True, stop=True)) for cs in range(2) for cc in range(2)],
                  reads=["c64", "s64", "fwb"], writes=["pM"])
            cx.op("act", lambda e: e.copy(out=M12[:].rearrange("p a b e -> p (a b e)"), in_=pM[:]),
                  reads=["pM"], writes=["M12"])
            it = 0
            for sbk in range(8):
                for cs, tsrc in ((0, tabc_d), (1, tabs_d)):
                    tb_ = it % 2
                    it += 1
                    cx.dma("sp", tabb[tb_][:], tsrc[sbk, :, :, :], writes=["tabb%d" % tb_])
                    for cc in range(2):
                        b = cc
                        cx.op("pe", [(lambda e, k=k: e.matmul(pF[b][:], lhsT=usb[:, k, cc * 128:(cc + 1) * 128],
                                                              rhs=tabb[tb_][:, k, :], start=(k == 0), stop=(k == 31)))
                                     for k in range(32)],
                              reads=[("usb", k_) for k_ in range(NT)] + ["tabb%d" % tb_], writes=["pF%d" % b])
                        if cc == 0:
                            cx.op("act", lambda e: e.copy(out=ABT[:, cs, cc, sbk * 512:(sbk + 1) * 512], in_=pF[b][:]),
                                  reads=["pF%d" % b], writes=["ABT"])
                        else:
                            cx.op("dve", lambda e: e.tensor_copy(out=ABT[:, cs, cc, sbk * 512:(sbk + 1) * 512], in_=pF[b][:]),
                                  reads=["pF%d" % b], writes=["ABT"])
            def f_s1(t):
                b = t % 2
                ts_ = slice(t * 128, (t + 1) * 128)
                mm = []
                for cc in range(2):
                    o_ap = pU[b][:, cc * 128:(cc + 1) * 128]
                    mm.append(lambda e, cc=cc, o_ap=o_ap: e.matmul(o_ap, lhsT=ABT[:, 0, cc, ts_], rhs=M12[:, 0, cc, :], start=True, stop=False))
                    mm.append(lambda e, cc=cc, o_ap=o_ap: e.matmul(o_ap, lhsT=ABT[:, 1, cc, ts_], rhs=M12[:, 1, cc, :], start=False, stop=False))
                    mm.append(lambda e, cc=cc, o_ap=o_ap: e.matmul(o_ap, lhsT=ones_b[0:1, :], rhs=fbb[0:1, cc * 128:(cc + 1) * 128], start=False, stop=True))
                cx.op("pe", mm, reads=["ABT", "M12", "ones_b", "fbb"], writes=["pU%d" % b])
                rms_rstd((fjunk[:], fssq[b][:], frstd[b][:], "fjunk", "fssq%d" % b, "frstd%d" % b),
                         pU[b][:, 0:256], "pU%d" % b, 256, "F")
                cx.op("dve", lambda e: e.scalar_tensor_tensor(out=fnb[b][:], in0=pU[b][:, 0:256], scalar=frstd[b][:, 0:1],
                                                              in1=gfb[:], op0=ALU.mult, op1=ALU.mult),
                      reads=["pU%d" % b, "frstd%d" % b, "gfb"], writes=["fnb%d" % b])

            def f_s2(t):
                b = t % 2
                ts_ = slice(t * 128, (t + 1) * 128)
                cx.op("pe", [(lambda e, cc=cc: e.transpose(out=pFT[b][:, cc, :], in_=fnb[b][:, cc * 128:(cc + 1) * 128],
                                                           identity=ident[:])) for cc in range(2)],
                      reads=["fnb%d" % b, "ident"], writes=["pFT%d" % b])
                cx.op("act", lambda e: e.copy(out=fourT[:, :, ts_], in_=pFT[b][:]),
                      reads=["pFT%d" % b], writes=["fourT"])

            for t in range(NT):
                f_s1(t)
                if t >= 1:
                    f_s2(t - 1)
            f_s2(NT - 1)
            for cc in range(2):
                cx.dma("sp", mix_d[768 + cc * 128:768 + (cc + 1) * 128, :], fourT[:, cc, :], reads=["fourT"],
                       writes=[("mix_d", 6 + cc)])

        cx.barrier()
        if _STOP <= 3:
            return nc
        sD.close()
        with ExitStack() as pb:
            EB = sb(pb, "EB", [128, 6, 3, 512], BF16)
            if _STOP <= 4:
                return nc
            wfr = sb(pb, "wfr", [128, 1024], F32)
            wf = [wfr[:].rearrange("p (c n) -> p c n", c=8)]
            wqkv = sb(pb, "wqkv", [128, 3, 8, 128], BF16)
            QT = sb(pb, "QT", [128, 2, S], BF16)
            VT = sb(pb, "VT", [128, S], BF16)
            KPb = [sb(pb, "KP%d" % d, [128, d * (S // d + 128)], BF16) for d in DILS]
            Vs = sb(pb, "Vs", [128, 48, 192], BF16)
            acc = [sb(pb, "acc%d" % i, [128, S], F32) for i in range(2)]
            attn = [sb(pb, "attn%d" % i, [128, 512], BF16) for i in range(2)]
            Eb = [sb(pb, "Eb%d" % i, [128, 512], BF16) for i in range(2)]
            PTb = [sb(pb, "PTb%d" % i, [128, 512], BF16) for i in range(4)]
            pVt2 = [ps(pb, "pVt%d" % i, [128, 8, 128], BF16) for i in range(2)]
            vbi = 0
            pS = [ps(pb, "pS%d" % i, [128, 512], F32) for i in range(2)]
            pI = pS
            pO = [ps(pb, "pO%d" % i, [128, 512], F32) for i in range(4)]

            cx.op("pool", lambda e: e.memset(Vs[:], 1.0), writes=["VsL", "VsU"])
            for i_ in range(3):
                cx.op("pool", lambda e: e.memset(KPb[i_][:], 0.0), writes=[("KP", i_)])
            cx.op("pool", lambda e: e.memset(QT[:], 0.0), writes=[("QT", i) for i in range(8)])
            win_v = win_d.rearrange("(c p) n -> p c n", p=128)
            def load_weights(j):
                for wi, col0 in enumerate((j * 128, 768 + j * 128, 1536 + j * 128)):
                    cx.dma("sp", wf[0], win_v[:, :, col0:col0 + 128], writes=["wfr"])
                    cx.op("pool", lambda e: e.tensor_copy(out=wqkv[:, wi, :, :], in_=wf[0]),
                          reads=["wfr"], writes=[("wqkv", wi)])

            def epilogue_piece(j, cb_):
                cs = slice(cb_ * 512, (cb_ + 1) * 512)
                rden = wfr[:, (cb_ % 2) * 512:(cb_ % 2) * 512 + 512]
                rk = "wfr"
                cx.op("dve", [lambda e: e.tensor_copy(out=rden[0:64, :], in_=acc[0][64:128, cs]),
                              lambda e: e.tensor_copy(out=rden[64:128, :], in_=acc[1][0:64, cs])],
                      reads=["acc0", "acc1"], writes=[rk])
                cx.op("act", lambda e: e.activation(out=rden, in_=rden, func=AF.Ln), reads=[rk], writes=[rk])
                cx.op("act", lambda e: e.activation(out=rden, in_=rden, func=AF.Exp, scale=-1.0), reads=[rk], writes=[rk])
                ab_ = cb_ % 2
                cx.op("dve", [lambda e: e.tensor_tensor(out=attn[ab_][0:64, :], in0=acc[0][0:64, cs], in1=rden[0:64, :], op=ALU.mult),
                              lambda e: e.tensor_tensor(out=attn[ab_][64:128, :], in0=acc[1][64:128, cs], in1=rden[64:128, :], op=ALU.mult)],
                      reads=["acc0", "acc1", rk], writes=["attn%d" % ab_])
                cx.dma("sp", mix_d[j * 128:(j + 1) * 128, cs], attn[ab_][:], reads=["attn%d" % ab_], writes=[("mix_d", j, cb_)])

            load_weights(0)
            for j in range(6):
                if j == 0:
                    for jp in range(6):
                        for d_ in range(3):
                            for (r0, off) in ((0, 63), (64, 127)):
                                for h_ in range(2):
                                    base = ((2 * jp + h_) * 3 + d_) * REP * P_G
                                    src = bass.AP(gr_d.tensor, base + off, [[P_G - 1, 64], [128, 2], [1, 128]])
                                    dst = EB[r0:r0 + 64, jp, d_, :].rearrange("p (k h c) -> p k h c", k=2, h=2)[:, :, h_, :]
                                    q_ = "sp"
                                    cx.dma(q_, dst, src, reads=["gr_d"], writes=[("EBp", jp, d_, r0, h_)])
                it = 0
                pend_epi = [(j - 1, c_) for c_ in range(8)] if j > 0 else []
                QTk = [("QT", i) for i in range(8)] + [("QTb", i) for i in range(8)]
                KTk = [("KT", i) for i in range(8)]
                VTk = [("VT", i) for i in range(8)]
                for wi, dst, key0 in ((0, QT, "QT"), (1, None, "KT"), (2, VT, "VT")):
                    for tb in range(8):
                        b = it % 2
                        it += 1
                        cx.op("pe", [(lambda e, c=c: e.matmul(pI[b][:], lhsT=wqkv[:, wi, c, :],
                                                              rhs=hT[:, c, tb * 512:(tb + 1) * 512],
                                                              start=(c == 0), stop=(c == 7))) for c in range(8)],
                              reads=[("wqkv", wi), ("hT", tb)], writes=["pS%d" % b])
                        key = (key0, tb)
                        tsl = slice(tb * 512, (tb + 1) * 512)
                        if wi == 0:
                            cx.op("act", lambda e: e.copy(out=QT[0:64, 0, tsl], in_=pI[b][0:64, :]),
                                  reads=["pS%d" % b], writes=[key])
                            cx.op("dve", lambda e: e.tensor_copy(out=QT[64:128, 1, tsl], in_=pI[b][64:128, :]),
                                  reads=["pS%d" % b], writes=[(key0 + "b", tb)])
                        elif wi == 2:
                            if tb % 2:
                                cx.op("act", lambda e: e.copy(out=VT[:, tsl], in_=pI[b][:]), reads=["pS%d" % b], writes=[key])
                            else:
                                cx.op("dve", lambda e: e.tensor_copy(out=VT[:, tsl], in_=pI[b][:]), reads=["pS%d" % b], writes=[key])
                        else:
                            src1 = pI[b][:].rearrange("p (m x) -> p m x", x=128)
                            k1 = KPb[0]
                            d0 = k1[:, tb * 512:tb * 512 + 512].rearrange("p (m x) -> p m x", x=128)
                            d1 = k1[:, tb * 512 + 128:tb * 512 + 640].rearrange("p (m x) -> p m x", x=128)
                            src4 = pI[b][:].rearrange("p (l r) -> p r l", r=4)
                            k4 = KPb[1][:].rearrange("p (r l) -> p r l", r=4)
                            src16 = pI[b][:].rearrange("p (l r) -> p r l", r=16)
                            k16 = KPb[2][:].rearrange("p (r l) -> p r l", r=16)
                            pos16 = 32 * tb + (128 if ((tb // 2) % 2) else 0)
                            for eng, pr in (("act", slice(0, 64)), ("dve", slice(64, 128))):
                                cp = (lambda e, o, i_: e.copy(out=o, in_=i_)) if eng == "act" else (lambda e, o, i_: e.tensor_copy(out=o, in_=i_))
                                cx.op(eng, [lambda e: cp(e, d0[pr, :, 0:64], src1[pr, :, 0:64]),
                                            lambda e: cp(e, d1[pr, :, 64:128], src1[pr, :, 64:128]),
                                            lambda e: cp(e, k4[pr, :, 128 * tb:128 * tb + 64], src4[pr, :, 0:64]),
                                            lambda e: cp(e, k4[pr, :, 128 * tb + 192:128 * tb + 256], src4[pr, :, 64:128]),
                                            lambda e: cp(e, k16[pr, :, pos16:pos16 + 32], src16[pr, :, :])],
                                      reads=["pS%d" % b], writes=[("KPa" if eng == "act" else "KPd", tb)])
                        if pend_epi and it % 3 == 0:
                            epilogue_piece(*pend_epi.pop(0))
                while pend_epi:
                    epilogue_piece(*pend_epi.pop(0))
                if j + 1 < 6:
                    load_weights(j + 1)
                KPk = [("KPa", i) for i in range(8)] + [("KPd", i) for i in range(8)] + [("KP", i) for i in range(3)]
                for h in range(2):
                    pass

                oi = 0
                si = 0
                for di, d in enumerate(DILS):
                    L = S // d
                    Lb = L + 128
                    nqb = L // 128
                    nsl = nqb + 1
                    KPv = KPb[di][:].rearrange("p (r l) -> p r l", r=d)
                    VTv = VT[:].rearrange("p (l r) -> p r l", r=d)
                    Vsv = Vs[:, 0:d * nsl, :].rearrange("p (r s) c -> p r s c", r=d)
                    for r in range(d):
                        for t0 in range(0, nqb, 8):
                            nt_ = min(8, nqb - t0)
                            vb = vbi % 2
                            vbi += 1
                            pVt = pVt2[vb]
                            cx.op("pe", [(lambda e, i=i: e.transpose(out=pVt[:, i, :],
                                                                     in_=VTv[:, r, (t0 + i) * 128:(t0 + i + 1) * 128],
                                                                     identity=ident[:])) for i in range(nt_)],
                                  reads=VTk + ["ident"], writes=["pVt%d" % vb])
                            cx.op("dve", [lambda e: e.tensor_copy(out=Vsv[0:64, r, t0:t0 + nt_, 0:64], in_=pVt[0:64, 0:nt_, 0:64]),
                                          lambda e: e.tensor_copy(out=Vsv[0:64, r, t0:t0 + nt_, 128:192], in_=pVt[0:64, 0:nt_, 64:128])],
                                  reads=["pVt%d" % vb], writes=["VsL"])
                            cx.op("act", [lambda e: e.copy(out=Vsv[64:128, r, t0 + 1:t0 + 1 + nt_, 0:64], in_=pVt[64:128, 0:nt_, 0:64]),
                                          lambda e: e.copy(out=Vsv[64:128, r, t0 + 1:t0 + 1 + nt_, 128:192], in_=pVt[64:128, 0:nt_, 64:128])],
                                  reads=["pVt%d" % vb], writes=["VsU"])
                    QTv = QT[:].rearrange("p a (l r) -> p a r l", r=d)
                    accv = [acc[h][:].rearrange("p (l r) -> p r l", r=d) for h in range(2)]
                    vcols = (slice(0, 128), slice(64, 192))
                    groups = []
                    if d == 16:
                        for r0 in range(0, 16, 2):
                            groups.append([(r0, 0), (r0, 1), (r0 + 1, 0), (r0 + 1, 1)])
                    else:
                        for r in range(d):
                            for g0 in range(0, nqb, 4):
                                groups.append([(r, g0 + i) for i in range(4)])
                    pend = []

                    def emit_pv(item):
                        (r, qb, sidx, ob, slot, grp) = item
                        pt = PTb[sidx]
                        lo_ok = qb > 0
                        hi_ok = qb < nqb - 1
                        for h in range(2):
                            vcol = vcols[h]
                            pob = pO[h * 2 + ob]
                            o_ap = pob[:, slot * 128:(slot + 1) * 128]
                            cb2 = slice(h * 128, h * 128 + 128)
                            ca = slice(256 + h * 128, 256 + h * 128 + 128)
                            mm = []
                            if lo_ok:
                                mm.append(lambda e: e.matmul(o_ap, lhsT=Vsv[:, r, qb, vcol], rhs=pt[:, ca], start=(slot == 0), stop=False, skip_group_check=True))
                            else:
                                mm.append(lambda e: e.matmul(o_ap, lhsT=Vsv[0:64, r, qb, vcol], rhs=pt[0:64, ca], start=(slot == 0), stop=False, skip_group_check=True))
                            if hi_ok:
                                mm.append(lambda e: e.matmul(o_ap, lhsT=Vsv[:, r, qb + 1, vcol], rhs=pt[:, cb2], start=False, stop=True, skip_group_check=True))
                            else:
                                mm.append(lambda e: e.matmul(o_ap, lhsT=Vsv[64:128, r, qb + 1, vcol], rhs=pt[64:128, cb2], start=False, stop=True, skip_group_check=True))
                            cx.op("pe", mm, reads=["VsL", "VsU", "PTb%d" % sidx], writes=[("pO", h * 2 + ob, slot)])
                            if slot == 3:
                                if d == 16:
                                    r0 = grp[0][0]
                                    dst = accv[h][:, r0:r0 + 2, :]
                                    srcp = pob[:].rearrange("p (a l) -> p a l", a=2)
                                else:
                                    r_, q0 = grp[0]
                                    dst = accv[h][:, r_, q0 * 128:q0 * 128 + 512]
                                    srcp = pob[:]
                                if di == 0:
                                    cx.op("act", lambda e: e.copy(out=dst, in_=srcp),
                                          reads=[("pO", h * 2 + ob, s_) for s_ in range(4)], writes=["acc%d" % h])
                                else:
                                    cx.op("dve", lambda e: e.tensor_tensor(out=dst, in0=srcp, in1=dst, op=ALU.add),
                                          reads=[("pO", h * 2 + ob, s_) for s_ in range(4)] + ["acc%d" % h], writes=["acc%d" % h])

                    for grp in groups:
                        ob = oi % 2
                        oi += 1
                        for slot, (r, qb) in enumerate(grp):
                            sidx = si % 4
                            ebi = si % 2
                            psi = si % 2
                            si += 1
                            q_ap = QTv[:, :, r, qb * 128:(qb + 1) * 128]
                            kA = KPv[:, r, qb * 128:(qb + 1) * 128]
                            kB = KPv[:, r, (qb + 1) * 128:(qb + 2) * 128]
                            oB = pS[psi][:, 0:256].rearrange("p (a q) -> p a q", a=2)
                            oA = pS[psi][:, 256:512].rearrange("p (a q) -> p a q", a=2)
                            cx.op("pe", [lambda e: e.matmul(oA, lhsT=kA, rhs=q_ap, start=True, stop=True),
                                         lambda e: e.matmul(oB, lhsT=kB, rhs=q_ap, start=True, stop=True)],
                                  reads=QTk + KPk, writes=["pS%d" % psi])
                            cx.op("act", lambda e: e.activation(out=Eb[ebi][:], in_=pS[psi][:], func=AF.Exp, scale=0.125),
                                  reads=["pS%d" % psi], writes=["Eb%d" % ebi])
                            cx.op("dve", lambda e: e.tensor_tensor(out=PTb[sidx][:], in0=Eb[ebi][:], in1=EB[:, j, di, :], op=ALU.mult),
                                  reads=["Eb%d" % ebi] + [("EBp", j, di, r0_, h_) for r0_ in (0, 64) for h_ in range(2)], writes=["PTb%d" % sidx])
                            pend.append((r, qb, sidx, ob, slot, grp))
                            if len(pend) > 2:
                                emit_pv(pend.pop(0))
                    while pend:
                        emit_pv(pend.pop(0))

            for cb_ in range(8):
                epilogue_piece(5, cb_)
        cx.barrier()
        if _STOP <= 5:
            return nc
        s2.close()
        sW = es.enter_context(ExitStack())
        wgb = sb(sW, "wgb", [128, 8, DFF], BF16)
        wvb = sb(sW, "wvb", [128, 8, DFF], BF16)
        wdb = sb(sW, "wdb", [128, NF, D], BF16)
        wst = [sb(sW, "wst%d" % i, [128, 704], F32) for i in range(3)]
        wjobs = []
        for (src_d, dstw, key) in ((wg_d, wgb, "wgb"), (wv_d, wvb, "wvb")):
            for c in range(8):
                for q4 in range(4):
                    cols = slice(q4 * 704, (q4 + 1) * 704)
                    wjobs.append((src_d[c * 128:(c + 1) * 128, cols], dstw[:, c, cols], 704, key))
        for f in range(NF):
            wjobs.append((wd_d[f * 128:(f + 1) * 128, 0:704], wdb[:, f, 0:704], 704, "wdb"))
            wjobs.append((wd_d[f * 128:(f + 1) * 128, 704:D], wdb[:, f, 704:D], D - 704, "wdb"))
        winfl = [None, None, None]
        wstate = {"i": 0}

        def emit_wslot(b):
            if winfl[b] is not None:
                dst, wid, key = winfl[b]
                i = wstate["i"]
                wstate["i"] += 1
                if i % 2:
                    cx.op("act", lambda e: e.copy(out=dst, in_=wst[b][:, 0:wid]), reads=["wst%d" % b], writes=[key])
                else:
                    cx.op("dve", lambda e: e.tensor_copy(out=dst, in_=wst[b][:, 0:wid]), reads=["wst%d" % b], writes=[key])
                winfl[b] = None
            if wjobs:
                src, dst, wid, key = wjobs.pop(0)
                cx.dma("sp", wst[b][:, 0:wid], src, writes=["wst%d" % b])
                winfl[b] = (dst, wid, key)

        with ExitStack() as pe2:
            wob = sb(pe2, "wob", [128, 8, D], BF16)
            gat = sb(pe2, "gat", [128, 6], F32)
            g2b = sb(pe2, "g2b", [128, D], F32)
            mixb = [sb(pe2, "mixb%d" % i, [128, 8, 512], BF16) for i in range(2)]
            sqb = [sb(pe2, "sqb%d" % i, [128, 512], BF16) for i in range(2)]
            rsa = sb(pe2, "rsa", [128, 512], F32)
            x1 = [sb(pe2, "x1_%d" % i, [128, D], F32) for i in range(3)]
            ssqE = [sb(pe2, "ssqE%d" % i, [128, 1], F32) for i in range(2)]
            rstdE = [sb(pe2, "rstdE%d" % i, [128, 1], F32) for i in range(2)]
            h2b = [sb(pe2, "h2b%d" % i, [128, D], BF16) for i in range(3)]
            h2T = [sb(pe2, "h2T%d" % i, [128, 8, 128], BF16) for i in range(2)]
            zcol = sb(pe2, "zcol", [128, 8, 1], BF16)
            pR = ps(pe2, "pR", [128, 512], F32)
            pY = [ps(pe2, "pY%d" % i, [128, D], F32) for i in range(2)]
            pT2 = [ps(pe2, "pT2%d" % i, [128, 8, 128], BF16) for i in range(2)]

            cx.dma("sp", gat[:], ga_d[:, :], writes=["gat"])
            cx.dma("sp", g2b[:], g2_d[:, :], writes=["g2b"])
            for c in range(8):
                b = c % 2
                for (c0_, c1_) in ((0, 704), (704, D)):
                    bb = (2 * c + (c0_ > 0)) % 3
                    cx.dma("sp", wst[bb][:, 0:c1_ - c0_], wout_d[c * 128:(c + 1) * 128, c0_:c1_], writes=["wst%d" % bb])
                    if c % 2:
                        cx.op("act", lambda e: e.copy(out=wob[:, c, c0_:c1_], in_=wst[bb][:, 0:c1_ - c0_]), reads=["wst%d" % bb], writes=["wob"])
                    else:
                        cx.op("dve", lambda e: e.tensor_copy(out=wob[:, c, c0_:c1_], in_=wst[bb][:, 0:c1_ - c0_]), reads=["wst%d" % bb], writes=["wob"])
            cx.op("pool", lambda e: e.memset(zcol[:], 0.0), writes=["zcol"])
            h2v = h2_d.rearrange("(c p) s -> p c s", p=128)
            cx.dma("act", h2v[:, :, 0:1], zcol[:], reads=["zcol"], writes=["h2halo"], allow_slow_non_contiguous=True)
            cx.dma("act", h2v[:, :, S + 1:S + 2], zcol[:], reads=["zcol"], writes=["h2halo"], allow_slow_non_contiguous=True)
            mixv = mix_d.rearrange("(c p) s -> p c s", p=128)
            def e_load(blk):
                mb = blk % 2
                bs = slice(blk * 512, (blk + 1) * 512)
                cx.dma("sp", mixb[mb][:], mixv[:, :, bs], reads=[("mix_d", i, blk) for i in range(6)] + [("mix_d", 6), ("mix_d", 7)], writes=["mixb%d" % mb])

            def e_prologue(blk):
                mb = blk % 2
                for jj in range(6):
                    q = jj % 2
                    cx.op("act", lambda e: e.activation(out=sqb[q][:], in_=mixb[mb][:, jj, :], func=AF.Square),
                          reads=["mixb%d" % mb], writes=["sqb%d" % q])
                    cx.op("pe", lambda e: e.matmul(pR[:], lhsT=ones_b[:], rhs=sqb[q][:], start=(jj == 0), stop=(jj == 5)),
                          reads=["ones_b", "sqb%d" % q], writes=["pR"])
                cx.op("act", lambda e: e.activation(out=rsa[:], in_=pR[:], func=AF.Sqrt, scale=1.0 / 768, bias=epsb[:, 0:1]),
                      reads=["pR", "epsb"], writes=["rsa"])
                cx.op("dve", lambda e: e.reciprocal(out=rsa[:], in_=rsa[:]), reads=["rsa"], writes=["rsa"])
                for jj in range(6):
                    cx.op("dve", lambda e: e.scalar_tensor_tensor(out=mixb[mb][:, jj, :], in0=mixb[mb][:, jj, :],
                                                                  scalar=gat[:, jj:jj + 1], in1=rsa[:],
                                                                  op0=ALU.mult, op1=ALU.mult),
                          reads=["mixb%d" % mb, "gat", "rsa"], writes=["mixb%d" % mb])

            def e_s1(t):
                blk, tt = t // 4, t % 4
                mb = blk % 2
                b = t % 2
                b3 = t % 3
                xk = "x1_%d" % b3
                mm = []
                for hf in range(2):
                    for c in range(8):
                        mm.append(lambda e, hf=hf, c=c: e.matmul(pY[b][:, hf * 512:(hf + 1) * 512],
                                                                 lhsT=mixb[mb][:, c, tt * 128:(tt + 1) * 128],
                                                                 rhs=wob[:, c, hf * 512:(hf + 1) * 512],
                                                                 start=(c == 0), stop=(c == 7)))
                cx.op("pe", mm, reads=["mixb%d" % mb, "wob"], writes=["pY%d" % b])
                cx.op("dve", lambda e: e.tensor_tensor(out=x1[b3][:], in0=pY[b][:], in1=x1[b3][:], op=ALU.add),
                      reads=["pY%d" % b, xk], writes=[xk])
                cx.dma("act", x1_d[t * 128:(t + 1) * 128, :], x1[b3][:], reads=[xk], writes=[("x1_d", t)])
                for b_w in range(3):
                    emit_wslot(b_w)

            def e_s1b(t):
                b = t % 2
                b3 = t % 3
                xk = "x1_%d" % b3
                rms_rstd((h2b[b3][:], ssqE[b][:], rstdE[b][:], "h2b%d" % b3, "ssqE%d" % b, "rstdE%d" % b),
                         x1[b3][:], xk, D, "E")
                cx.op("dve", lambda e: e.scalar_tensor_tensor(out=h2b[b3][:], in0=x1[b3][:], scalar=rstdE[b][:, 0:1],
                                                              in1=g2b[:], op0=ALU.mult, op1=ALU.mult),
                      reads=[xk, "rstdE%d" % b, "g2b"], writes=["h2b%d" % b3])

            def e_s2(t):
                b = t % 2
                b3 = t % 3
                cx.op("pe", [(lambda e, c=c: e.transpose(out=pT2[b][:, c, :], in_=h2b[b3][:, c * 128:(c + 1) * 128],
                                                         identity=ident[:])) for c in range(8)],
                      reads=["h2b%d" % b3, "ident"], writes=["pT2%d" % b])
                cx.op("act", lambda e: e.copy(out=h2T[b][:], in_=pT2[b][:]), reads=["pT2%d" % b], writes=["h2T%d" % b])
                cx.dma("act", h2v[:, :, 1 + t * 128:1 + (t + 1) * 128], h2T[b][:], reads=["h2T%d" % b],
                       writes=[("h2_d", t)])

            def load_x(t):
                cx.dma("sp", x1[t % 3][:], x_d[t * 128:(t + 1) * 128, :], writes=["x1_%d" % (t % 3)])

            e_load(0)
            e_prologue(0)
            load_x(0)
            for t in range(NT):
                if t % 4 == 0 and t // 4 + 1 < 8:
                    e_load(t // 4 + 1)
                if t + 1 < NT:
                    load_x(t + 1)
                e_s1(t)
                if t >= 1:
                    e_s1b(t - 1)
                if t % 4 == 3 and t // 4 + 1 < 8:
                    e_prologue(t // 4 + 1)
                if t >= 3:
                    e_s2(t - 3)
            e_s1b(NT - 1)
            e_s2(NT - 3)
            e_s2(NT - 2)
            e_s2(NT - 1)
            while wjobs or any(w is not None for w in winfl):
                for b_w in range(3):
                    emit_wslot(b_w)
        cx.barrier()
        if _STOP <= 6:
            return nc
        with ExitStack() as pf:
            cwt = sb(pf, "cwt", [128, NF, 3], F32)
            cbt = sb(pf, "cbt", [128, NF], F32)
            gFb = sb(pf, "gFb", [128, D], F32)
            h2s = [sb(pf, "h2s%d" % i, [128, 8, 258], BF16) for i in range(2)]
            t1 = [sb(pf, "t1_%d" % i, [128, 256], F32) for i in range(2)]
            t2 = [sb(pf, "t2_%d" % i, [128, 256], F32) for i in range(2)]
            sg = [sb(pf, "sg_%d" % i, [128, 256], F32) for i in range(2)]
            aT = [sb(pf, "aT_%d" % i, [128, 256], BF16) for i in range(4)]
            x1r = [sb(pf, "x1r%d" % i, [128, D], F32) for i in range(2)]
            of_ = [sb(pf, "of%d" % i, [128, D], F32) for i in range(2)]
            junkF = sb(pf, "junkF", [128, D], F32)
            ssqF = [sb(pf, "ssqF%d" % i, [128, 1], F32) for i in range(2)]
            rstdF = [sb(pf, "rstdF%d" % i, [128, 1], F32) for i in range(2)]
            yo = [sb(pf, "yo%d" % i, [128, D], F32) for i in range(2)]
            pGt = [ps(pf, "pGt%d" % i, [128, 512], F32) for i in range(2)]
            pVl = [ps(pf, "pVl%d" % i, [128, 512], F32) for i in range(2)]
            pD = [ps(pf, "pD%d" % i, [128, D], F32) for i in range(2)]

            cx.dma("sp", cwt[:], cw_d[:, :, :], writes=["cwt"])
            cx.dma("sp", cbt[:], cb_d[:, :], writes=["cbt"])
            cx.dma("sp", gFb[:], gF_d[:, :], writes=["gFb"])
            h2v = h2_d.rearrange("(c p) s -> p c s", p=128)
            _SUB = int(os.environ.get("KSUB", "9"))
            def load_h2s(st_):
                cx.dma("sp", h2s[st_ % 2][:], h2v[:, :, st_ * 256:st_ * 256 + 258],
                       reads=[("h2_d", t) for t in range(max(0, 2 * st_ - 1), min(NT, 2 * st_ + 3))] + ["h2halo"],
                       writes=["h2s%d" % (st_ % 2)])

            load_h2s(0)
            for st in range(16 if _SUB > 1 else 0):
                hb_ = st % 2
                for tt in range(2):
                    t = st * 2 + tt
                    cx.dma("sp", x1r[t % 2][:], x1_d[t * 128:(t + 1) * 128, :], reads=[("x1_d", t)], writes=["x1r%d" % (t % 2)])
                if st + 1 < 16:
                    load_h2s(st + 1)
                pend = []

                def emit_down(item):
                    f, ab = item
                    mm = []
                    for tt in range(2):
                        for hf in range(2):
                            mm.append(lambda e, tt=tt, hf=hf: e.matmul(pD[tt][:, hf * 512:(hf + 1) * 512],
                                                                       lhsT=aT[ab][:, tt * 128:(tt + 1) * 128],
                                                                       rhs=wdb[:, f, hf * 512:(hf + 1) * 512],
                                                                       start=(f == 0), stop=(f == NF - 1)))
                    cx.op("pe", mm, reads=["aT_%d" % ab, "wdb"], writes=(["pD0", "pD1"] if f in (0, NF - 1) else []))

                for f in range(NF):
                    b = f % 2
                    ab = f % 4
                    fs = slice(f * 128, (f + 1) * 128)
                    cx.op("pe", [(lambda e, c=c: e.matmul(pGt[b][:, 0:258], lhsT=wgb[:, c, fs], rhs=h2s[hb_][:, c, 0:258],
                                                          start=(c == 0), stop=(c == 7))) for c in range(8)],
                          reads=["wgb", "h2s%d" % hb_], writes=["pGt%d" % b])
                    cx.op("pe", [(lambda e, c=c: e.matmul(pVl[b][:, 0:256], lhsT=wvb[:, c, fs], rhs=h2s[hb_][:, c, 1:257],
                                                          start=(c == 0), stop=(c == 7))) for c in range(8)],
                          reads=["wvb", "h2s%d" % hb_], writes=["pVl%d" % b])
                    cx.op("act", lambda e: e.activation(out=t1[b][:], in_=pGt[b][:, 1:257], func=AF.Identity,
                                                        scale=cwt[:, f, 1:2], bias=cbt[:, f:f + 1]),
                          reads=["pGt%d" % b, "cwt", "cbt"], writes=["t1_%d" % b])
                    cx.op("dve", lambda e: e.scalar_tensor_tensor(out=t2[b][:], in0=pGt[b][:, 0:256], scalar=cwt[:, f, 0:1],
                                                                  in1=t1[b][:], op0=ALU.mult, op1=ALU.add),
                          reads=["pGt%d" % b, "cwt", "t1_%d" % b], writes=["t2_%d" % b])
                    cx.op("dve", lambda e: e.scalar_tensor_tensor(out=t1[b][:], in0=pGt[b][:, 2:258], scalar=cwt[:, f, 2:3],
                                                                  in1=t2[b][:], op0=ALU.mult, op1=ALU.add),
                          reads=["pGt%d" % b, "cwt", "t2_%d" % b], writes=["t1_%d" % b])
                    cx.op("act", lambda e: e.activation(out=sg[b][:], in_=t1[b][:], func=AF.Silu),
                          reads=["t1_%d" % b], writes=["sg_%d" % b])
                    cx.op("dve", lambda e: e.tensor_tensor(out=aT[ab][:], in0=pVl[b][:, 0:256], in1=sg[b][:], op=ALU.mult),
                          reads=["pVl%d" % b, "sg_%d" % b], writes=["aT_%d" % ab])
                    if _SUB > 2:
                        pend.append((f, ab))
                    if len(pend) > 2:
                        emit_down(pend.pop(0))
                while pend:
                    emit_down(pend.pop(0))
                for tt in range(2 if _SUB > 3 else 0):
                    t = st * 2 + tt
                    b = t % 2
                    cx.op("act", lambda e: e.copy(out=of_[b][:], in_=pD[tt][:]), reads=["pD%d" % tt], writes=["of%d" % b])
                    cx.op("dve", lambda e: e.tensor_tensor(out=of_[b][:], in0=of_[b][:], in1=x1r[b][:], op=ALU.add),
                          reads=["of%d" % b, "x1r%d" % b], writes=["of%d" % b])
                    rms_rstd((junkF[:], ssqF[b][:], rstdF[b][:], "junkF", "ssqF%d" % b, "rstdF%d" % b),
                             of_[b][:], "of%d" % b, D, "F")
                    cx.op("dve", lambda e: e.scalar_tensor_tensor(out=yo[b][:], in0=of_[b][:], scalar=rstdF[b][:, 0:1],
                                                                  in1=gFb[:], op0=ALU.mult, op1=ALU.mult),
                          reads=["of%d" % b, "rstdF%d" % b, "gFb"], writes=["yo%d" % b])
                    cx.dma(os.environ.get("KYQ", "sp"), y_d[t * 128:(t + 1) * 128, :], yo[b][:], reads=["yo%d" % b], writes=[("y", t)])
            cx.finish("sp", [("y", t) for t in range(NT)])
            cx.barrier()
    return nc


def _t5_bucket_np(rel):
    nb = 16
    max_exact = 8
    ret = np.where(rel > 0, nb, 0)
    n = np.abs(rel)
    nf = np.maximum(n, 1).astype(np.float32)
    large = max_exact + (np.log(nf / np.float32(max_exact)) / np.float32(math.log(1024 / max_exact))
                         * np.float32(nb - max_exact)).astype(np.int32)
    large = np.minimum(large, nb - 1)
    return ret + np.where(n < max_exact, n, large)


_CONST = {}


def _constants():
    if _CONST:
        return _CONST
    bf = ml_dtypes.bfloat16
    m = np.arange(P_G)
    rel = 191 - m
    oh = np.zeros((32, 3, P_G), np.float32)
    gmask = np.zeros((12, 3, P_G), np.float32)
    for di, d in enumerate(DILS):
        bk = _t5_bucket_np(rel * d)
        valid = np.abs(rel) <= 64
        oh[bk[valid], di, m[valid]] = 1.0
        gmask[:, di, valid] = 1.0
    _CONST["onehot"] = oh
    _CONST["gmask"] = gmask
    c = np.arange(64)
    ang = 2 * np.pi * np.outer(c, c) / 64
    C64 = np.cos(ang) / 512.0
    S64 = -np.sin(ang) / 512.0
    z = np.zeros((64, 64))
    _CONST["c64blk"] = np.block([[C64, z], [z, C64]]).astype(bf)
    _CONST["s64blk"] = np.block([[S64, z], [z, S64]]).astype(bf)
    _CONST["ident"] = np.eye(128, dtype=np.float32).astype(bf)
    s = np.arange(S, dtype=np.int64)
    prod = (s[:, None] * s[None, :]) % S
    angt = prod.astype(np.float64) * (2 * np.pi / S)
    for name, fn in (("dft_cos", np.cos), ("dft_sin", np.sin)):
        t = fn(angt).astype(np.float32)
        t = t.reshape(32, 128, 8, 512).transpose(2, 1, 0, 3)
        _CONST[name] = np.ascontiguousarray(t).astype(bf)
    return _CONST


_NC = {}


def kernel(x, norm_mix_gain, w_in, attn_out_gain, rel_bias_table, fourier_w, fourier_b, fourier_out_gain,
           w_out, norm_ffn_gain, w_gate, w_val, conv_w, conv_b, w_down, final_norm_gain):
    f32 = np.float32
    x = np.asarray(x, f32)
    cst = _constants()
    bc = lambda v, n: np.ascontiguousarray(np.broadcast_to(np.asarray(v, f32).reshape(1, n), (128, n)))
    fw = np.asarray(fourier_w, f32)[0]
    fw_blk = np.zeros((128, 2, 128), f32)
    for cc in range(2):
        fw_blk[0:64, cc, 0:64] = fw[2 * cc]
        fw_blk[64:128, cc, 64:128] = fw[2 * cc + 1]
    shared = {
        "g1b": bc(norm_mix_gain[0], D), "g2b": bc(norm_ffn_gain[0], D), "gFb": bc(final_norm_gain, D),
        "gfb": bc(fourier_out_gain[0], 256),
        "ga_t": np.ascontiguousarray(np.asarray(attn_out_gain, f32)[0].reshape(6, 128).T),
        "w_in": np.ascontiguousarray(np.asarray(w_in, f32)[0]),
        "w_out": np.ascontiguousarray(np.asarray(w_out, f32)[0]),
        "w_gate": np.ascontiguousarray(np.asarray(w_gate, f32)[0]),
        "w_val": np.ascontiguousarray(np.asarray(w_val, f32)[0]),
        "w_down": np.ascontiguousarray(np.asarray(w_down, f32)[0]),
        "cw_t": np.ascontiguousarray(np.asarray(conv_w, f32)[0].reshape(3, NF, 128).transpose(2, 1, 0)),
        "cb_t": np.ascontiguousarray(np.asarray(conv_b, f32)[0].reshape(NF, 128).T),
        "rel_tab": np.ascontiguousarray(np.asarray(rel_bias_table, f32)),
        "onehot": cst["onehot"], "gmask": cst["gmask"],
        "fw_blk": fw_blk,
        "fb_row": np.ascontiguousarray(np.asarray(fourier_b, f32)[0].reshape(1, 256)),
        "c64blk": cst["c64blk"], "s64blk": cst["s64blk"], "ident": cst["ident"],
        "dft_cos": cst["dft_cos"], "dft_sin": cst["dft_sin"],
    }
    n = x.shape[0]
    if "nc" not in _NC:
        _NC["nc"] = build_nc()
    nc = _NC["nc"]
    in_maps = [dict(shared, x=np.ascontiguousarray(x[b])) for b in range(n)]
    res = run_bass_kernel_spmd(nc, in_maps, core_ids=list(range(n)))
    return np.stack([np.asarray(r["y"], f32) for r in res.results], axis=0)
```
